# Optimizing a Trainium2 kernel written in Bass

```python
import math
import jax, jax.numpy as jnp
from jax import lax
import numpy as np

D_MODEL = 2048
BATCH = 16
SEQ = 256
DEPTH = 1
DEC_BATCH = 2
DEC_SEQ = 2048
PAST_LEN = 512

GRID_W = 64
SSM_WIDTH = D_MODEL // 2
SSM_GROUP = 16
SSM_GROUPS = SSM_WIDTH // SSM_GROUP
SSM_STATE = 64
N_HEADS = 16
HEAD_DIM = 64
ATT_WIDTH = N_HEADS * HEAD_DIM
WIN_ROWS = 8
WIN_COLS = 16
KEY_COLS = 2 * WIN_COLS
N_COL_BLOCKS = GRID_W // WIN_COLS
Q_BLOCK = 128
IN_WIDTH = 2 * SSM_WIDTH + 4 * ATT_WIDTH + 2 * D_MODEL
EPS = 1e-6
NEG_INF = -1e30

kernel_name = 'hybrid_s5_natten_prefix_dit_step'


def _rms(x, w):
    xf = x.astype(jnp.float32)
    y = xf * lax.rsqrt(jnp.mean(xf * xf, axis=-1, keepdims=True) + EPS)
    return (y * w.astype(jnp.float32)).astype(x.dtype)


def _modulation(cond, w_ada, b_ada):
    mod = jax.nn.silu(cond) @ w_ada + b_ada
    shift, scale, gate = jnp.split(mod[:, None, :], 3, axis=-1)
    return shift, scale, gate


def _front(x, shift, scale, norm_w, w_in, q_norm_w, k_norm_w):
    h = _rms(x, norm_w) * (1 + scale) + shift
    proj = h @ w_in
    cuts = [SSM_WIDTH, 2 * SSM_WIDTH, 2 * SSM_WIDTH + ATT_WIDTH, 2 * SSM_WIDTH + 2 * ATT_WIDTH,
            2 * SSM_WIDTH + 3 * ATT_WIDTH, 2 * SSM_WIDTH + 4 * ATT_WIDTH,
            2 * SSM_WIDTH + 4 * ATT_WIDTH + D_MODEL]
    u, z_s, q, k, v, z_a, g_s, g_a = jnp.split(proj, cuts, axis=-1)
    n, L, _ = x.shape
    q = _rms(q.reshape(n, L, N_HEADS, HEAD_DIM), q_norm_w)
    k = _rms(k.reshape(n, L, N_HEADS, HEAD_DIM), k_norm_w)
    v = v.reshape(n, L, N_HEADS, HEAD_DIM)
    return u, z_s, q, k, v, z_a, g_s, g_a


def _s5_discretize(a_re, a_im, log_dt, b_re, b_im):
    f32 = jnp.float32
    lam = lax.complex(jnp.minimum(a_re.astype(f32), -1e-4), a_im.astype(f32))
    dt = jnp.exp(log_dt.astype(f32))[..., None]
    lam_bar = jnp.exp(lam * dt)
    b = lax.complex(b_re.astype(f32), b_im.astype(f32))
    b_bar = ((lam_bar - 1) / lam)[..., None] * b
    return lam_bar, b_bar


def _lin_scan(bu, lam_bar, h0):
    bu = bu.at[:, 0].add(lam_bar * h0)
    a = jnp.broadcast_to(lam_bar, bu.shape)

    def combine(e1, e2):
        a1, b1 = e1
        a2, b2 = e2
        return a1 * a2, a2 * b1 + b2

    _, h = lax.associative_scan(combine, (a, bu), axis=1)
    return h


def _s5_bidir(u, lam_bar, b_bar, c_re, c_im, d, h0_f, h0_b):
    n, L, _ = u.shape
    f32 = jnp.float32
    uf = u.astype(f32)
    ug = uf.reshape(n, L, SSM_GROUPS, SSM_GROUP).astype(jnp.complex64)
    c = lax.complex(c_re.astype(f32), c_im.astype(f32))
    bu_f = jnp.einsum('blgi,gpi->blgp', ug, b_bar[0])
    bu_b = jnp.einsum('blgi,gpi->blgp', ug, b_bar[1])
    h_f = _lin_scan(bu_f, lam_bar[0], h0_f)
    h_b = jnp.flip(_lin_scan(jnp.flip(bu_b, axis=1), lam_bar[1], h0_b), axis=1)
    y = (jnp.einsum('blgp,gip->blgi', h_f, c[0]) + jnp.einsum('blgp,gip->blgi', h_b, c[1])).real
    y = y.reshape(n, L, SSM_WIDTH) + d.astype(f32) * uf
    return y, h_f[:, -1], h_b[:, 0]


def _context_attention(q, k, v):
    n, L, H, Dh = q.shape
    nblk = L // Q_BLOCK
    qb = q.reshape(n, nblk, Q_BLOCK, H, Dh).transpose(1, 0, 2, 3, 4)

    def block(qi):
        s = jnp.einsum('bqhd,bkhd->bhqk', qi, k).astype(jnp.float32) * (HEAD_DIM ** -0.5)
        p = jax.nn.softmax(s, axis=-1).astype(v.dtype)
        return jnp.einsum('bhqk,bkhd->bqhd', p, v)

    o = lax.map(block, qb)
    return o.transpose(1, 0, 2, 3, 4).reshape(n, L, H * Dh)


def _neighbourhood_attention(q, k, v, k_ctx, v_ctx, rpb):
    n, L, H, Dh = q.shape
    rows = L // GRID_W
    kh = min(WIN_ROWS, rows)
    qg = q.reshape(n, rows, GRID_W, H, Dh)
    kg = k.reshape(n, rows, GRID_W, H, Dh)
    vg = v.reshape(n, rows, GRID_W, H, Dh)
    jb = np.arange(N_COL_BLOCKS)
    col_start = np.clip(jb * WIN_COLS - WIN_COLS // 2, 0, GRID_W - KEY_COLS)
    col_idx = (col_start[:, None] + np.arange(KEY_COLS)[None, :]).astype(np.int32)
    q_col = jb[:, None] * WIN_COLS + np.arange(WIN_COLS)[None, :]
    cs = np.clip(q_col - WIN_COLS // 2, 0, GRID_W - WIN_COLS)
    kc = col_idx[:, None, :]
    mask = (kc >= cs[..., None]) & (kc < cs[..., None] + WIN_COLS)
    dc_idx = np.clip(kc - q_col[..., None] + WIN_COLS - 1, 0, 2 * WIN_COLS - 2).astype(np.int32)
    rpb32 = rpb.astype(jnp.float32)
    scale = HEAD_DIM ** -0.5

    def row_fn(r):
        rs = jnp.clip(r - kh // 2, 0, rows - kh)
        k_rows = lax.dynamic_slice_in_dim(kg, rs, kh, axis=1)
        v_rows = lax.dynamic_slice_in_dim(vg, rs, kh, axis=1)
        k_blk = k_rows[:, :, col_idx]
        v_blk = v_rows[:, :, col_idx]
        q_r = lax.dynamic_index_in_dim(qg, r, axis=1, keepdims=False)
        q_r = q_r.reshape(n, N_COL_BLOCKS, WIN_COLS, H, Dh)
        s_win = jnp.einsum('bjqhd,bkjmhd->bjqhkm', q_r, k_blk).astype(jnp.float32) * scale
        dr_idx = rs + jnp.arange(kh) - r + WIN_ROWS - 1
        bias = rpb32[:, dr_idx[:, None, None, None], dc_idx[None]]
        bias = bias.transpose(2, 3, 0, 1, 4)
        s_win = jnp.where(mask[:, :, None, None, :], s_win + bias, NEG_INF)
        s_ctx = jnp.einsum('bjqhd,bchd->bjqhc', q_r, k_ctx).astype(jnp.float32) * scale
        nw = kh * KEY_COLS
        s = jnp.concatenate([s_win.reshape(n, N_COL_BLOCKS, WIN_COLS, H, nw), s_ctx], axis=-1)
        p = jax.nn.softmax(s, axis=-1)
        p_win = p[..., :nw].reshape(n, N_COL_BLOCKS, WIN_COLS, H, kh, KEY_COLS).astype(v.dtype)
        p_ctx = p[..., nw:].astype(v_ctx.dtype)
        o = (jnp.einsum('bjqhkm,bkjmhd->bjqhd', p_win, v_blk)
             + jnp.einsum('bjqhc,bchd->bjqhd', p_ctx, v_ctx))
        return o.reshape(n, GRID_W, H * Dh)

    out = lax.map(row_fn, jnp.arange(rows))
    return out.transpose(1, 0, 2, 3).reshape(n, L, H * Dh)


def _back(x, y_ssm, z_s, attn, z_a, g_s, g_a, gate, w_glu, b_glu, w_ssm_out, w_att_out, w_o):
    ys = jax.nn.gelu(y_ssm.astype(x.dtype))
    ys = ys * jax.nn.sigmoid(ys @ w_glu + b_glu)
    ys = ys * jax.nn.silu(z_s)
    p_s = ys @ w_ssm_out
    p_a = (attn * jax.nn.silu(z_a)) @ w_att_out
    merged = jax.nn.sigmoid(g_s) * p_s + jax.nn.sigmoid(g_a) * p_a
    return x + gate * (merged @ w_o)


def setup_inputs(seed: int = 0) -> dict:
    key = jax.random.key(seed)
    ks = jax.random.split(key, 32)
    f32 = jnp.float32

    def nrm(k, shape, s):
        return jax.random.normal(k, shape, f32) * s

    L = DEPTH
    G, P, GC = SSM_GROUPS, SSM_STATE, SSM_GROUP
    n_idx = jnp.arange(P, dtype=f32)
    return {
        'x_prompt': nrm(ks[0], (BATCH, SEQ, D_MODEL), 1.0),
        'x_sample': nrm(ks[1], (DEC_BATCH, DEC_SEQ, D_MODEL), 1.0),
        'cache_k': nrm(ks[2], (DEC_BATCH, L, PAST_LEN, N_HEADS, HEAD_DIM), 1.0),
        'cache_v': nrm(ks[3], (DEC_BATCH, L, PAST_LEN, N_HEADS, HEAD_DIM), 1.0),
        'state_ssm_re': nrm(ks[4], (DEC_BATCH, L, 2, G, P), 0.5),
        'state_ssm_im': nrm(ks[5], (DEC_BATCH, L, 2, G, P), 0.5),
        'c': nrm(ks[6], (DEC_BATCH, D_MODEL), 1.0),
        'c_ctx': nrm(ks[7], (D_MODEL,), 1.0),
        'norm_w': 1.0 + nrm(ks[8], (L, D_MODEL), 0.02),
        'w_ada': nrm(ks[9], (L, D_MODEL, 3 * D_MODEL), 0.5 * D_MODEL ** -0.5),
        'b_ada': nrm(ks[10], (L, 3 * D_MODEL), 0.02),
        'w_in': nrm(ks[11], (L, D_MODEL, IN_WIDTH), D_MODEL ** -0.5),
        'q_norm_w': 1.0 + nrm(ks[12], (L, HEAD_DIM), 0.02),
        'k_norm_w': 1.0 + nrm(ks[13], (L, HEAD_DIM), 0.02),
        'rel_pos_bias': nrm(ks[14], (L, N_HEADS, 2 * WIN_ROWS - 1, 2 * WIN_COLS - 1), 0.02),
        'ssm_a_re': -0.5 + nrm(ks[15], (L, 2, G, P), 0.01),
        'ssm_a_im': jnp.pi * n_idx + nrm(ks[16], (L, 2, G, P), 0.01),
        'ssm_log_dt': jax.random.uniform(ks[17], (L, 2, G), f32, math.log(1e-3), math.log(1e-1)),
        'ssm_b_re': nrm(ks[18], (L, 2, G, P, GC), (2 * GC) ** -0.5),
        'ssm_b_im': nrm(ks[19], (L, 2, G, P, GC), (2 * GC) ** -0.5),
        'ssm_c_re': nrm(ks[20], (L, 2, G, GC, P), (2 * P) ** -0.5),
        'ssm_c_im': nrm(ks[21], (L, 2, G, GC, P), (2 * P) ** -0.5),
        'ssm_d': nrm(ks[22], (L, SSM_WIDTH), 1.0),
        'w_glu': nrm(ks[23], (L, SSM_WIDTH, SSM_WIDTH), SSM_WIDTH ** -0.5),
        'b_glu': nrm(ks[24], (L, SSM_WIDTH), 0.02),
        'w_ssm_out': nrm(ks[25], (L, SSM_WIDTH, D_MODEL), SSM_WIDTH ** -0.5),
        'w_att_out': nrm(ks[26], (L, ATT_WIDTH, D_MODEL), ATT_WIDTH ** -0.5),
        'w_o': nrm(ks[27], (L, D_MODEL, D_MODEL), D_MODEL ** -0.5),
    }


def reference(x_prompt, x_sample, cache_k, cache_v, state_ssm_re, state_ssm_im, c, c_ctx,
              norm_w, w_ada, b_ada, w_in, q_norm_w, k_norm_w, rel_pos_bias,
              ssm_a_re, ssm_a_im, ssm_log_dt, ssm_b_re, ssm_b_im, ssm_c_re, ssm_c_im, ssm_d,
              w_glu, b_glu, w_ssm_out, w_att_out, w_o):
    f32 = jnp.float32
    xp = x_prompt
    xs = x_sample
    new_k, new_v, new_re, new_im = [], [], [], []
    for l in range(DEPTH):
        lam_bar, b_bar = _s5_discretize(ssm_a_re[l], ssm_a_im[l], ssm_log_dt[l], ssm_b_re[l], ssm_b_im[l])

        shift, scale, gate = _modulation(c_ctx[None, :], w_ada[l], b_ada[l])
        u, z_s, q, k, v, z_a, g_s, g_a = _front(xp, shift, scale, norm_w[l], w_in[l], q_norm_w[l], k_norm_w[l])
        h0 = jnp.zeros((xp.shape[0], SSM_GROUPS, SSM_STATE), jnp.complex64)
        y_ssm, hf, hb = _s5_bidir(u, lam_bar, b_bar, ssm_c_re[l], ssm_c_im[l], ssm_d[l], h0, h0)
        attn = _context_attention(q, k, v)
        h_last = jnp.stack([hf, hb], axis=1)
        new_k.append(k)
        new_v.append(v)
        new_re.append(h_last.real)
        new_im.append(h_last.imag)
        xp = _back(xp, y_ssm, z_s, attn, z_a, g_s, g_a, gate,
                   w_glu[l], b_glu[l], w_ssm_out[l], w_att_out[l], w_o[l])

        shift, scale, gate = _modulation(c, w_ada[l], b_ada[l])
        u, z_s, q, k, v, z_a, g_s, g_a = _front(xs, shift, scale, norm_w[l], w_in[l], q_norm_w[l], k_norm_w[l])
        st = lax.complex(state_ssm_re[:, l].astype(f32), state_ssm_im[:, l].astype(f32))
        y_ssm, _, _ = _s5_bidir(u, lam_bar, b_bar, ssm_c_re[l], ssm_c_im[l], ssm_d[l], st[:, 0], st[:, 1])
        attn = _neighbourhood_attention(q, k, v, cache_k[:, l], cache_v[:, l], rel_pos_bias[l])
        xs = _back(xs, y_ssm, z_s, attn, z_a, g_s, g_a, gate,
                   w_glu[l], b_glu[l], w_ssm_out[l], w_att_out[l], w_o[l])

    new_cache_k = jnp.stack(new_k, axis=1)
    new_cache_v = jnp.stack(new_v, axis=1)
    new_state_re = jnp.stack(new_re, axis=1)
    new_state_im = jnp.stack(new_im, axis=1)
    return (xp, xs, new_cache_k, new_cache_v, new_state_re, new_state_im)
```

```python
import numpy as np
from contextlib import ExitStack
import concourse.bass as bass
import concourse.mybir as mybir
from concourse.bass_utils import run_bass_kernel_spmd

F32 = mybir.dt.float32
BF16 = mybir.dt.bfloat16
ALU = mybir.AluOpType
AF = mybir.ActivationFunctionType
AX = mybir.AxisListType

D = 2048
NCORES = 8
EPS = 1e-6
IN_W = 10240
NS = 2048
NPR = 512
NT = 2560
NCH = 320
NEG = -1e30
GELU_C = 1.5957691216057308


class Prog:
    NDMA = 8
    ENGS = ['pe', 'act', 'dve', 'pool', 'sp']

    def __init__(self, nc):
        self.nc = nc
        self.ops = []
        self.lastw = {}
        self.readers = {}
        self.forced = set()

    def add(self, eng, fn, r=(), w=(), dma=False, deps=None):
        idx = len(self.ops)
        dd = set(deps) if deps else set()
        for x in r:
            if x in self.lastw:
                dd.add(self.lastw[x])
        for x in w:
            if x in self.lastw:
                dd.add(self.lastw[x])
            for kk, vv in self.readers.get(x, {}).items():
                if kk == 'dmas':
                    dd.update(vv)
                elif vv is not None:
                    dd.add(vv)
        for x in r:
            self.readers.setdefault(x, {})[(eng, dma)] = idx if not dma else None
            if dma:
                self.readers[x].setdefault('dmas', []).append(idx)
        for x in w:
            self.lastw[x] = idx
            self.readers[x] = {}
        self.ops.append(dict(eng=eng, fn=fn, deps=dd, dma=dma))
        return idx

    def dma(self, eng, out, in_, r=(), w=(), **kw):
        return self.add(eng, lambda e: e.dma_start(out=out, in_=in_, **kw), r, w, dma=True)

    def barrier(self):
        lastc = {}
        lastd = {}
        for i, op in enumerate(self.ops):
            if op['fn'] is None:
                continue
            if op['dma']:
                lastd.setdefault(op['eng'], []).append(i)
            else:
                lastc[op['eng']] = i
        deps = set(lastc.values())
        for e, l in lastd.items():
            deps.update(l[-self.NDMA:])
        self.forced.update(lastc.values())
        for e in self.ENGS:
            self.add(e, None, deps=deps)
        self.lastw = {}
        self.readers = {}

    def emit(self, es):
        nc = self.nc
        ops = self.ops
        engs = self.ENGS
        csem = {e: es.enter_context(nc.semaphore("c_" + e)) for e in engs if e != 'sp'}
        dsem = {e: [es.enter_context(nc.semaphore("d_%s%d" % (e, i))) for i in range(self.NDMA)]
                for e in ['sp', 'act', 'pool']}

        def elide(dop, op):
            return (not dop['dma']) and (not op['dma']) and dop['eng'] == 'pe' and op['eng'] == 'pe' \
                and op['fn'] is not None

        needed = [False] * len(ops)
        for i in self.forced:
            needed[i] = True
        for i, op in enumerate(ops):
            for d in op['deps']:
                if elide(ops[d], op):
                    continue
                needed[d] = True
        ccount = {e: 0 for e in engs}
        dcount = {e: 0 for e in engs}
        ev = [None] * len(ops)
        pre = [None] * len(ops)
        for i, op in enumerate(ops):
            e = op['eng']
            if op['fn'] is None:
                continue
            if op['dma']:
                n = dcount[e]
                dcount[e] += 1
                sem = dsem[e][n % self.NDMA]
                ev[i] = (sem, 16 * (n // self.NDMA + 1))
                if n >= self.NDMA:
                    pre[i] = (sem, 16 * (n // self.NDMA))
            elif needed[i]:
                ccount[e] += 1
                ev[i] = (csem[e], ccount[e])
        per = {e: [] for e in engs}
        for i, op in enumerate(ops):
            per[op['eng']].append(i)
        self.stats = dict(ccount=ccount, dcount=dcount, nops={e: len(per[e]) for e in engs})

        def run(ename, eobj):
            waited = {}
            for i in per[ename]:
                op = ops[i]
                waits = []
                if pre[i] is not None:
                    waits.append(pre[i])
                for d in sorted(op['deps']):
                    if elide(ops[d], op):
                        continue
                    if ev[d] is None:
                        continue
                    waits.append(ev[d])
                for sem, val in waits:
                    key = id(sem)
                    if waited.get(key, 0) >= val:
                        continue
                    waited[key] = val
                    eobj.wait_ge(sem, val)
                if op['fn'] is None:
                    continue
                ins = op['fn'](eobj)
                if ev[i] is not None:
                    ins.then_inc(ev[i][0], 16 if op['dma'] else 1)

        with nc.Block() as block:
            @block.tensor
            def _(e):
                run('pe', e)

            @block.scalar
            def _(e):
                run('act', e)

            @block.vector
            def _(e):
                run('dve', e)

            @block.gpsimd
            def _(e):
                run('pool', e)

            @block.sync
            def _(e):
                run('sp', e)


class Arena:
    BASE = 16512
    LIMIT = 229344

    def __init__(self, nc):
        self.nc = nc
        self.off = self.BASE
        self.cnt = 0

    def alloc(self, name, shape, dt=F32):
        n = 1
        for s in shape[1:]:
            n *= s
        nb = n * (4 if dt == F32 else 2)
        nb = (nb + 63) // 64 * 64
        assert self.off + nb <= self.LIMIT, "SBUF arena overflow at %s: %d + %d" % (name, self.off, nb)
        self.cnt += 1
        t = self.nc.alloc_sbuf_tensor_at("%s_%d" % (name, self.cnt), shape, dt, offset=self.off)
        self.off += nb
        return t

    def mark(self):
        return self.off

    def reset(self, m):
        self.off = m


def build(stages="ACTBDEF"):
    nc = bass.Bass("TRN2", target_bir_lowering=False)
    P = Prog(nc)
    es = ExitStack()
    A = Arena(nc)

    def din(name, shape):
        return nc.dram_tensor(name, shape, F32, kind="ExternalInput").ap()

    def dout(name, shape):
        return nc.dram_tensor(name, shape, F32, kind="ExternalOutput").ap()

    def dscr(name, shape, dt=F32):
        return nc.dram_tensor(name, shape, dt, kind="Internal").ap()

    xs = din("xs", [NS, D])
    xp = din("xp", [NPR, D])
    xown = din("xown", [512, D])
    xhalo = din("xhalo", [512, D])
    sel3 = din("sel3", [128, 3, 128])
    rmask = din("rmask", [128, 8, 8])
    cond2 = din("cond2", [32, 128])
    ck = din("ck", [512, 1024])
    cv = din("cv", [512, 1024])
    h0re = din("h0re", [128, 64])
    h0im = din("h0im", [128, 64])
    are = din("are", [128, 64])
    aim = din("aim", [128, 64])
    logdt = din("logdt", [1, 128])
    bre = din("bre", [128, 64, 16])
    bim = din("bim", [128, 64, 16])
    cre = din("cre", [128, 16, 64])
    cim = din("cim", [128, 16, 64])
    dcol = din("dcol", [128, 64])
    w_ada = din("w_ada", [D, 3 * D])
    b_ada = din("b_ada", [1, 3 * D])
    norm_w = din("norm_w", [1, D])
    w_in = din("w_in", [D, IN_W])
    qnw = din("qnw", [1, 64])
    knw = din("knw", [1, 64])
    rpbT = din("rpbT", [31, 240])
    w_glu = din("w_glu", [1024, 1024])
    bglu = din("bglu", [128, 8])
    w_so = din("w_so", [1024, D])
    w_ao = din("w_ao", [1024, D])
    w_o = din("w_o", [D, D])
    ident_d = din("ident", [128, 128])
    maskf_d = din("maskf", [128, 128])
    maskb_d = din("maskb", [128, 128])
    colmask_d = din("colmask", [64, 64])
    onehot_d = din("onehot", [31, 64, 64])

    ys_o = dout("ys_o", [512, D])
    yp_o = dout("yp_o", [NPR, D])
    ko = dout("ko", [NPR, 1024])
    vo = dout("vo", [NPR, 1024])
    sre_o = dout("sre_o", [2, 64, 128])
    sim_o = dout("sim_o", [2, 64, 128])

    mod_d = dscr("mod_d", [2, 3 * D])
    v_d = dscr("v_d", [1536, 1024])
    qT_d = dscr("qT_d", [1024, 1024], BF16)
    kT_d = dscr("kT_d", [1024, 1536], BF16)
    kcT_d = dscr("kcT_d", [1024, 512], BF16)
    zs_d = dscr("zs_d", [1024, 1024])
    za_d = dscr("za_d", [1024, 1024])
    gs_d = dscr("gs_d", [D, 1024])
    ga_d = dscr("ga_d", [D, 1024])
    ysg_d = dscr("ysg_d", [1024, 1024])
    attnT_d = dscr("attnT_d", [1024, 1024])
    xmat_d = dscr("xmat_d", [128, 64, 2, 2, 64], BF16)
    cl_d = dscr("cl_d", [64, 64, 2, 2, 128], BF16)
    mg_d = dscr("mg_d", [128, 64, 128], BF16)
    tab_d = dscr("tab_d", [64, 2, 2, 8, 128])
    tb_d = dscr("tb_d", [16, 128, 17, 64])

    ps = [es.enter_context(nc.psum_tensor("ps%d" % i, [128, 512], F32)) for i in range(8)]
    pcnt = [0]

    def nextps():
        i = pcnt[0] % 8
        pcnt[0] += 1
        return i

    def TT(eng, out, in0, in1, op, r, w):
        P.add(eng, lambda e: e.tensor_tensor(out=out, in0=in0, in1=in1, op=op), r, w)

    def TS(eng, out, in0, s1, s2, op0, op1, r, w):
        if s2 is None:
            P.add(eng, lambda e: e.tensor_scalar(out=out, in0=in0, scalar1=s1, scalar2=None, op0=op0), r, w)
        else:
            P.add(eng, lambda e: e.tensor_scalar(out=out, in0=in0, scalar1=s1, scalar2=s2, op0=op0, op1=op1), r, w)

    def STT(eng, out, in0, scalar, in1, op0, op1, r, w):
        P.add(eng, lambda e: e.scalar_tensor_tensor(out=out, in0=in0, scalar=scalar, in1=in1, op0=op0, op1=op1), r, w)

    def ACT(out, in_, func, r, w, **kw):
        P.add('act', lambda e: e.activation(out=out, in_=in_, func=func, **kw), r, w)

    def CP(eng, out, in_, r, w):
        if eng == 'act':
            P.add('act', lambda e: e.copy(out=out, in_=in_), r, w)
        else:
            P.add(eng, lambda e: e.tensor_copy(out=out, in_=in_), r, w)

    def MS(eng, out, val, w):
        P.add(eng, lambda e: e.memset(out, val), (), w)

    def MM(out, lhsT, rhs, start, stop, r, w):
        P.add('pe', lambda e: e.matmul(out, lhsT=lhsT, rhs=rhs, start=start, stop=stop), r, w)

    def TR(out, in_, idn, r, w):
        P.add('pe', lambda e: e.transpose(out=out, in_=in_, identity=idn), r, w)

    def RCP(out, in_, r, w):
        P.add('dve', lambda e: e.reciprocal(out=out, in_=in_), r, w)

    def SWP(eng, out, in_, tab, r, w):
        TT(eng, out[:, 0], in_[:, 1], tab[:, 0], ALU.mult, r, w)
        TT(eng, out[:, 1], in_[:, 0], tab[:, 1], ALU.mult, r, w)

    dq = [0]

    def DQ():
        dq[0] += 1
        return ['sp', 'pool'][dq[0] % 2]

    ident = A.alloc("ident", [128, 128])
    identb = A.alloc("identb", [128, 128], BF16)
    epsc = A.alloc("epsc", [128, 1])
    H0 = A.alloc("H0", [64, 2, 128])
    Hl2 = A.alloc("Hl2", [128, 2, 2, 2, 8, 4])
    PERS0 = A.mark()
    U2 = A.alloc("U2", [128, 64, NCH], BF16)
    PERS = A.mark()

    P.dma('sp', ident[:], ident_d[:, :], w=['ident'])
    CP('pool', identb[:], ident[:], ['ident'], ['identb'])
    MS('pool', epsc[:], EPS, ['epsc'])
    P.barrier()

    def stage_mod():
        cc = A.alloc("cc", [32, 128])
        sT = A.alloc("sT", [128, 32])
        badar = [A.alloc("badar", [2, 128]) for _ in range(3)]
        modrow = [A.alloc("modrow", [2, 128]) for _ in range(3)]
        Wst = [A.alloc("Wst", [128, 16, 128]) for _ in range(3)]
        P.dma('sp', cc[:], cond2[:, :], w=['cc'])
        ACT(cc[:], cc[:], AF.Silu, ['cc'], ['cc'])
        b0 = nextps()
        TR(ps[b0][:, 0:32], cc[:, :], ident[0:32, 0:32], ['cc'], [('ps', b0)])
        CP('dve', sT[:], ps[b0][:, 0:32], [('ps', b0)], ['sT'])
        w_ada_v = w_ada.rearrange("(k p) n -> p k n", p=128)
        st = [0]

        def blocks(n):
            for _ in range(n):
                nb = st[0]
                if nb >= 48:
                    return
                st[0] += 1
                wi = nb % 3
                for kh in range(2):
                    P.dma('sp', Wst[wi][:, kh * 8:(kh + 1) * 8, :],
                          w_ada_v[:, kh * 8:(kh + 1) * 8, nb * 128:(nb + 1) * 128], w=[('Wst', wi, kh)])
                P.dma('sp', badar[wi][:], b_ada[:, nb * 128:(nb + 1) * 128].partition_broadcast(2), w=[('badar', wi)])
                b = nextps()
                for k in range(16):
                    MM(ps[b][0:2, 0:128], sT[:, k::16], Wst[wi][:, k, :], k == 0, k == 15,
                       ['sT', ('Wst', wi, k // 8)], [('ps', b)])
                TT('dve', modrow[wi][0:2, 0:128], ps[b][0:2, 0:128], badar[wi][0:2, 0:128], ALU.add,
                   [('ps', b), ('badar', wi)], [('modrow', wi)])
                P.dma('pool', mod_d[:, nb * 128:(nb + 1) * 128], modrow[wi][0:2, 0:128], r=[('modrow', wi)], w=[('mod_d', nb)])
        return blocks

    mod_blocks = None
    if 'A' in stages:
        mod_blocks = stage_mod()
        mod_blocks(4)

    def setup_ssm():
        ld = A.alloc("ld", [128, 64])
        AreT = A.alloc("AreT", [64, 128])
        AimT = A.alloc("AimT", [64, 128])
        dtb = A.alloc("dtb", [64, 128])
        th = A.alloc("th", [64, 128])
        tn = A.alloc("tn", [64, 128])
        rho = A.alloc("rho", [64, 128])
        irho2 = A.alloc("irho2", [64, 128])
        cs = A.alloc("cs", [64, 128])
        sn = A.alloc("sn", [64, 128])
        Pw = A.alloc("Pw", [64, 2, 9, 128])
        Qw = A.alloc("Qw", [64, 2, 8, 128])
        t1 = A.alloc("t1", [64, 128])
        t2 = A.alloc("t2", [64, 128])
        t3 = A.alloc("t3", [64, 128])
        kap = A.alloc("kap", [64, 2, 128])
        for src, dst, nm in ((are, AreT[:], 'AreT'), (aim, AimT[:], 'AimT'), (h0re, H0[:, 0, :], 'H0'), (h0im, H0[:, 1, :], 'H0')):
            P.dma('sp', ld[:], src[:, :], w=['ld'])
            b = nextps()
            TR(ps[b][0:64, 0:128], ld[:, :], ident[:, :], ['ld'], [('ps', b)])
            CP('dve', dst, ps[b][0:64, 0:128], [('ps', b), nm], [nm])
        P.dma('sp', dtb[:], logdt.partition_broadcast(64), w=['dtb'])
        ACT(dtb[:], dtb[:], AF.Exp, ['dtb'], ['dtb'])
        TS('dve', AreT[:], AreT[:], -1e-4, None, ALU.min, None, ['AreT'], ['AreT'])
        TT('dve', t1[:], AreT[:], dtb[:], ALU.mult, ['AreT', 'dtb'], ['t1'])
        ACT(rho[:], t1[:], AF.Exp, ['t1'], ['rho'])
        ACT(irho2[:], t1[:], AF.Exp, ['t1'], ['irho2'], scale=-2.0)
        TT('dve', th[:], AimT[:], dtb[:], ALU.mult, ['AimT', 'dtb'], ['th'])
        TS('dve', tn[:], th[:], float(1 / (2 * np.pi)), 12582912.0, ALU.mult, ALU.add, ['th'], ['tn'])
        TS('dve', tn[:], tn[:], -12582912.0, float(-2 * np.pi), ALU.add, ALU.mult, ['tn'], ['tn'])
        TT('dve', th[:], th[:], tn[:], ALU.add, ['th', 'tn'], ['th'])
        ACT(sn[:], th[:], AF.Sin, ['th'], ['sn'])
        ACT(t2[:], th[:], AF.Sin, ['th', 't2'], ['t2'], scale=0.5)
        TT('dve', t2[:], t2[:], t2[:], ALU.mult, ['t2'], ['t2'])
        TS('dve', cs[:], t2[:], -2.0, 1.0, ALU.mult, ALU.add, ['t2'], ['cs'])
        MS('pool', Pw[:, 0, 0, :], 1.0, ['Pw'])
        MS('pool', Pw[:, 1, 0, :], 0.0, ['Pw'])
        TT('dve', Pw[:, 0, 1, :], rho[:], cs[:], ALU.mult, ['rho', 'cs', 'Pw'], ['Pw'])
        TT('dve', Pw[:, 1, 1, :], rho[:], sn[:], ALU.mult, ['rho', 'sn', 'Pw'], ['Pw'])

        def cmul(eng, outr, outi, ar, ai, br, bi, rr, ww):
            TT(eng, t1[:], ar, br, ALU.mult, rr + ['t1'], ['t1'])
            TT(eng, t2[:], ai, bi, ALU.mult, rr + ['t2'], ['t2'])
            TT(eng, t3[:], ar, bi, ALU.mult, rr + ['t3'], ['t3'])
            TT(eng, outr, t1[:], t2[:], ALU.subtract, ['t1', 't2'] + ww, ww)
            TT(eng, t1[:], ai, br, ALU.mult, rr + ww + ['t1'], ['t1'])
            TT(eng, outi, t3[:], t1[:], ALU.add, ['t1', 't3'] + ww, ww)

        for k in range(2, 9):
            cmul('dve', Pw[:, 0, k, :], Pw[:, 1, k, :], Pw[:, 0, k - 1, :], Pw[:, 1, k - 1, :],
                 Pw[:, 0, 1, :], Pw[:, 1, 1, :], ['Pw'], ['Pw'])
        MS('pool', Qw[:, 0, 0, :], 1.0, ['Qw'])
        MS('pool', Qw[:, 1, 0, :], 0.0, ['Qw'])
        TT('dve', Qw[:, 0, 1, :], Pw[:, 0, 1, :], irho2[:], ALU.mult, ['Pw', 'irho2', 'Qw'], ['Qw'])
        STT('dve', Qw[:, 1, 1, :], Pw[:, 1, 1, :], -1.0, irho2[:], ALU.mult, ALU.mult, ['Pw', 'irho2', 'Qw'], ['Qw'])
        for k in range(2, 8):
            cmul('dve', Qw[:, 0, k, :], Qw[:, 1, k, :], Qw[:, 0, k - 1, :], Qw[:, 1, k - 1, :],
                 Qw[:, 0, 1, :], Qw[:, 1, 1, :], ['Qw'], ['Qw'])
        Pwr = A.alloc("Pwr", [64, 2, 9, 128])
        for k in range(9):
            CP('dve', Pwr[:, :, k, :], Pw[:, :, 8 - k, :], ['Pw', 'Pwr'], ['Pwr'])
        TAB = A.alloc("TAB", [64, 2, 2, 8, 128])
        CP('dve', TAB[:, 0, 0, 0, :], Pw[:, 0, 8, :], ['Pw'], ['TAB'])
        CP('dve', TAB[:, 1, 1, 0, :], Pw[:, 1, 8, :], ['Pw', 'TAB'], ['TAB'])
        for k in range(1, 8):
            cmul('dve', TAB[:, 0, 0, k, :], TAB[:, 1, 1, k, :], TAB[:, 0, 0, k - 1, :], TAB[:, 1, 1, k - 1, :],
                 Pw[:, 0, 8, :], Pw[:, 1, 8, :], ['TAB', 'Pw'], ['TAB'])
        CP('dve', TAB[:, 0, 1, :, :], TAB[:, 0, 0, :, :], ['TAB'], ['TAB'])
        TS('dve', TAB[:, 1, 0, :, :], TAB[:, 1, 1, :, :], -1.0, None, ALU.mult, None, ['TAB'], ['TAB'])
        P.dma('sp', tab_d[:, :, :, :, :], TAB[:], r=['TAB'], w=['tab_d'])
        nr = A.alloc("nr", [64, 128])
        den = A.alloc("den", [64, 128])
        TS('dve', nr[:], Pw[:, 0, 1, :], -1.0, None, ALU.add, None, ['Pw'], ['nr'])
        TT('dve', t1[:], AreT[:], AreT[:], ALU.mult, ['AreT', 't1'], ['t1'])
        TT('dve', t2[:], AimT[:], AimT[:], ALU.mult, ['AimT', 't2'], ['t2'])
        TT('dve', den[:], t1[:], t2[:], ALU.add, ['t1', 't2'], ['den'])
        RCP(den[:], den[:], ['den'], ['den'])
        TT('dve', t1[:], nr[:], AreT[:], ALU.mult, ['nr', 'AreT', 't1'], ['t1'])
        TT('dve', t2[:], Pw[:, 1, 1, :], AimT[:], ALU.mult, ['Pw', 'AimT', 't2'], ['t2'])
        TT('dve', t1[:], t1[:], t2[:], ALU.add, ['t1', 't2'], ['t1'])
        TT('dve', kap[:, 0, :], t1[:], den[:], ALU.mult, ['t1', 'den'], ['kap'])
        TT('dve', t1[:], Pw[:, 1, 1, :], AreT[:], ALU.mult, ['Pw', 'AreT', 't1'], ['t1'])
        TT('dve', t2[:], nr[:], AimT[:], ALU.mult, ['nr', 'AimT', 't2'], ['t2'])
        TT('dve', t1[:], t1[:], t2[:], ALU.subtract, ['t1', 't2'], ['t1'])
        TT('dve', kap[:, 1, :], t1[:], den[:], ALU.mult, ['t1', 'den', 'kap'], ['kap'])

        maskf = A.alloc("maskf", [128, 128])
        maskb = A.alloc("maskb", [128, 128])
        dcs = A.alloc("dcs", [128, 64])
        P.dma('sp', maskf[:], maskf_d[:, :], w=['maskf'])
        P.dma('sp', maskb[:], maskb_d[:, :], w=['maskb'])
        P.dma('sp', dcs[:], dcol[:, :], w=['dcs'])
        Bl = A.alloc("Bl", [64, 2, 16, 16])
        Bb = A.alloc("Bb", [64, 2, 16, 16])
        Cl0 = A.alloc("Cl0", [128, 2, 16, 64])
        CT = A.alloc("CT", [64, 2, 16, 16])
        Bs = A.alloc("Bs", [64, 2, 16, 8, 16])
        Cr = A.alloc("Cr", [64, 2, 16, 8, 16])
        XW = A.alloc("XW", [64, 2, 16, 8, 16], BF16)
        CLb = A.alloc("CLb", [64, 16, 2, 128], BF16)
        CLv = CLb[:].rearrange("p a r (s i) -> p a r s i", i=16)
        XMb = A.alloc("XMb", [128, 16, 2, 64], BF16)
        MGb = A.alloc("MGb", [128, 8, 128], BF16)
        tm1 = A.alloc("tm1", [128, 128])
        tm2 = A.alloc("tm2", [128, 128])
        u1 = A.alloc("u1", [64, 8, 8, 16])
        u2 = A.alloc("u2", [64, 8, 8, 16])
        u3 = A.alloc("u3", [64, 8, 8, 16])
        u4 = A.alloc("u4", [64, 8, 8, 16])
        P.dma('sp', Cl0[:, 0, :, :], cre[:, :, :], w=['Cl0'])
        P.dma('pool', Cl0[:, 1, :, :], cim[:, :, :], w=['Cl0b'])
        bre_v = bre.rearrange("a p j -> p a j")
        bim_v = bim.rearrange("a p j -> p a j")
        for gb in range(8):
            if mod_blocks is not None:
                mod_blocks(6)
            for d in range(2):
                c0 = d * 64 + gb * 8
                P.dma('sp', Bl[:, 0, d * 8:(d + 1) * 8, :], bre_v[:, c0:c0 + 8, :], w=['Bl'])
                P.dma('pool', Bl[:, 1, d * 8:(d + 1) * 8, :], bim_v[:, c0:c0 + 8, :], w=['Bl'])

            def kb(ri, d):
                return kap[:, ri, d * 64 + gb * 8:d * 64 + gb * 8 + 8][:, :, None].broadcast_to([64, 8, 16])

            ub1 = u1[:, :, 0, :]
            ub2 = u2[:, :, 0, :]
            for d in range(2):
                sl = slice(d * 8, (d + 1) * 8)
                TT('dve', ub1, Bl[:, 0, sl, :], kb(0, d), ALU.mult, ['Bl', 'kap', 'u1'], ['u1'])
                TT('dve', ub2, Bl[:, 1, sl, :], kb(1, d), ALU.mult, ['Bl', 'kap', 'u2'], ['u2'])
                TT('dve', Bb[:, 0, sl, :], ub1, ub2, ALU.subtract, ['u1', 'u2', 'Bb'], ['Bb'])
                TT('dve', ub1, Bl[:, 1, sl, :], kb(0, d), ALU.mult, ['Bl', 'kap', 'u1', 'Bb'], ['u1'])
                TT('dve', ub2, Bl[:, 0, sl, :], kb(1, d), ALU.mult, ['Bl', 'kap', 'u2', 'Bb'], ['u2'])
                TT('dve', Bb[:, 1, sl, :], ub1, ub2, ALU.add, ['u1', 'u2', 'Bb'], ['Bb'])
            for ri in range(2):
                for i4 in range(4):
                    b = nextps()
                    for ii in range(4):
                        i = i4 * 4 + ii
                        TR(ps[b][0:64, ii * 128:(ii + 1) * 128], Cl0[:, ri, i, :], ident[:, :],
                           ['Cl0', 'Cl0b'], [('ps', b)])
                    for d in range(2):
                        src = ps[b][0:64, :].rearrange("p (i c) -> p c i", i=4)[:, d * 64 + gb * 8:d * 64 + gb * 8 + 8, :]
                        CP('act', CT[:, ri, d * 8:(d + 1) * 8, i4 * 4:(i4 + 1) * 4], src, [('ps', b), 'CT'], ['CT'])

            def tabv(t, ri, ks, d):
                v = t[:, ri, ks, d * 64 + gb * 8:d * 64 + gb * 8 + 8].rearrange("p k g -> p g k")
                return v[:, :, :, None].broadcast_to([64, 8, 8, 16])

            def bcs(a):
                return a[:, :, None, :].broadcast_to([64, 8, 8, 16])

            def cm4(eng, outr, outi, ar, ai, tab, ks, d, neg_im, rr, ww):
                TT(eng, u1[:], bcs(ar), tabv(tab, 0, ks, d), ALU.mult, rr + ['u1'], ['u1'])
                TT(eng, u2[:], bcs(ai), tabv(tab, 1, ks, d), ALU.mult, rr + ['u2'], ['u2'])
                TT(eng, u3[:], bcs(ar), tabv(tab, 1, ks, d), ALU.mult, rr + ['u3'], ['u3'])
                TT(eng, u4[:], bcs(ai), tabv(tab, 0, ks, d), ALU.mult, rr + ['u4'], ['u4'])
                TT(eng, outr, u1[:], u2[:], ALU.subtract, ['u1', 'u2'] + ww, ww)
                if neg_im:
                    STT(eng, outi, u3[:], -1.0, u4[:], ALU.mult, ALU.subtract, ['u3', 'u4'] + ww, ww)
                else:
                    TT(eng, outi, u3[:], u4[:], ALU.add, ['u3', 'u4'] + ww, ww)

            asc = slice(0, 8)
            desc7 = slice(7, None, -1)
            for d in range(2):
                sl = slice(d * 8, (d + 1) * 8)
                eng = 'dve'
                eng2 = 'dve'
                cm4(eng, Bs[:, 0, sl, :, :], Bs[:, 1, sl, :, :], Bb[:, 0, sl, :], Bb[:, 1, sl, :],
                    Qw if d == 0 else Pw, asc, d, False, ['Bb', 'Pw', 'Qw'], [('Bs', d)])
                cm4(eng, Cr[:, 0, sl, :, :], Cr[:, 1, sl, :, :], CT[:, 0, sl, :], CT[:, 1, sl, :],
                    Pw if d == 0 else Qw, asc, d, True, ['CT', 'Pw', 'Qw'], [('Cr', d)])
                cm4(eng2, XW[:, 0, sl, :, :], XW[:, 1, sl, :, :], Bb[:, 0, sl, :], Bb[:, 1, sl, :],
                    Pwr if d == 0 else Pw, slice(1, 9) if d == 0 else asc, d, False, ['Bb', 'Pw', 'Pwr'], [('XW', d)])
                cm4(eng2, CLv[:, sl, 0, :, :], CLv[:, sl, 1, :, :], CT[:, 0, sl, :], CT[:, 1, sl, :],
                    Pw if d == 0 else Pwr, slice(1, 9) if d == 0 else asc, d, True, ['CT', 'Pw', 'Pwr'], [('CLb', d)])
            for d in range(2):
                P.dma('sp', cl_d[:, gb * 8:(gb + 1) * 8, d, :, :], CLb[:, d * 8:(d + 1) * 8, :, :], r=[('CLb', d)], w=[('cl_d', gb, d)])
            for dg8 in range(2):
                for ri in range(2):
                    b = nextps()
                    pb = ps[b][:, :].bitcast(BF16)
                    for q in range(8):
                        dg = dg8 * 8 + q
                        TR(pb[:, q * 64:(q + 1) * 64], XW[:, ri, dg, :, :].rearrange("p s j -> p (s j)"),
                           identb[0:64, 0:64], [('XW', 0), ('XW', 1)], [('ps', b)])
                    CP('act', XMb[:, dg8 * 8:(dg8 + 1) * 8, ri, :],
                       pb[:, 0:512].rearrange("p (q c) -> p q c", q=8), [('ps', b), 'XMb'], ['XMb'])
            for d in range(2):
                P.dma('pool', xmat_d[:, gb * 8:(gb + 1) * 8, d, :, :], XMb[:, d * 8:(d + 1) * 8, :, :], r=['XMb'], w=[('xmat_d', gb, d)])
            for g8 in range(8):
                g = gb * 8 + g8
                bf = nextps()
                for d in range(2):
                    o = ps[bf][:, d * 128:(d + 1) * 128]
                    dg = d * 8 + g8
                    MM(o, Bs[:, 0, dg, :, :].rearrange("p s j -> p (s j)"), Cr[:, 0, dg, :, :].rearrange("p s j -> p (s j)"),
                       True, False, [('Bs', d), ('Cr', d)], [('ps', bf)])
                    MM(o, Bs[:, 1, dg, :, :].rearrange("p s j -> p (s j)"), Cr[:, 1, dg, :, :].rearrange("p s j -> p (s j)"),
                       False, True, [('Bs', d), ('Cr', d)], [('ps', bf)])
                TT('dve', tm1[:], ps[bf][:, 0:128], maskf[:], ALU.mult, [('ps', bf), 'maskf', 'tm1'], ['tm1'])
                TT('dve', tm2[:], ps[bf][:, 128:256], maskb[:], ALU.mult, [('ps', bf), 'maskb', 'tm2'], ['tm2'])
                TT('dve', tm1[:], tm1[:], tm2[:], ALU.add, ['tm1', 'tm2'], ['tm1'])
                STT('dve', MGb[:, g8, :], ident[:], dcs[:, g:g + 1], tm1[:], ALU.mult, ALU.add,
                    ['ident', 'dcs', 'tm1', 'MGb'], ['MGb'])
            P.dma('sp', mg_d[:, gb * 8:(gb + 1) * 8, :], MGb[:], r=['MGb'], w=[('mg_d', gb)])

    if 'C' in stages:
        setup_ssm()
        if mod_blocks is not None:
            mod_blocks(48)
        P.barrier()
    A.reset(PERS0)

    def setup_attn():
        rT = A.alloc("rT", [31, 240])
        oh = A.alloc("oh", [31, 64, 64])
        cmk = A.alloc("cmk", [64, 64])
        T0 = A.alloc("T0", [64, 240, 64])
        zt = A.alloc("zt", [64, 16, 64])
        P.dma('sp', rT[:], rpbT[:, :], w=['rT'])
        P.dma('pool', oh[:], onehot_d[:, :, :], w=['oh'])
        P.dma('sp', cmk[:], colmask_d[:, :], w=['cmk'])
        MS('pool', zt[:], 0.0, ['zt'])
        for qc in range(64):
            b = nextps()
            MM(ps[b][0:64, 0:240], oh[:, qc, :], rT[:, :], True, True, ['oh', 'rT'], [('ps', b)])
            CP(['act', 'dve'][qc % 2], T0[:, :, qc], ps[b][0:64, 0:240], [('ps', b), 'T0'], ['T0'])
        TT('dve', T0[:], T0[:], cmk[:, None, :].broadcast_to([64, 240, 64]), ALU.add, ['T0', 'cmk'], ['T0'])
        tbv = tb_d.rearrange("h p a q -> p h a q")
        T0v = T0[:].rearrange("p (h a) q -> p h a q", h=16)
        for h in range(16):
            P.dma('pool', tbv[0:64, h, 1:16, :], T0v[:, h, :, :], r=['T0'], w=[('tb_d', h, 0)])
            P.dma('pool', tbv[64:128, h, 0:15, :], T0v[:, h, :, :], r=['T0'], w=[('tb_d', h, 1)])
        P.dma('sp', tbv[0:64, :, 0, :], zt[:], r=['zt'], w=[('tb_d', 'z0')])
        P.dma('sp', tbv[0:64, :, 16, :], zt[:], r=['zt'], w=[('tb_d', 'z1')])
        P.dma('pool', tbv[64:128, :, 15, :], zt[:], r=['zt'], w=[('tb_d', 'z2')])
        P.dma('pool', tbv[64:128, :, 16, :], zt[:], r=['zt'], w=[('tb_d', 'z3')])
        kcl = [A.alloc("kcl", [128, 1024]) for _ in range(2)]
        kcb = A.alloc("kcb", [128, 1024], BF16)
        kct = A.alloc("kct", [128, 8, 128], BF16)
        for t in range(4):
            P.dma('sp', kcl[t % 2][:], ck[t * 128:(t + 1) * 128, :], w=[('kcl', t % 2)])
            CP('dve', kcb[:], kcl[t % 2][:], [('kcl', t % 2), 'kcb'], ['kcb'])
            for hh in range(2):
                b = nextps()
                pb = ps[b][:, :].bitcast(BF16)
                for q in range(4):
                    TR(pb[:, q * 128:(q + 1) * 128], kcb[:, (hh * 4 + q) * 128:(hh * 4 + q + 1) * 128], identb[:, :],
                       ['kcb', 'identb'], [('ps', b)])
                CP('act', kct[:, hh * 4:(hh + 1) * 4, :], pb[:, 0:512].rearrange("p (q c) -> p q c", q=4),
                   [('ps', b), 'kct'], ['kct'])
            P.dma('sp', kcT_d.rearrange("(a p) n -> p a n", p=128)[:, :, t * 128:(t + 1) * 128], kct[:], r=['kct'],
                  w=[('kcT_d', t)])

    if 'T' in stages:
        setup_attn()
        P.barrier()
    A.reset(PERS)

    NT2 = 1536
    NB = 1024

    def front(hT, tiles):
        normw_bc = A.alloc("normw_bc", [128, D])
        mbc = [A.alloc("mbc", [128, D]) for _ in range(2)]
        shbc = [A.alloc("shbc", [128, D]) for _ in range(2)]
        xt = [A.alloc("xt", [128, D]) for _ in range(3)]
        junk = A.alloc("junk", [128, D])
        xm = A.alloc("xm", [128, D])
        nt = len(tiles)
        ss = A.alloc("ss", [128, nt])
        rs = A.alloc("rs", [128, nt])
        P.dma('sp', normw_bc[:], norm_w.partition_broadcast(128), w=['normw_bc'])
        MS('pool', ss[:], 0.0, [('ss', t) for t in range(nt)])
        for v in range(2):
            P.dma('sp', shbc[v][:], mod_d[v:v + 1, 0:D].partition_broadcast(128), w=[('shbc', v)])
            P.dma('pool', mbc[v][:], mod_d[v:v + 1, D:2 * D].partition_broadcast(128), w=[('mbc', v)])
            STT('dve', mbc[v][:], mbc[v][:], 1.0, normw_bc[:], ALU.add, ALU.mult, [('mbc', v), 'normw_bc'], [('mbc', v)])
        for t, (src, v, c0) in enumerate(tiles):
            xi = t % 3
            P.dma('sp', xt[xi][:], src, w=[('xt', xi)])
            ACT(junk[:], xt[xi][:], AF.Square, [('xt', xi), 'junk'], ['junk', ('ss', t)], accum_out=ss[:, t:t + 1])
            ACT(rs[:, t:t + 1], ss[:, t:t + 1], AF.Sqrt, [('ss', t), 'epsc'], [('rs', t)], bias=epsc[:, 0:1], scale=1.0 / D)
            RCP(rs[:, t:t + 1], rs[:, t:t + 1], [('rs', t)], [('rs', t)])
            STT('dve', xm[:], xt[xi][:], rs[:, t:t + 1], mbc[v][:], ALU.mult, ALU.mult,
                [('xt', xi), ('rs', t), ('mbc', v), 'xm'], ['xm'])
            TT('dve', xm[:], xm[:], shbc[v][:], ALU.add, ['xm', ('shbc', v)], ['xm'])
            for k4 in range(4):
                b = nextps()
                for kk in range(4):
                    k = k4 * 4 + kk
                    TR(ps[b][:, kk * 128:(kk + 1) * 128], xm[:, k * 128:(k + 1) * 128], ident[:, :], ['xm'], [('ps', b)])
                CP('act', hT[:, k4 * 4:(k4 + 1) * 4, c0:c0 + 128], ps[b][:, :].rearrange("p (a b) -> p a b", a=4),
                   [('ps', b)], [('hT', t, k4)])

    w_in_v = w_in.rearrange("(k p) n -> p k n", p=128)
    wc = [0]

    def mk_loader(WD=512):
        Wst = [A.alloc("Wst", [128, 16, WD]) for _ in range(2)]
        Wbf = [A.alloc("Wbf", [128, 16, WD], BF16) for _ in range(2)]

        def load_w(c0):
            i = wc[0] % 2
            wc[0] += 1
            for kh in range(2):
                P.dma('sp', Wst[i][:, kh * 8:(kh + 1) * 8, :], w_in_v[:, kh * 8:(kh + 1) * 8, c0:c0 + WD],
                      w=[('Wst', i, kh)])
                CP(['dve', 'act'][kh], Wbf[i][:, kh * 8:(kh + 1) * 8, :], Wst[i][:, kh * 8:(kh + 1) * 8, :],
                   [('Wst', i, kh)], [('Wbf', i, kh)])
            return i
        return Wbf, load_w

    def u_proj(hT, grps, Wbf, load_w, WD=512):
        Ubuf = A.alloc("Ubuf", [128, 64, 8, 16], BF16)
        for (t0, ncnk, cofs) in grps:
            ng = WD // 16
            for cb in range(1024 // WD):
                wi = load_w(cb * WD)
                for s_ in range(8):
                    b = nextps()
                    for k in range(16):
                        MM(ps[b][0:ncnk, 0:WD], hT[:, k, t0 + s_:t0 + 8 * ncnk:8], Wbf[wi][:, k, :], k == 0, k == 15,
                           [('Wbf', wi, k // 8)], [('ps', b)])
                    CP(['act', 'dve'][s_ % 2], Ubuf[0:ncnk, cb * ng:(cb + 1) * ng, s_, :],
                       ps[b][0:ncnk, 0:WD].rearrange("p (g j) -> p g j", g=ng), [('ps', b), 'Ubuf'], ['Ubuf'])
            for g4 in range(16):
                b = nextps()
                pb = ps[b][:, :].bitcast(BF16)
                for q in range(4):
                    g = g4 * 4 + q
                    TR(pb[:, q * 128:q * 128 + ncnk], Ubuf[0:ncnk, g, :, :].rearrange("p s j -> p (s j)"),
                       identb[0:ncnk, 0:ncnk], ['Ubuf'], [('ps', b)])
                CP(['act', 'dve'][g4 % 2], U2[:, g4 * 4:(g4 + 1) * 4, cofs:cofs + ncnk],
                   pb[:, 0:512].rearrange("p (q c) -> p q c", q=4)[:, :, 0:ncnk], [('ps', b), 'U2'], ['U2'])

    def projections(hT, Wbf, load_w):
        qnw_bc = A.alloc("qnw_bc", [128, 64])
        knw_bc = A.alloc("knw_bc", [128, 64])
        P.dma('sp', qnw_bc[:], qnw.partition_broadcast(128), w=['qnw_bc'])
        P.dma('sp', knw_bc[:], knw.partition_broadcast(128), w=['knw_bc'])
        tok = [A.alloc("tok", [128, 512]) for _ in range(2)]
        tokb = [A.alloc("tokb", [128, 512], BF16) for _ in range(2)]
        tokT = [A.alloc("tokT", [128, 4, 128], BF16) for _ in range(2)]
        sq = A.alloc("sq", [128, 512])
        ms = A.alloc("ms", [128, 8])
        tc_ = [0]
        for cb in range(6):
            c0 = 2048 + cb * 512
            kind = cb // 2
            o0 = (cb % 2) * 512
            wi = load_w(c0)
            tl = [0, 1, 2, 3, 8, 9, 10, 11] if kind == 0 else list(range(12))
            for t in tl:
                b = nextps()
                ti = tc_[0] % 2
                tc_[0] += 1
                for k in range(16):
                    MM(ps[b][:, :], hT[:, k, t * 128:(t + 1) * 128], Wbf[wi][:, k, :], k == 0, k == 15,
                       [('Wbf', wi, k // 8)], [('ps', b)])
                CP('act', tok[ti][:], ps[b][:, :], [('ps', b), ('tok', ti)], [('tok', ti)])
                if kind == 2:
                    P.dma('pool', v_d[t * 128:(t + 1) * 128, o0:o0 + 512], tok[ti][:], r=[('tok', ti)], w=[('v_d', t, cb)])
                    if t >= 8:
                        P.dma('pool', vo[(t - 8) * 128:(t - 7) * 128, o0:o0 + 512], tok[ti][:], r=[('tok', ti)],
                              w=[('vo', t, cb)])
                    continue
                nw = qnw_bc if kind == 0 else knw_bc
                t3 = tok[ti][:].rearrange("p (h d) -> p h d", h=8)
                TT('dve', sq[:], tok[ti][:], tok[ti][:], ALU.mult, [('tok', ti), 'sq'], ['sq'])
                P.add('dve', lambda e: e.tensor_reduce(out=ms[:], in_=sq[:].rearrange("p (h d) -> p h d", h=8),
                                                       axis=AX.X, op=ALU.add), ['sq', 'ms'], ['ms'])
                ACT(ms[:], ms[:], AF.Sqrt, ['ms', 'epsc'], ['ms'], bias=epsc[:, 0:1], scale=1.0 / 64)
                RCP(ms[:], ms[:], ['ms'], ['ms'])
                TT('dve', t3, t3, ms[:, :, None].broadcast_to([128, 8, 64]), ALU.mult, [('tok', ti), 'ms'], [('tok', ti)])
                TT('dve', t3, t3, nw[:, None, :].broadcast_to([128, 8, 64]), ALU.mult, [('tok', ti), 'qnw_bc', 'knw_bc'],
                   [('tok', ti)])
                if kind == 1 and t >= 8:
                    P.dma('pool', ko[(t - 8) * 128:(t - 7) * 128, o0:o0 + 512], tok[ti][:], r=[('tok', ti)], w=[('ko', t, cb)])
                CP('act', tokb[ti][:], tok[ti][:], [('tok', ti), ('tokb', ti)], [('tokb', ti)])
                b2 = nextps()
                pb = ps[b2][:, :].bitcast(BF16)
                for q in range(4):
                    TR(pb[:, q * 128:(q + 1) * 128], tokb[ti][:, q * 128:(q + 1) * 128], identb[:, :], [('tokb', ti)],
                       [('ps', b2)])
                CP('dve', tokT[ti][:], pb[:, 0:512].rearrange("p (q c) -> p q c", q=4), [('ps', b2), ('tokT', ti)], [('tokT', ti)])
                if kind == 0:
                    tq = t if t < 4 else t - 4
                    dsl = qT_d[o0:o0 + 512, tq * 128:(tq + 1) * 128]
                else:
                    dsl = kT_d[o0:o0 + 512, t * 128:(t + 1) * 128]
                P.dma('pool', dsl.rearrange("(q p) n -> p q n", p=128), tokT[ti][:],
                      r=[('tokT', ti)], w=[('qkT', kind, t, cb)])
        fm = [A.alloc("fm", [128, 512]) for _ in range(2)]
        fc_ = [0]
        specs = [(1024, 2, zs_d, AF.Silu), (5120, 2, za_d, AF.Silu), (6144, 4, gs_d, AF.Sigmoid), (8192, 4, ga_d, AF.Sigmoid)]
        for (cbase, nblk, dst, fn) in specs:
            for cb in range(nblk):
                wi = load_w(cbase + cb * 512)
                for f2 in range(4):
                    for tb, hoff in enumerate((0, 1024)):
                        b = nextps()
                        fi = fc_[0] % 2
                        fc_[0] += 1
                        for k in range(16):
                            MM(ps[b][:, :], Wbf[wi][:, k, f2 * 128:(f2 + 1) * 128], hT[:, k, hoff:hoff + 512],
                               k == 0, k == 15, [('Wbf', wi, k // 8)], [('ps', b)])
                        ACT(fm[fi][:], ps[b][:, :], fn, [('ps', b), ('fm', fi)], [('fm', fi)])
                        r0 = cb * 512 + f2 * 128
                        P.dma('pool', dst[r0:r0 + 128, tb * 512:(tb + 1) * 512], fm[fi][:], r=[('fm', fi)],
                              w=[('fmd', r0, tb, cbase)])

    if 'B' in stages:
        hT1 = A.alloc("hT1", [128, 16, NS], BF16)
        m1_ = A.mark()
        front(hT1, [(xs[t * 128:(t + 1) * 128, :], 1, t * 128) for t in range(16)])
        P.barrier()
        A.reset(m1_)
        Wbf, load_w = mk_loader(256)
        u_proj(hT1, [(0, 128, 0), (1024, 128, 128)], Wbf, load_w, 256)
        P.barrier()
        A.reset(PERS)
        hT2 = A.alloc("hT2", [128, 16, NT2], BF16)
        m2_ = A.mark()
        tiles = [(xown[t * 128:(t + 1) * 128, :], 1, t * 128) for t in range(4)]
        tiles += [(xhalo[t * 128:(t + 1) * 128, :], 1, 512 + t * 128) for t in range(4)]
        tiles += [(xp[t * 128:(t + 1) * 128, :], 0, 1024 + t * 128) for t in range(4)]
        front(hT2, tiles)
        P.barrier()
        A.reset(m2_)
        Wbf, load_w = mk_loader()
        m3_ = A.mark()
        u_proj(hT2, [(1024, 64, 256)], Wbf, load_w)
        P.barrier()
        A.reset(m3_)
        projections(hT2, Wbf, load_w)
        P.barrier()
    A.reset(PERS)

    def ssm_main():
        Xb2 = [A.alloc("Xb", [128, 2, 8, NCH]) for _ in range(2)]
        HP2 = [A.alloc("HP", [128, 2, 8, NCH], BF16) for _ in range(2)]
        xmp = A.alloc("xmp", [128, 8, 2, 2, 128], BF16)
        clb2 = [A.alloc("clb", [128, 4, 2, 2, 128], BF16) for _ in range(2)]
        mgb = [A.alloc("mgb", [128, 8, 128], BF16) for _ in range(2)]
        TABs = A.alloc("TABs", [64, 2, 2, 8, 128])
        P.dma('sp', TABs[:], tab_d[:, :, :, :, :], w=['TABs'])
        MS('pool', xmp[:], 0.0, ['xmp'])
        TABb = A.alloc("TABb", [128, 2, 2, 8, 2, 4])
        TAB2b = A.alloc("TAB2b", [128, 2, 2, 5, 2, 4])
        h0b = A.alloc("h0b", [128, 2, 2, 4])
        W1 = [A.alloc("W1", [128, 2, 4, 40]) for _ in range(2)]
        W2 = [A.alloc("W2", [128, 2, 4, 40]) for _ in range(2)]
        HS = [A.alloc("HS", [128, 2, 4, 40]) for _ in range(2)]
        s1p = [A.alloc("s1p", [128, 2, 4, 2, 4]) for _ in range(2)]
        s2p = [A.alloc("s2p", [128, 2, 4, 2, 4]) for _ in range(2)]
        TAB2 = A.alloc("TAB2", [64, 2, 2, 5, 128])
        q1 = A.alloc("q1", [64, 128])
        q2 = A.alloc("q2", [64, 128])
        CP('dve', TAB2[:, 0, 0, 0, :], TABs[:, 0, 0, 7, :], ['TABs'], ['TAB2'])
        CP('dve', TAB2[:, 1, 1, 0, :], TABs[:, 1, 1, 7, :], ['TABs', 'TAB2'], ['TAB2'])
        for j in range(1, 5):
            pr_, pi_ = TAB2[:, 0, 0, j - 1, :], TAB2[:, 1, 1, j - 1, :]
            TT('dve', q1[:], pr_, pr_, ALU.mult, ['TAB2', 'q1'], ['q1'])
            TT('dve', q2[:], pi_, pi_, ALU.mult, ['TAB2', 'q2'], ['q2'])
            TT('dve', TAB2[:, 0, 0, j, :], q1[:], q2[:], ALU.subtract, ['q1', 'q2', 'TAB2'], ['TAB2'])
            TT('dve', q1[:], pr_, pi_, ALU.mult, ['TAB2', 'q1'], ['q1'])
            TS('dve', TAB2[:, 1, 1, j, :], q1[:], 2.0, None, ALU.mult, None, ['q1', 'TAB2'], ['TAB2'])
        CP('dve', TAB2[:, 0, 1, :, :], TAB2[:, 0, 0, :, :], ['TAB2'], ['TAB2'])
        TS('dve', TAB2[:, 1, 0, :, :], TAB2[:, 1, 1, :, :], -1.0, None, ALU.mult, None, ['TAB2'], ['TAB2'])
        Ysb = [A.alloc("Ysb", [128, NCH]) for _ in range(2)]
        Ytok = A.alloc("Ytok", [128, 3, 8, 128])
        g1 = A.alloc("g1", [128, 1024])
        Y2 = A.alloc("Y2", [128, 1024])
        selT = A.alloc("selT", [128, 3, 128])
        P.dma('sp', selT[:], sel3[:, :, :], w=['selT'])
        ysT = [A.alloc("ysT", [128, 128, 8]) for _ in range(2)]
        pieces = [(0, 128), (128, 128), (256, 64)]
        yc = [0]
        MS('pool', Ytok[:], 0.0, ['Ytok'])
        TABsv = TABs[:].rearrange("p a r k (d g) -> p a r k d g", d=2)
        TAB2v = TAB2[:].rearrange("p a r j (d g) -> p a r j d g", d=2)
        H0v = H0[:].rearrange("p r (d g) -> p r d g", d=2)
        rngs = [(0, 256, slice(255, None, -1)), (256, 288, slice(287, 255, -1)), (288, 320, slice(319, 287, -1))]
        def p_load(gb):
            bi = gb % 2
            XB, HPb = Xb2[bi], HP2[bi]
            nX, nH = ('Xb', bi), ('HP', bi)
            XB4 = XB[:].rearrange("p r (d g) c -> p r d g c", d=2)
            XB5 = XB[:].rearrange("p r (d g) (k s) -> p r d g s k", d=2, k=8)
            nXall = [nX, ('Xbd', bi, 0), ('Xbd', bi, 1)]
            for gh in range(2):
                g0 = gb * 8 + gh * 4
                ps_ = slice(gh * 64, (gh + 1) * 64)
                P.dma('sp', xmp[:, gh * 4:(gh + 1) * 4, :, :, gh * 64:(gh + 1) * 64], xmat_d[:, g0:g0 + 4, :, :, :], w=['xmp'])
                P.dma('sp', clb2[bi][ps_, :, :, :, :], cl_d[:, g0:g0 + 4, :, :, :], w=[('clb', bi)])
            P.dma('sp', mgb[bi][:], mg_d[:, gb * 8:(gb + 1) * 8, :], w=[('mgb', bi)])

        def p_xc(gb):
            bi = gb % 2
            XB, HPb = Xb2[bi], HP2[bi]
            nX, nH = ('Xb', bi), ('HP', bi)
            XB4 = XB[:].rearrange("p r (d g) c -> p r d g c", d=2)
            XB5 = XB[:].rearrange("p r (d g) (k s) -> p r d g s k", d=2, k=8)
            nXall = [nX, ('Xbd', bi, 0), ('Xbd', bi, 1)]
            for g4 in range(4):
                for d in range(2):
                    for ri in range(2):
                        b = nextps()
                        for (c0_, c1_, rv) in ([(0, NCH, slice(0, NCH))] if d == 0 else rngs):
                            for gh in range(2):
                                g = gb * 8 + gh * 4 + g4
                                lhs = xmp[:, gh * 4 + g4, d, ri, :]
                                MM(ps[b][:, c0_:c1_], lhs, U2[:, g, rv], gh == 0, gh == 1, ['xmp'], [('ps', b)])
                        CP('act', XB[:, ri, d * 4 + g4, :].rearrange("p (k s) -> p s k", k=8),
                           ps[b][:, 0:NCH].rearrange("p (s k) -> p s k", k=8), [('ps', b), nX, ('Xbd', bi, 0), ('Xbd', bi, 1)], [nX])

        def p_rec(gb):
            bi = gb % 2
            XB, HPb = Xb2[bi], HP2[bi]
            nX, nH = ('Xb', bi), ('HP', bi)
            XB4 = XB[:].rearrange("p r (d g) c -> p r d g c", d=2)
            XB5 = XB[:].rearrange("p r (d g) (k s) -> p r d g s k", d=2, k=8)
            nXall = [nX, ('Xbd', bi, 0), ('Xbd', bi, 1)]
            for gh in range(2):
                g0 = gb * 8 + gh * 4
                ps_ = slice(gh * 64, (gh + 1) * 64)
                for a in range(2):
                    for ri in range(2):
                        CP('dve', TABb[ps_, a, ri], TABsv[:, a, ri, :, :, g0:g0 + 4], ['TABs', 'TABb'], ['TABb'])
                        CP('dve', TAB2b[ps_, a, ri], TAB2v[:, a, ri, :, :, g0:g0 + 4], ['TAB2', 'TAB2b'], ['TAB2b'])
                CP('dve', h0b[ps_], H0v[:, :, :, g0:g0 + 4], ['h0b'], ['h0b'])
            eng = 'dve'

            def ctx(d):
                return (('Xbd', bi, d), W1[d], W2[d], HS[d], ('W1', d), ('W2', d), ('HS', d), ('S1', d), ('S2', d))

            def ak(a, k, n, d):
                return TABb[:, a, :, k, d, :][:, :, :, None].broadcast_to([128, 2, 4, n])

            def aj(a, shape, j, d):
                v = TAB2b[:, a, :, j, d, :]
                for _ in range(len(shape) - 3):
                    v = v.unsqueeze(len(v.shape))
                return v.broadcast_to(shape)
            for k in range(1, 8):
                for d in range(2):
                    nXd, w1, w2, hs, nW1, nW2, nHs, nS1, nS2 = ctx(d)
                    prev = XB5[:, :, d, :, :, k - 1]
                    TT(eng, w1[:], prev, ak(0, 0, 40, d), ALU.mult, [nX, nXd, nW1, 'TABb'], [nW1])
                    SWP(eng, w2[:], prev, ak(1, 0, 40, d), [nX, nXd, nW2, 'TABb'], [nW2])
                for d in range(2):
                    nXd, w1, w2, hs, nW1, nW2, nHs, nS1, nS2 = ctx(d)
                    cur = XB5[:, :, d, :, :, k]
                    TT(eng, cur, cur, w1[:], ALU.add, [nX, nXd, nW1], [nXd])
                    TT(eng, cur, cur, w2[:], ALU.add, [nX, nXd, nW2], [nXd])
            for d in range(2):
                nXd, w1, w2, hs, nW1, nW2, nHs, nS1, nS2 = ctx(d)
                P.add(eng, lambda e, hs=hs: e.memset(hs[:, :, :, 32:40:4], 0.0), [nHs], [nHs])
                CP(eng, hs[:, :, :, 0], h0b[:, :, d, :], [nHs, 'h0b'], [nHs])
                CP(eng, hs[:, :, :, 1:32], XB5[:, :, d, :, 0:31, 7], [nX, nXd, nHs], [nHs])
                hp = hs[:, :, :, 32:40].rearrange("p r g (s k) -> p r g s k", k=4)
                xpv = XB5[:, :, d, :, 32:40, 7].rearrange("p r g (s k) -> p r g s k", k=4)
                for ri in range(2):
                    CP(eng, hp[:, ri, :, :, 1:4], xpv[:, ri, :, :, 0:3], [nX, nXd, nHs], [nHs])
            for j in range(5):
                o = 1 << j
                n = 32 - o
                for d in range(2):
                    nXd, w1, w2, hs, nW1, nW2, nHs, nS1, nS2 = ctx(d)
                    src = hs[:, :, :, 0:n]
                    TT(eng, w1[:, :, :, 0:n], src, aj(0, [128, 2, 4, n], j, d), ALU.mult, [nHs, nW1, 'TAB2b'], [nW1])
                    SWP(eng, w2[:, :, :, 0:n], src, aj(1, [128, 2, 4, n], j, d), [nHs, nW2, 'TAB2b'], [nW2])
                for d in range(2):
                    nXd, w1, w2, hs, nW1, nW2, nHs, nS1, nS2 = ctx(d)
                    dst = hs[:, :, :, o:32]
                    TT(eng, dst, dst, w1[:, :, :, 0:n], ALU.add, [nHs, nW1], [nHs])
                    TT(eng, dst, dst, w2[:, :, :, 0:n], ALU.add, [nHs, nW2], [nHs])
                if o < 4:
                    m = 4 - o
                    for d in range(2):
                        nXd, w1, w2, hs, nW1, nW2, nHs, nS1, nS2 = ctx(d)
                        hp = hs[:, :, :, 32:40].rearrange("p r g (s k) -> p r g s k", k=4)
                        srcp = hp[:, :, :, :, 0:m]
                        dstp = hp[:, :, :, :, o:4]
                        s1v = s1p[d][:, :, :, :, 0:m]
                        s2v = s2p[d][:, :, :, :, 0:m]
                        a0 = aj(0, [128, 2, 4, 2, m], j, d)
                        for ri in range(2):
                            TT(eng, s1v[:, ri], srcp[:, ri], a0[:, ri], ALU.mult, [nHs, nS1, 'TAB2b'], [nS1])
                        SWP(eng, s2v, srcp, aj(1, [128, 2, 4, 2, m], j, d), [nHs, nS2, 'TAB2b'], [nS2])
                        for ri in range(2):
                            TT(eng, dstp[:, ri], dstp[:, ri], s1v[:, ri], ALU.add, [nHs, nS1], [nHs])
                            TT(eng, dstp[:, ri], dstp[:, ri], s2v[:, ri], ALU.add, [nHs, nS2], [nHs])
            for k in range(8):
                for d in range(2):
                    nXd, w1, w2, hs, nW1, nW2, nHs, nS1, nS2 = ctx(d)
                    TT(eng, w1[:], hs[:], ak(0, k, 40, d), ALU.mult, [nHs, nW1, 'TABb'], [nW1])
                    SWP(eng, w2[:], hs[:], ak(1, k, 40, d), [nHs, nW2, 'TABb'], [nW2])
                for d in range(2):
                    nXd, w1, w2, hs, nW1, nW2, nHs, nS1, nS2 = ctx(d)
                    cur = XB5[:, :, d, :, :, k]
                    TT(eng, cur, cur, w1[:], ALU.add, [nX, nXd, nW1], [nXd])
                    TT(eng, cur, cur, w2[:], ALU.add, [nX, nXd, nW2], [nXd])

        def p_hp(gb):
            bi = gb % 2
            XB, HPb = Xb2[bi], HP2[bi]
            nX, nH = ('Xb', bi), ('HP', bi)
            XB4 = XB[:].rearrange("p r (d g) c -> p r d g c", d=2)
            XB5 = XB[:].rearrange("p r (d g) (k s) -> p r d g s k", d=2, k=8)
            nXall = [nX, ('Xbd', bi, 0), ('Xbd', bi, 1)]
            for sq_ in range(2):
                CP('act', Hl2[:, :, sq_, :, gb, :], XB5[:, :, :, :, 35 + 4 * sq_, 7], nXall + ['Hl2'], ['Hl2'])
            HP4 = HPb[:].rearrange("p r (d g) c -> p r d g c", d=2)
            XBs = XB[:].rearrange("p r q (k s) -> p r q s k", k=8)
            for ri in range(2):
                CP('act', HPb[:, ri, :, 1:257].rearrange("p q (s k) -> p q s k", k=8), XBs[:, ri, :, 0:32, :], nXall + [nH], [nH])
                CP('act', HPb[:, ri, :, 257:289].rearrange("p q (s k) -> p q s k", k=8), XBs[:, ri, :, 32:36, :], nXall + [nH], [nH])
                CP('act', HPb[:, ri, :, 289:313].rearrange("p q (s k) -> p q s k", k=8), XBs[:, ri, :, 36:39, :], nXall + [nH], [nH])
                CP('act', HPb[:, ri, :, 313:320], XBs[:, ri, :, 39, 0:7], nXall + [nH], [nH])
            CP('act', HP4[:, :, :, :, 0], h0b[:], [nH, 'h0b'], [nH])
            P.add('pool', lambda e, HPb=HPb: e.memset(HPb[:, :, :, 256:289:32], 0.0), [nH], [nH])

        def p_y(gb):
            bi = gb % 2
            XB, HPb = Xb2[bi], HP2[bi]
            nX, nH = ('Xb', bi), ('HP', bi)
            XB4 = XB[:].rearrange("p r (d g) c -> p r d g c", d=2)
            XB5 = XB[:].rearrange("p r (d g) (k s) -> p r d g s k", d=2, k=8)
            nXall = [nX, ('Xbd', bi, 0), ('Xbd', bi, 1)]
            for g8 in range(8):
                g = gb * 8 + g8
                gh, g4 = g8 // 4, g8 % 4
                ps_ = slice(gh * 64, (gh + 1) * 64)
                b = nextps()
                yi = yc[0] % 2
                yc[0] += 1
                o = ps[b]
                MM(o[:, 0:NCH], mgb[bi][:, g8, :], U2[:, g, :], True, False, [('mgb', bi)], [('ps', b)])
                for ri in range(2):
                    MM(o[:, 0:NCH], clb2[bi][ps_, g4, 0, ri, :], HPb[ps_, ri, g4, :], False, False, [nH, ('clb', bi)], [('ps', b)])
                for ri in range(2):
                    l = clb2[bi][ps_, g4, 1, ri, :]
                    for qi, (c0_, c1_, rv) in enumerate(rngs):
                        MM(o[:, c0_:c1_], l, HPb[ps_, ri, 4 + g4, rv], False, (ri == 1 and qi == 2), [nH, ('clb', bi)], [('ps', b)])
                CP('act', Ysb[yi][:], o[:, 0:NCH], [('ps', b), ('Ysb', yi)], [('Ysb', yi)])
                b2 = nextps()
                for pi, (c0, n) in enumerate(pieces):
                    TR(ps[b2][0:n, pi * 128:(pi + 1) * 128], Ysb[yi][:, c0:c0 + n], ident[:, :], [('Ysb', yi)], [('ps', b2)])
                for pi, (c0, n) in enumerate(pieces):
                    CP('act', Ytok[0:n, pi, :, g8 * 16:(g8 + 1) * 16],
                       ps[b2][0:n, pi * 128:(pi + 1) * 128].rearrange("p (r i) -> p r i", r=8), [('ps', b2), 'Ytok'], ['Ytok'])

        def p_tail(gb):
            bi = gb % 2
            XB, HPb = Xb2[bi], HP2[bi]
            nX, nH = ('Xb', bi), ('HP', bi)
            XB4 = XB[:].rearrange("p r (d g) c -> p r d g c", d=2)
            XB5 = XB[:].rearrange("p r (d g) (k s) -> p r d g s k", d=2, k=8)
            nXall = [nX, ('Xbd', bi, 0), ('Xbd', bi, 1)]
            Yf = Ytok[:].rearrange("p a r c -> p a (r c)")
            bs = [nextps(), nextps()]
            for hh in range(2):
                for pi in range(3):
                    MM(ps[bs[hh]][:, :], selT[:, pi, :], Yf[:, pi, hh * 512:(hh + 1) * 512], pi == 0, pi == 2,
                       ['Ytok', 'selT'], [('ps', bs[hh])])
                CP('act', Y2[:, hh * 512:(hh + 1) * 512], ps[bs[hh]][:, :], [('ps', bs[hh]), 'Y2'], ['Y2'])
            TT('dve', g1[:], Y2[:], Y2[:], ALU.mult, ['Y2', 'g1'], ['g1'])
            TS('dve', g1[:], g1[:], 0.044715, 1.0, ALU.mult, ALU.add, ['g1'], ['g1'])
            TT('dve', g1[:], g1[:], Y2[:], ALU.mult, ['g1', 'Y2'], ['g1'])
            ACT(g1[:], g1[:], AF.Sigmoid, ['g1'], ['g1'], scale=GELU_C)
            TT('dve', g1[:], g1[:], Y2[:], ALU.mult, ['Y2', 'g1'], ['g1'])
            g1v = g1[:].rearrange("p (r c) -> p r c", r=8)
            yti = gb % 2
            for r4 in range(2):
                b3 = nextps()
                for q in range(4):
                    TR(ps[b3][:, q * 128:(q + 1) * 128], g1v[:, r4 * 4 + q, :], ident[:, :], ['g1'], [('ps', b3)])
                CP('act', ysT[yti][:, :, r4 * 4:(r4 + 1) * 4].rearrange("p c r -> p r c"),
                   ps[b3][:, :].rearrange("p (q c) -> p q c", q=4), [('ps', b3), ('ysT', yti)], [('ysT', yti)])
            P.dma('pool', ysg_d[gb * 128:(gb + 1) * 128, :], ysT[yti][:].rearrange("p c r -> p (c r)"),
                  r=[('ysT', yti)], w=[('ysg_d', gb)])

        p_load(0)
        p_xc(0)
        p_rec(0)
        for gb in range(8):
            if gb + 1 < 8:
                p_load(gb + 1)
                p_xc(gb + 1)
            p_hp(gb)
            p_y(gb)
            if gb + 1 < 8:
                p_rec(gb + 1)
            p_tail(gb)
        st = A.alloc("st", [64, 128])
        for ri, dst in ((0, sre_o), (1, sim_o)):
            for sq_ in range(2):
                b = nextps()
                TR(ps[b][0:64, 0:128], Hl2[:, ri, sq_].rearrange("p d b g -> p (d b g)"), ident[:, :], ['Hl2'], [('ps', b)])
                CP('dve', st[:], ps[b][0:64, 0:128], [('ps', b), 'st'], ['st'])
                P.dma('pool', dst[sq_, :, :], st[:], r=['st'], w=[('so', ri, sq_)])

    if 'D' in stages:
        ssm_main()
        P.barrier()
    A.reset(PERS0)

    wback = {}

    def attention():
        Wg = A.alloc("Wg", [128, 8, 1024], BF16)
        Wso = A.alloc("Wso", [128, 8, D], BF16)
        Wao = A.alloc("Wao", [128, 8, D], BF16)
        wst = [A.alloc("wst", [128, 4, 512]) for _ in range(2)]
        wback.update(Wg=Wg, Wso=Wso, Wao=Wao, mark=A.mark())
        chunks = []
        for (src, nk, ncol, dstw, nm) in ((w_glu, 8, 1024, Wg, 'Wg'), (w_so, 8, D, Wso, 'Wso'), (w_ao, 8, D, Wao, 'Wao')):
            v = src.rearrange("(k p) n -> p k n", p=128)
            for k4 in range(nk // 4):
                for c in range(ncol // 512):
                    chunks.append((v, k4, c, dstw, nm))
        wn = [0]

        def wload(n):
            for _ in range(n):
                if wn[0] >= len(chunks):
                    return
                v, k4, c, dstw, nm = chunks[wn[0]]
                i = wn[0] % 2
                wn[0] += 1
                P.dma('sp', wst[i][:], v[:, k4 * 4:(k4 + 1) * 4, c * 512:(c + 1) * 512], w=[('wst', i)])
                CP(['dve', 'act'][i], dstw[:, k4 * 4:(k4 + 1) * 4, c * 512:(c + 1) * 512], wst[i][:], [('wst', i), nm], [nm])
        pair_tile = [4, 5, 0, 1, 2, 3, 6, 7]
        V1 = A.alloc("V1", [128, 12, 16, 65], BF16)
        V1c = A.alloc("V1c", [128, 4, 16, 65], BF16)
        vl = [A.alloc("vl", [128, 1024]) for _ in range(2)]
        rmf = A.alloc("rmf", [128, 8, 8])
        rmb = A.alloc("rmb", [128, 8, 8], BF16)
        P.dma('sp', rmf[:], rmask[:, :, :], w=['rmf'])
        CP('dve', rmb[:], rmf[:], ['rmf'], ['rmb'])
        MS('pool', V1[:], 1.0, ['V1'])
        MS('pool', V1c[:], 1.0, ['V1c'])
        for t in range(16):
            src = v_d[t * 128:(t + 1) * 128, :] if t < 12 else cv[(t - 12) * 128:(t - 11) * 128, :]
            P.dma('sp', vl[t % 2][:], src, w=[('vl', t % 2)])
            dstv = V1[:, t, :, 0:64] if t < 12 else V1c[:, t - 12, :, 0:64]
            CP('dve', dstv, vl[t % 2][:].rearrange("p (h d) -> p h d", h=16), [('vl', t % 2), 'V1', 'V1c'], ['V1', 'V1c'])
        qT = [A.alloc("qT", [64, 1024], BF16) for _ in range(2)]
        kT = [A.alloc("kT", [64, 1536], BF16) for _ in range(2)]
        kcT = [A.alloc("kcT", [64, 512], BF16) for _ in range(2)]
        TB = [A.alloc("TB", [128, 17, 64]) for _ in range(2)]
        Pc = A.alloc("Pc", [128, 4, 512], BF16)
        Pw_ = [A.alloc("Pw_", [128, 6, 64], BF16) for _ in range(2)]
        Sw = [A.alloc("Sw", [128, 6, 64]) for _ in range(2)]
        Pp = [A.alloc("Pp", [128, 2, 256], BF16) for _ in range(2)]
        Ah = A.alloc("Ah", [128, 2, 64])
        Ar = A.alloc("Ar", [64, 8, 64])
        rc = A.alloc("rc", [128, 1])
        aT = [A.alloc("aT", [64, 512]) for _ in range(2)]
        cnt = [0]
        ac = [0]
        for h in range(16):
            hi = h % 2
            wload(2)
            P.dma('sp', qT[hi][:], qT_d[h * 64:(h + 1) * 64, :], w=[('qT', hi)])
            P.dma('sp', kT[hi][:], kT_d[h * 64:(h + 1) * 64, :], w=[('kT', hi)])
            P.dma('sp', kcT[hi][:], kcT_d[h * 64:(h + 1) * 64, :], w=[('kcT', hi)])
            P.dma('sp', TB[hi][:], tb_d[h, :, :, :], w=[('TB', hi)])
            for ct in range(4):
                b = nextps()
                MM(ps[b][:, :], kcT[hi][:, ct * 128:(ct + 1) * 128], qT[hi][:, 0:512], True, True,
                   [('kcT', hi), ('qT', hi)], [('ps', b)])
                ACT(Pc[:, ct, :], ps[b][:, :], AF.Exp, [('ps', b), 'Pc'], ['Pc'], scale=0.125)
            for ql in range(8):
                j0 = min(ql, 4)
                j1 = max(ql, 4) + 7
                p0 = j0 // 2
                p1 = j1 // 2
                npair = p1 - p0 + 1
                wi_ = cnt[0] % 2
                cnt[0] += 1
                b = nextps()
                for pi in range(npair):
                    tl = pair_tile[p0 + pi]
                    MM(ps[b][:, pi * 64:(pi + 1) * 64], kT[hi][:, tl * 128:(tl + 1) * 128], qT[hi][:, ql * 64:(ql + 1) * 64],
                       True, True, [('kT', hi), ('qT', hi)], [('ps', b)])
                i0 = 2 * p0 - ql + 3 + 1
                STT('dve', Sw[wi_][:, 0:npair, :], ps[b][:, 0:npair * 64].rearrange("p (a q) -> p a q", q=64), 0.125,
                    TB[hi][:, i0:i0 + 2 * npair:2, :], ALU.mult, ALU.add, [('ps', b), ('TB', hi), ('Sw', wi_)], [('Sw', wi_)])
                ACT(Pw_[wi_][:, 0:npair, :], Sw[wi_][:, 0:npair, :], AF.Exp, [('Sw', wi_), ('Pw_', wi_)], [('Pw_', wi_)])
                TT('dve', Pw_[wi_][:, 0:npair, :], Pw_[wi_][:, 0:npair, :],
                   rmb[:, ql, p0:p0 + npair][:, :, None].broadcast_to([128, npair, 64]), ALU.mult,
                   [('Pw_', wi_), 'rmb'], [('Pw_', wi_)])
                b2 = nextps()
                for pi in range(npair):
                    MM(ps[b2][0:64, 0:65], Pw_[wi_][:, pi, :], V1[:, pair_tile[p0 + pi], h, :], pi == 0, False,
                       [('Pw_', wi_), 'V1'], [('ps', b2)])
                for ct in range(4):
                    MM(ps[b2][0:64, 0:65], Pc[:, ct, ql * 64:(ql + 1) * 64], V1c[:, ct, h, :], False, ct == 3,
                       ['Pc', 'V1c'], [('ps', b2)])
                RCP(rc[0:64, :], ps[b2][0:64, 64:65], [('ps', b2), 'rc'], ['rc'])
                TS('dve', Ar[:, ql, :], ps[b2][0:64, 0:64], rc[0:64, 0:1], None, ALU.mult, None, [('ps', b2), 'rc', 'Ar'], ['Ar'])
            b = nextps()
            for q in range(8):
                TR(ps[b][0:64, q * 64:(q + 1) * 64], Ar[:, q, :], ident[0:64, 0:64], ['Ar'], [('ps', b)])
            ai = ac[0] % 2
            ac[0] += 1
            CP('act', aT[ai][:], ps[b][0:64, :], [('ps', b), ('aT', ai)], [('aT', ai)])
            P.dma('pool', attnT_d[h * 64:(h + 1) * 64, 0:512], aT[ai][:], r=[('aT', ai)], w=[('attnT', h)])
            for sq_ in range(2):
                tk = 1024 + sq_ * 256
                tq = 512 + sq_ * 256
                pi_ = cnt[0] % 2
                cnt[0] += 1
                for kt in range(2):
                    b = nextps()
                    MM(ps[b][:, 0:256], kT[hi][:, tk + kt * 128:tk + (kt + 1) * 128], qT[hi][:, tq:tq + 256], True, True,
                       [('kT', hi), ('qT', hi)], [('ps', b)])
                    ACT(Pp[pi_][:, kt, :], ps[b][:, 0:256], AF.Exp, [('ps', b), ('Pp', pi_)], [('Pp', pi_)], scale=0.125)
                b4 = nextps()
                for qc in range(2):
                    b2 = nextps()
                    for kt in range(2):
                        MM(ps[b2][:, 0:65], Pp[pi_][:, kt, qc * 128:(qc + 1) * 128], V1[:, 8 + sq_ * 2 + kt, h, :], kt == 0, kt == 1,
                           [('Pp', pi_), 'V1'], [('ps', b2)])
                    RCP(rc[:, :], ps[b2][:, 64:65], [('ps', b2), 'rc'], ['rc'])
                    TS('dve', Ah[:, qc, :], ps[b2][:, 0:64], rc[:, 0:1], None, ALU.mult, None, [('ps', b2), 'rc', 'Ah'], ['Ah'])
                    TR(ps[b4][0:64, qc * 128:(qc + 1) * 128], Ah[:, qc, :], ident[:, :], ['Ah'], [('ps', b4)])
                ai = ac[0] % 2
                ac[0] += 1
                CP('act', aT[ai][:, 0:256], ps[b4][0:64, 0:256], [('ps', b4), ('aT', ai)], [('aT', ai)])
                P.dma('pool', attnT_d[h * 64:(h + 1) * 64, tq:tq + 256], aT[ai][:, 0:256], r=[('aT', ai)], w=[('attnTp', h, sq_)])

    if 'E' in stages:
        attention()
        P.barrier()
    A.reset(wback.get('mark', PERS0))

    mgd_d = dscr("mgd_d", [D, 1024], BF16)

    def back1():
        Wg, Wso, Wao = wback['Wg'], wback['Wso'], wback['Wao']
        bg = A.alloc("bg", [128, 8])
        P.dma('sp', bg[:], bglu[:, :], w=['bg'])
        TBK = 256
        ysg = A.alloc("ysg", [128, 8, TBK])
        ysgb = A.alloc("ysgb", [128, 8, TBK], BF16)
        zs = A.alloc("zs", [128, 8, TBK])
        za = A.alloc("za", [128, 8, TBK])
        at = A.alloc("at", [128, 8, TBK])
        atb = A.alloc("atb", [128, 8, TBK], BF16)
        ys2 = A.alloc("ys2", [128, 8, TBK], BF16)
        glu = A.alloc("glu", [128, TBK])
        gsa = [A.alloc("gsa", [128, 2, TBK]) for _ in range(2)]
        mg = A.alloc("mg", [128, 16, TBK], BF16)
        m1 = A.alloc("m1", [128, TBK])
        m2 = A.alloc("m2", [128, TBK])
        fmv = lambda dten, tb: dten[:, tb * TBK:(tb + 1) * TBK].rearrange("(k p) n -> p k n", p=128)
        for tb in range(NB // TBK):
            P.dma('sp', ysg[:], fmv(ysg_d, tb), w=['ysg'])
            P.dma('sp', zs[:], fmv(zs_d, tb), w=['zs'])
            P.dma('sp', za[:], fmv(za_d, tb), w=['za'])
            P.dma('sp', at[:], fmv(attnT_d, tb), w=['at'])
            CP('act', ysgb[:], ysg[:], ['ysg', 'ysgb'], ['ysgb'])
            TT('dve', at[:], at[:], za[:], ALU.mult, ['at', 'za'], ['at'])
            CP('act', atb[:], at[:], ['at', 'atb'], ['atb'])
            for m in range(8):
                b = nextps()
                for k in range(8):
                    MM(ps[b][:, 0:TBK], Wg[:, k, m * 128:(m + 1) * 128], ysgb[:, k, :], k == 0, k == 7, ['Wg', 'ysgb'], [('ps', b)])
                ACT(glu[:], ps[b][:, 0:TBK], AF.Sigmoid, [('ps', b), 'bg', 'glu'], ['glu'], bias=bg[:, m:m + 1])
                TT('dve', glu[:], glu[:], ysg[:, m, :], ALU.mult, ['glu', 'ysg'], ['glu'])
                TT('dve', ys2[:, m, :], glu[:], zs[:, m, :], ALU.mult, ['glu', 'zs', ('ys2', m)], [('ys2', m)])
            allys2 = [('ys2', m) for m in range(8)]
            for nn in range(16):
                gi = nn % 2
                P.dma('sp', gsa[gi][:, 0, :], gs_d[nn * 128:(nn + 1) * 128, tb * TBK:(tb + 1) * TBK], w=[('gsa', gi, 0)])
                P.dma('sp', gsa[gi][:, 1, :], ga_d[nn * 128:(nn + 1) * 128, tb * TBK:(tb + 1) * TBK], w=[('gsa', gi, 1)])
                b = nextps()
                b2 = nextps()
                for k in range(8):
                    MM(ps[b][:, 0:TBK], Wso[:, k, nn * 128:(nn + 1) * 128], ys2[:, k, :], k == 0, k == 7, ['Wso'] + allys2, [('ps', b)])
                for k in range(8):
                    MM(ps[b2][:, 0:TBK], Wao[:, k, nn * 128:(nn + 1) * 128], atb[:, k, :], k == 0, k == 7, ['Wao', 'atb'], [('ps', b2)])
                TT('dve', m1[:], ps[b][:, 0:TBK], gsa[gi][:, 0, :], ALU.mult, [('ps', b), ('gsa', gi, 0), 'm1'], ['m1'])
                TT('dve', m2[:], ps[b2][:, 0:TBK], gsa[gi][:, 1, :], ALU.mult, [('ps', b2), ('gsa', gi, 1), 'm2'], ['m2'])
                TT('dve', mg[:, nn, :], m1[:], m2[:], ALU.add, ['m1', 'm2', 'mg'], ['mg'])
            P.dma('pool', fmv(mgd_d, tb), mg[:], r=['mg'], w=[('mgd_d', tb)])

    def back2():
        Wo = A.alloc("Wo", [128, 16, D], BF16)
        wst = [A.alloc("wst", [128, 4, 512]) for _ in range(2)]
        n = [0]
        v_ = w_o.rearrange("(k p) n -> p k n", p=128)
        for k4 in range(4):
            for c in range(4):
                i = n[0] % 2
                n[0] += 1
                P.dma('sp', wst[i][:], v_[:, k4 * 4:(k4 + 1) * 4, c * 512:(c + 1) * 512], w=[('wst', i)])
                CP(['dve', 'act'][i], Wo[:, k4 * 4:(k4 + 1) * 4, c * 512:(c + 1) * 512], wst[i][:], [('wst', i), 'Wo'], ['Wo'])
        gbc = [A.alloc("gbc", [128, D]) for _ in range(2)]
        for v in range(2):
            P.dma('sp', gbc[v][:], mod_d[v:v + 1, 2 * D:3 * D].partition_broadcast(128), w=[('gbc', v)])
        mgl = [A.alloc("mgl", [128, 16, 128], BF16) for _ in range(2)]
        xin = [A.alloc("xin", [128, D]) for _ in range(2)]
        yo = [A.alloc("yo", [128, D]) for _ in range(2)]
        for t in range(NB // 128):
            tok0 = t * 128
            xi = t % 2
            v = 1 if tok0 < 512 else 0
            src = xown[tok0:tok0 + 128, :] if tok0 < 512 else xp[tok0 - 512:tok0 - 512 + 128, :]
            dsto = ys_o[tok0:tok0 + 128, :] if tok0 < 512 else yp_o[tok0 - 512:tok0 - 512 + 128, :]
            P.dma('sp', xin[xi][:], src, w=[('xin', xi)])
            P.dma('sp', mgl[xi][:], mgd_d[:, tok0:tok0 + 128].rearrange("(k p) n -> p k n", p=128), w=[('mgl', xi)])
            for nb in range(4):
                b = nextps()
                for k in range(16):
                    MM(ps[b][:, :], mgl[xi][:, k, :], Wo[:, k, nb * 512:(nb + 1) * 512], k == 0, k == 15,
                       ['Wo', ('mgl', xi)], [('ps', b)])
                TT('dve', yo[xi][:, nb * 512:(nb + 1) * 512], ps[b][:, :], gbc[v][:, nb * 512:(nb + 1) * 512], ALU.mult,
                   [('ps', b), ('gbc', v), ('yo', xi)], [('yo', xi)])
            TT('dve', yo[xi][:], yo[xi][:], xin[xi][:], ALU.add, [('yo', xi), ('xin', xi)], [('yo', xi)])
            P.dma('pool', dsto, yo[xi][:], r=[('yo', xi)], w=[('yout', tok0)])

    if 'F' in stages:
        back1()
        P.barrier()
        A.reset(PERS0)
        back2()
        P.barrier()
    P.emit(es)
    es.close()
    return nc, P


_NC = None


def _consts():
    c = {}
    c["ident"] = np.eye(128, dtype=np.float32)
    s = np.arange(128) // 16
    c["maskf"] = (s[None, :] >= s[:, None]).astype(np.float32)
    c["maskb"] = (s[:, None] >= s[None, :]).astype(np.float32)
    kc = np.arange(64)[:, None]
    qc = np.arange(64)[None, :]
    cs = np.clip(qc - 8, 0, 48)
    c["colmask"] = np.where((kc >= cs) & (kc < cs + 16), 0.0, NEG).astype(np.float32)
    dc = np.clip(kc - qc + 15, 0, 30)
    oh = np.zeros((31, 64, 64), np.float32)
    for q in range(64):
        for k in range(64):
            oh[dc[k, q], q, k] = 1.0
    c["onehot"] = oh
    return c


def kernel(x_prompt, x_sample, cache_k, cache_v, state_ssm_re, state_ssm_im, c, c_ctx,
           norm_w, w_ada, b_ada, w_in, q_norm_w, k_norm_w, rel_pos_bias,
           ssm_a_re, ssm_a_im, ssm_log_dt, ssm_b_re, ssm_b_im, ssm_c_re, ssm_c_im, ssm_d,
           w_glu, b_glu, w_ssm_out, w_att_out, w_o):
    global _NC
    f = lambda a: np.ascontiguousarray(np.asarray(a, dtype=np.float32))
    if _NC is None:
        _NC = build()
    nc = _NC[0]
    x_prompt = f(x_prompt)
    x_sample = f(x_sample)
    c = f(c)
    c_ctx = f(c_ctx)
    shared = _consts()
    shared.update({
        "w_ada": f(w_ada)[0], "b_ada": f(b_ada)[0][None, :], "norm_w": f(norm_w)[0][None, :], "w_in": f(w_in)[0],
        "qnw": f(q_norm_w)[0][None, :], "knw": f(k_norm_w)[0][None, :],
        "rpbT": np.ascontiguousarray(f(rel_pos_bias)[0].reshape(240, 31).T),
        "are": f(ssm_a_re)[0].reshape(128, 64), "aim": f(ssm_a_im)[0].reshape(128, 64),
        "logdt": f(ssm_log_dt)[0].reshape(1, 128),
        "bre": f(ssm_b_re)[0].reshape(128, 64, 16), "bim": f(ssm_b_im)[0].reshape(128, 64, 16),
        "cre": f(ssm_c_re)[0].reshape(128, 16, 64), "cim": f(ssm_c_im)[0].reshape(128, 16, 64),
        "dcol": np.ascontiguousarray(np.tile(f(ssm_d)[0].reshape(64, 16).T, (8, 1))),
        "w_glu": f(w_glu)[0], "bglu": np.ascontiguousarray(f(b_glu)[0].reshape(8, 128).T),
        "w_so": f(w_ssm_out)[0], "w_ao": f(w_att_out)[0], "w_o": f(w_o)[0],
    })
    in_maps = []
    for core in range(NCORES):
        b, q = core % 2, core // 2
        m = dict(shared)
        m["xs"] = x_sample[b]
        m["xown"] = x_sample[b, q * 512:(q + 1) * 512]
        kb = 8 * q - 4
        hrows = [min(max(kb + j, 0), 31) for j in (0, 1, 2, 3, 12, 13, 14, 15)]
        m["xhalo"] = np.ascontiguousarray(np.concatenate([x_sample[b, r * 64:(r + 1) * 64] for r in hrows], 0))
        sel = np.zeros((128, 3, 128), np.float32)
        for pi in range(2):
            for cc in range(128):
                o = pi * 128 + cc - 64 * q
                if 0 <= o < 64:
                    sel[cc, pi, o] = 1.0
        for cc in range(64):
            sel[cc, 2, 64 + cc] = 1.0
        m["sel3"] = sel
        rm = np.zeros((128, 8, 8), np.float32)
        for ql in range(8):
            rs = min(max(8 * q + ql - 4, 0), 24)
            for j in range(16):
                if rs - kb <= j < rs - kb + 8:
                    rm[(j % 2) * 64:(j % 2) * 64 + 64, ql, j // 2] = 1.0
        m["rmask"] = rm
        m["xp"] = x_prompt[2 * core:2 * core + 2].reshape(NPR, D)
        m["cond2"] = np.ascontiguousarray(np.stack([c_ctx, c[b]]).reshape(32, 128))
        m["ck"] = f(cache_k)[b, 0].reshape(512, 1024)
        m["cv"] = f(cache_v)[b, 0].reshape(512, 1024)
        m["h0re"] = f(state_ssm_re)[b, 0].reshape(128, 64)
        m["h0im"] = f(state_ssm_im)[b, 0].reshape(128, 64)
        in_maps.append(m)
    res = run_bass_kernel_spmd(nc, in_maps, core_ids=list(range(NCORES)))
    R = res.results
    y_p = np.concatenate([r["yp_o"].reshape(2, 256, D) for r in R], 0)
    y_s = np.zeros((2, NS, D), np.float32)
    for core in range(NCORES):
        b, q = core % 2, core // 2
        y_s[b, q * 512:(q + 1) * 512] = R[core]["ys_o"]
    new_k = np.concatenate([r["ko"].reshape(2, 1, 256, 16, 64) for r in R], 0)
    new_v = np.concatenate([r["vo"].reshape(2, 1, 256, 16, 64) for r in R], 0)
    def _st(a):
        return a.reshape(2, 2, 8, 4, 2, 64).transpose(0, 1, 2, 4, 3, 5).reshape(2, 1, 2, 64, 64)
    st_re = np.concatenate([_st(r["sre_o"]) for r in R], 0)
    st_im = np.concatenate([_st(r["sim_o"]) for r in R], 0)
    return (y_p, y_s, new_k, new_v, st_re, st_im)
```

```python
import numpy as np
from contextlib import ExitStack
import concourse.bass as bass
import concourse.mybir as mybir
from concourse.bass_utils import run_bass_kernel_spmd

F32 = mybir.dt.float32
BF16 = mybir.dt.bfloat16
ALU = mybir.AluOpType
AF = mybir.ActivationFunctionType
AX = mybir.AxisListType

D = 2048
NCORES = 8
EPS = 1e-6
IN_W = 10240
NS = 2048
NPR = 512
NT = 2560
NCH = 320
NEG = -1e30
GELU_C = 1.5957691216057308


class Prog:
    NDMA = 8
    ENGS = ['pe', 'act', 'dve', 'pool', 'sp']

    def __init__(self, nc):
        self.nc = nc
        self.ops = []
        self.lastw = {}
        self.readers = {}
        self.forced = set()

    def add(self, eng, fn, r=(), w=(), dma=False, deps=None):
        idx = len(self.ops)
        dd = set(deps) if deps else set()
        for x in r:
            if x in self.lastw:
                dd.add(self.lastw[x])
        for x in w:
            if x in self.lastw:
                dd.add(self.lastw[x])
            for kk, vv in self.readers.get(x, {}).items():
                if kk == 'dmas':
                    dd.update(vv)
                elif vv is not None:
                    dd.add(vv)
        for x in r:
            self.readers.setdefault(x, {})[(eng, dma)] = idx if not dma else None
            if dma:
                self.readers[x].setdefault('dmas', []).append(idx)
        for x in w:
            self.lastw[x] = idx
            self.readers[x] = {}
        self.ops.append(dict(eng=eng, fn=fn, deps=dd, dma=dma))
        return idx

    def dma(self, eng, out, in_, r=(), w=(), **kw):
        return self.add(eng, lambda e: e.dma_start(out=out, in_=in_, **kw), r, w, dma=True)

    def barrier(self):
        lastc = {}
        lastd = {}
        for i, op in enumerate(self.ops):
            if op['fn'] is None:
                continue
            if op['dma']:
                lastd.setdefault(op['eng'], []).append(i)
            else:
                lastc[op['eng']] = i
        deps = set(lastc.values())
        for e, l in lastd.items():
            deps.update(l[-self.NDMA:])
        self.forced.update(lastc.values())
        for e in self.ENGS:
            self.add(e, None, deps=deps)
        self.lastw = {}
        self.readers = {}

    def emit(self, es):
        nc = self.nc
        ops = self.ops
        engs = self.ENGS
        csem = {e: es.enter_context(nc.semaphore("c_" + e)) for e in engs if e != 'sp'}
        dsem = {e: [es.enter_context(nc.semaphore("d_%s%d" % (e, i))) for i in range(self.NDMA)]
                for e in ['sp', 'act', 'pool']}

        def elide(dop, op):
            return (not dop['dma']) and (not op['dma']) and dop['eng'] == 'pe' and op['eng'] == 'pe' \
                and op['fn'] is not None

        needed = [False] * len(ops)
        for i in self.forced:
            needed[i] = True
        for i, op in enumerate(ops):
            for d in op['deps']:
                if elide(ops[d], op):
                    continue
                needed[d] = True
        ccount = {e: 0 for e in engs}
        dcount = {e: 0 for e in engs}
        ev = [None] * len(ops)
        pre = [None] * len(ops)
        for i, op in enumerate(ops):
            e = op['eng']
            if op['fn'] is None:
                continue
            if op['dma']:
                n = dcount[e]
                dcount[e] += 1
                sem = dsem[e][n % self.NDMA]
                ev[i] = (sem, 16 * (n // self.NDMA + 1))
                if n >= self.NDMA:
                    pre[i] = (sem, 16 * (n // self.NDMA))
            elif needed[i]:
                ccount[e] += 1
                ev[i] = (csem[e], ccount[e])
        per = {e: [] for e in engs}
        for i, op in enumerate(ops):
            per[op['eng']].append(i)
        self.stats = dict(ccount=ccount, dcount=dcount, nops={e: len(per[e]) for e in engs})

        def run(ename, eobj):
            waited = {}
            for i in per[ename]:
                op = ops[i]
                waits = []
                if pre[i] is not None:
                    waits.append(pre[i])
                for d in sorted(op['deps']):
                    if elide(ops[d], op):
                        continue
                    if ev[d] is None:
                        continue
                    waits.append(ev[d])
                for sem, val in waits:
                    key = id(sem)
                    if waited.get(key, 0) >= val:
                        continue
                    waited[key] = val
                    eobj.wait_ge(sem, val)
                if op['fn'] is None:
                    continue
                ins = op['fn'](eobj)
                if ev[i] is not None:
                    ins.then_inc(ev[i][0], 16 if op['dma'] else 1)

        with nc.Block() as block:
            @block.tensor
            def _(e):
                run('pe', e)

            @block.scalar
            def _(e):
                run('act', e)

            @block.vector
            def _(e):
                run('dve', e)

            @block.gpsimd
            def _(e):
                run('pool', e)

            @block.sync
            def _(e):
                run('sp', e)


class Arena:
    BASE = 16512
    LIMIT = 229344

    def __init__(self, nc):
        self.nc = nc
        self.off = self.BASE
        self.cnt = 0

    def alloc(self, name, shape, dt=F32):
        n = 1
        for s in shape[1:]:
            n *= s
        nb = n * (4 if dt == F32 else 2)
        nb = (nb + 63) // 64 * 64
        assert self.off + nb <= self.LIMIT, "SBUF arena overflow at %s: %d + %d" % (name, self.off, nb)
        self.cnt += 1
        t = self.nc.alloc_sbuf_tensor_at("%s_%d" % (name, self.cnt), shape, dt, offset=self.off)
        self.off += nb
        return t

    def mark(self):
        return self.off

    def reset(self, m):
        self.off = m


def build(stages="ACTBDEF"):
    nc = bass.Bass("TRN2", target_bir_lowering=False)
    P = Prog(nc)
    es = ExitStack()
    A = Arena(nc)

    def din(name, shape):
        return nc.dram_tensor(name, shape, F32, kind="ExternalInput").ap()

    def dout(name, shape):
        return nc.dram_tensor(name, shape, F32, kind="ExternalOutput").ap()

    def dscr(name, shape, dt=F32):
        return nc.dram_tensor(name, shape, dt, kind="Internal").ap()

    xs = din("xs", [NS, D])
    xp = din("xp", [NPR, D])
    xown = din("xown", [512, D])
    xhalo = din("xhalo", [512, D])
    sel3 = din("sel3", [128, 3, 128])
    rmask = din("rmask", [128, 8, 8])
    cond2 = din("cond2", [32, 128])
    ck = din("ck", [512, 1024])
    cv = din("cv", [512, 1024])
    h0re = din("h0re", [128, 64])
    h0im = din("h0im", [128, 64])
    are = din("are", [128, 64])
    aim = din("aim", [128, 64])
    logdt = din("logdt", [1, 128])
    bre = din("bre", [128, 64, 16])
    bim = din("bim", [128, 64, 16])
    cre = din("cre", [128, 16, 64])
    cim = din("cim", [128, 16, 64])
    dcol = din("dcol", [128, 64])
    w_ada = din("w_ada", [D, 3 * D])
    b_ada = din("b_ada", [1, 3 * D])
    norm_w = din("norm_w", [1, D])
    w_in = din("w_in", [D, IN_W])
    qnw = din("qnw", [1, 64])
    knw = din("knw", [1, 64])
    rpbT = din("rpbT", [31, 240])
    w_glu = din("w_glu", [1024, 1024])
    bglu = din("bglu", [128, 8])
    w_so = din("w_so", [1024, D])
    w_ao = din("w_ao", [1024, D])
    w_o = din("w_o", [D, D])
    ident_d = din("ident", [128, 128])
    maskf_d = din("maskf", [128, 128])
    maskb_d = din("maskb", [128, 128])
    colmask_d = din("colmask", [64, 64])
    onehot_d = din("onehot", [31, 64, 64])

    ys_o = dout("ys_o", [512, D])
    yp_o = dout("yp_o", [NPR, D])
    ko = dout("ko", [NPR, 1024])
    vo = dout("vo", [NPR, 1024])
    sre_o = dout("sre_o", [2, 64, 128])
    sim_o = dout("sim_o", [2, 64, 128])

    mod_d = dscr("mod_d", [2, 3 * D])
    v_d = dscr("v_d", [1536, 1024])
    qT_d = dscr("qT_d", [1024, 1024], BF16)
    kT_d = dscr("kT_d", [1024, 1536], BF16)
    kcT_d = dscr("kcT_d", [1024, 512], BF16)
    zs_d = dscr("zs_d", [1024, 1024])
    za_d = dscr("za_d", [1024, 1024])
    gs_d = dscr("gs_d", [D, 1024])
    ga_d = dscr("ga_d", [D, 1024])
    ysg_d = dscr("ysg_d", [1024, 1024])
    attnT_d = dscr("attnT_d", [1024, 1024])
    xmat_d = dscr("xmat_d", [128, 64, 2, 2, 64], BF16)
    cl_d = dscr("cl_d", [64, 64, 2, 2, 128], BF16)
    mg_d = dscr("mg_d", [128, 64, 128], BF16)
    tab_d = dscr("tab_d", [64, 2, 2, 8, 128])
    tb_d = dscr("tb_d", [16, 128, 17, 64])

    ps = [es.enter_context(nc.psum_tensor("ps%d" % i, [128, 512], F32)) for i in range(8)]
    pcnt = [0]

    def nextps():
        i = pcnt[0] % 8
        pcnt[0] += 1
        return i

    def TT(eng, out, in0, in1, op, r, w):
        P.add(eng, lambda e: e.tensor_tensor(out=out, in0=in0, in1=in1, op=op), r, w)

    def TS(eng, out, in0, s1, s2, op0, op1, r, w):
        if s2 is None:
            P.add(eng, lambda e: e.tensor_scalar(out=out, in0=in0, scalar1=s1, scalar2=None, op0=op0), r, w)
        else:
            P.add(eng, lambda e: e.tensor_scalar(out=out, in0=in0, scalar1=s1, scalar2=s2, op0=op0, op1=op1), r, w)

    def STT(eng, out, in0, scalar, in1, op0, op1, r, w):
        P.add(eng, lambda e: e.scalar_tensor_tensor(out=out, in0=in0, scalar=scalar, in1=in1, op0=op0, op1=op1), r, w)

    def ACT(out, in_, func, r, w, **kw):
        P.add('act', lambda e: e.activation(out=out, in_=in_, func=func, **kw), r, w)

    def CP(eng, out, in_, r, w):
        if eng == 'act':
            P.add('act', lambda e: e.copy(out=out, in_=in_), r, w)
        else:
            P.add(eng, lambda e: e.tensor_copy(out=out, in_=in_), r, w)

    def MS(eng, out, val, w):
        P.add(eng, lambda e: e.memset(out, val), (), w)

    def MM(out, lhsT, rhs, start, stop, r, w):
        P.add('pe', lambda e: e.matmul(out, lhsT=lhsT, rhs=rhs, start=start, stop=stop), r, w)

    def TR(out, in_, idn, r, w):
        P.add('pe', lambda e: e.transpose(out=out, in_=in_, identity=idn), r, w)

    def RCP(out, in_, r, w):
        P.add('dve', lambda e: e.reciprocal(out=out, in_=in_), r, w)

    def SWP(eng, out, in_, tab, r, w):
        TT(eng, out[:, 0], in_[:, 1], tab[:, 0], ALU.mult, r, w)
        TT(eng, out[:, 1], in_[:, 0], tab[:, 1], ALU.mult, r, w)

    dq = [0]

    def DQ():
        dq[0] += 1
        return ['sp', 'pool'][dq[0] % 2]

    ident = A.alloc("ident", [128, 128])
    identb = A.alloc("identb", [128, 128], BF16)
    epsc = A.alloc("epsc", [128, 1])
    H0 = A.alloc("H0", [64, 2, 128])
    Hl2 = A.alloc("Hl2", [128, 2, 2, 2, 8, 4])
    PERS0 = A.mark()
    U2 = A.alloc("U2", [128, 64, NCH], BF16)
    PERS = A.mark()

    P.dma('sp', ident[:], ident_d[:, :], w=['ident'])
    CP('pool', identb[:], ident[:], ['ident'], ['identb'])
    MS('pool', epsc[:], EPS, ['epsc'])
    P.barrier()

    def stage_mod():
        cc = A.alloc("cc", [32, 128])
        sT = A.alloc("sT", [128, 32])
        badar = [A.alloc("badar", [2, 128]) for _ in range(3)]
        modrow = [A.alloc("modrow", [2, 128]) for _ in range(3)]
        Wst = [A.alloc("Wst", [128, 16, 128]) for _ in range(3)]
        P.dma('sp', cc[:], cond2[:, :], w=['cc'])
        ACT(cc[:], cc[:], AF.Silu, ['cc'], ['cc'])
        b0 = nextps()
        TR(ps[b0][:, 0:32], cc[:, :], ident[0:32, 0:32], ['cc'], [('ps', b0)])
        CP('dve', sT[:], ps[b0][:, 0:32], [('ps', b0)], ['sT'])
        w_ada_v = w_ada.rearrange("(k p) n -> p k n", p=128)
        st = [0]

        def blocks(n):
            for _ in range(n):
                nb = st[0]
                if nb >= 48:
                    return
                st[0] += 1
                wi = nb % 3
                for kh in range(2):
                    P.dma('sp', Wst[wi][:, kh * 8:(kh + 1) * 8, :],
                          w_ada_v[:, kh * 8:(kh + 1) * 8, nb * 128:(nb + 1) * 128], w=[('Wst', wi, kh)])
                P.dma('sp', badar[wi][:], b_ada[:, nb * 128:(nb + 1) * 128].partition_broadcast(2), w=[('badar', wi)])
                b = nextps()
                for k in range(16):
                    MM(ps[b][0:2, 0:128], sT[:, k::16], Wst[wi][:, k, :], k == 0, k == 15,
                       ['sT', ('Wst', wi, k // 8)], [('ps', b)])
                TT('dve', modrow[wi][0:2, 0:128], ps[b][0:2, 0:128], badar[wi][0:2, 0:128], ALU.add,
                   [('ps', b), ('badar', wi)], [('modrow', wi)])
                P.dma('pool', mod_d[:, nb * 128:(nb + 1) * 128], modrow[wi][0:2, 0:128], r=[('modrow', wi)], w=[('mod_d', nb)])
        return blocks

    mod_blocks = None
    if 'A' in stages:
        mod_blocks = stage_mod()
        mod_blocks(4)

    def setup_ssm():
        ld = A.alloc("ld", [128, 64])
        AreT = A.alloc("AreT", [64, 128])
        AimT = A.alloc("AimT", [64, 128])
        dtb = A.alloc("dtb", [64, 128])
        th = A.alloc("th", [64, 128])
        tn = A.alloc("tn", [64, 128])
        rho = A.alloc("rho", [64, 128])
        irho2 = A.alloc("irho2", [64, 128])
        cs = A.alloc("cs", [64, 128])
        sn = A.alloc("sn", [64, 128])
        Pw = A.alloc("Pw", [64, 2, 9, 128])
        Qw = A.alloc("Qw", [64, 2, 8, 128])
        t1 = A.alloc("t1", [64, 128])
        t2 = A.alloc("t2", [64, 128])
        t3 = A.alloc("t3", [64, 128])
        kap = A.alloc("kap", [64, 2, 128])
        for src, dst, nm in ((are, AreT[:], 'AreT'), (aim, AimT[:], 'AimT'), (h0re, H0[:, 0, :], 'H0'), (h0im, H0[:, 1, :], 'H0')):
            P.dma('sp', ld[:], src[:, :], w=['ld'])
            b = nextps()
            TR(ps[b][0:64, 0:128], ld[:, :], ident[:, :], ['ld'], [('ps', b)])
            CP('dve', dst, ps[b][0:64, 0:128], [('ps', b), nm], [nm])
        P.dma('sp', dtb[:], logdt.partition_broadcast(64), w=['dtb'])
        ACT(dtb[:], dtb[:], AF.Exp, ['dtb'], ['dtb'])
        TS('dve', AreT[:], AreT[:], -1e-4, None, ALU.min, None, ['AreT'], ['AreT'])
        TT('dve', t1[:], AreT[:], dtb[:], ALU.mult, ['AreT', 'dtb'], ['t1'])
        ACT(rho[:], t1[:], AF.Exp, ['t1'], ['rho'])
        ACT(irho2[:], t1[:], AF.Exp, ['t1'], ['irho2'], scale=-2.0)
        TT('dve', th[:], AimT[:], dtb[:], ALU.mult, ['AimT', 'dtb'], ['th'])
        TS('dve', tn[:], th[:], float(1 / (2 * np.pi)), 12582912.0, ALU.mult, ALU.add, ['th'], ['tn'])
        TS('dve', tn[:], tn[:], -12582912.0, float(-2 * np.pi), ALU.add, ALU.mult, ['tn'], ['tn'])
        TT('dve', th[:], th[:], tn[:], ALU.add, ['th', 'tn'], ['th'])
        ACT(sn[:], th[:], AF.Sin, ['th'], ['sn'])
        ACT(t2[:], th[:], AF.Sin, ['th', 't2'], ['t2'], scale=0.5)
        TT('dve', t2[:], t2[:], t2[:], ALU.mult, ['t2'], ['t2'])
        TS('dve', cs[:], t2[:], -2.0, 1.0, ALU.mult, ALU.add, ['t2'], ['cs'])
        MS('pool', Pw[:, 0, 0, :], 1.0, ['Pw'])
        MS('pool', Pw[:, 1, 0, :], 0.0, ['Pw'])
        TT('dve', Pw[:, 0, 1, :], rho[:], cs[:], ALU.mult, ['rho', 'cs', 'Pw'], ['Pw'])
        TT('dve', Pw[:, 1, 1, :], rho[:], sn[:], ALU.mult, ['rho', 'sn', 'Pw'], ['Pw'])

        def cmul(eng, outr, outi, ar, ai, br, bi, rr, ww):
            TT(eng, t1[:], ar, br, ALU.mult, rr + ['t1'], ['t1'])
            TT(eng, t2[:], ai, bi, ALU.mult, rr + ['t2'], ['t2'])
            TT(eng, t3[:], ar, bi, ALU.mult, rr + ['t3'], ['t3'])
            TT(eng, outr, t1[:], t2[:], ALU.subtract, ['t1', 't2'] + ww, ww)
            TT(eng, t1[:], ai, br, ALU.mult, rr + ww + ['t1'], ['t1'])
            TT(eng, outi, t3[:], t1[:], ALU.add, ['t1', 't3'] + ww, ww)

        for k in range(2, 9):
            cmul('dve', Pw[:, 0, k, :], Pw[:, 1, k, :], Pw[:, 0, k - 1, :], Pw[:, 1, k - 1, :],
                 Pw[:, 0, 1, :], Pw[:, 1, 1, :], ['Pw'], ['Pw'])
        MS('pool', Qw[:, 0, 0, :], 1.0, ['Qw'])
        MS('pool', Qw[:, 1, 0, :], 0.0, ['Qw'])
        TT('dve', Qw[:, 0, 1, :], Pw[:, 0, 1, :], irho2[:], ALU.mult, ['Pw', 'irho2', 'Qw'], ['Qw'])
        STT('dve', Qw[:, 1, 1, :], Pw[:, 1, 1, :], -1.0, irho2[:], ALU.mult, ALU.mult, ['Pw', 'irho2', 'Qw'], ['Qw'])
        for k in range(2, 8):
            cmul('dve', Qw[:, 0, k, :], Qw[:, 1, k, :], Qw[:, 0, k - 1, :], Qw[:, 1, k - 1, :],
                 Qw[:, 0, 1, :], Qw[:, 1, 1, :], ['Qw'], ['Qw'])
        Pwr = A.alloc("Pwr", [64, 2, 9, 128])
        for k in range(9):
            CP('dve', Pwr[:, :, k, :], Pw[:, :, 8 - k, :], ['Pw', 'Pwr'], ['Pwr'])
        TAB = A.alloc("TAB", [64, 2, 2, 8, 128])
        CP('dve', TAB[:, 0, 0, 0, :], Pw[:, 0, 8, :], ['Pw'], ['TAB'])
        CP('dve', TAB[:, 1, 1, 0, :], Pw[:, 1, 8, :], ['Pw', 'TAB'], ['TAB'])
        for k in range(1, 8):
            cmul('dve', TAB[:, 0, 0, k, :], TAB[:, 1, 1, k, :], TAB[:, 0, 0, k - 1, :], TAB[:, 1, 1, k - 1, :],
                 Pw[:, 0, 8, :], Pw[:, 1, 8, :], ['TAB', 'Pw'], ['TAB'])
        CP('dve', TAB[:, 0, 1, :, :], TAB[:, 0, 0, :, :], ['TAB'], ['TAB'])
        TS('dve', TAB[:, 1, 0, :, :], TAB[:, 1, 1, :, :], -1.0, None, ALU.mult, None, ['TAB'], ['TAB'])
        P.dma('sp', tab_d[:, :, :, :, :], TAB[:], r=['TAB'], w=['tab_d'])
        nr = A.alloc("nr", [64, 128])
        den = A.alloc("den", [64, 128])
        TS('dve', nr[:], Pw[:, 0, 1, :], -1.0, None, ALU.add, None, ['Pw'], ['nr'])
        TT('dve', t1[:], AreT[:], AreT[:], ALU.mult, ['AreT', 't1'], ['t1'])
        TT('dve', t2[:], AimT[:], AimT[:], ALU.mult, ['AimT', 't2'], ['t2'])
        TT('dve', den[:], t1[:], t2[:], ALU.add, ['t1', 't2'], ['den'])
        RCP(den[:], den[:], ['den'], ['den'])
        TT('dve', t1[:], nr[:], AreT[:], ALU.mult, ['nr', 'AreT', 't1'], ['t1'])
        TT('dve', t2[:], Pw[:, 1, 1, :], AimT[:], ALU.mult, ['Pw', 'AimT', 't2'], ['t2'])
        TT('dve', t1[:], t1[:], t2[:], ALU.add, ['t1', 't2'], ['t1'])
        TT('dve', kap[:, 0, :], t1[:], den[:], ALU.mult, ['t1', 'den'], ['kap'])
        TT('dve', t1[:], Pw[:, 1, 1, :], AreT[:], ALU.mult, ['Pw', 'AreT', 't1'], ['t1'])
        TT('dve', t2[:], nr[:], AimT[:], ALU.mult, ['nr', 'AimT', 't2'], ['t2'])
        TT('dve', t1[:], t1[:], t2[:], ALU.subtract, ['t1', 't2'], ['t1'])
        TT('dve', kap[:, 1, :], t1[:], den[:], ALU.mult, ['t1', 'den', 'kap'], ['kap'])

        maskf = A.alloc("maskf", [128, 128])
        maskb = A.alloc("maskb", [128, 128])
        dcs = A.alloc("dcs", [128, 64])
        P.dma('sp', maskf[:], maskf_d[:, :], w=['maskf'])
        P.dma('sp', maskb[:], maskb_d[:, :], w=['maskb'])
        P.dma('sp', dcs[:], dcol[:, :], w=['dcs'])
        Bl = A.alloc("Bl", [64, 2, 16, 16])
        Bb = A.alloc("Bb", [64, 2, 16, 16])
        Cl0 = A.alloc("Cl0", [128, 2, 16, 64])
        CT = A.alloc("CT", [64, 2, 16, 16])
        Bs = A.alloc("Bs", [64, 2, 16, 8, 16])
        Cr = A.alloc("Cr", [64, 2, 16, 8, 16])
        XW = A.alloc("XW", [64, 2, 16, 8, 16], BF16)
        CLb = A.alloc("CLb", [64, 16, 2, 128], BF16)
        CLv = CLb[:].rearrange("p a r (s i) -> p a r s i", i=16)
        XMb = A.alloc("XMb", [128, 16, 2, 64], BF16)
        MGb = A.alloc("MGb", [128, 8, 128], BF16)
        tm1 = A.alloc("tm1", [128, 128])
        tm2 = A.alloc("tm2", [128, 128])
        u1 = A.alloc("u1", [64, 8, 8, 16])
        u2 = A.alloc("u2", [64, 8, 8, 16])
        u3 = A.alloc("u3", [64, 8, 8, 16])
        u4 = A.alloc("u4", [64, 8, 8, 16])
        P.dma('sp', Cl0[:, 0, :, :], cre[:, :, :], w=['Cl0'])
        P.dma('pool', Cl0[:, 1, :, :], cim[:, :, :], w=['Cl0b'])
        bre_v = bre.rearrange("a p j -> p a j")
        bim_v = bim.rearrange("a p j -> p a j")
        for gb in range(8):
            if mod_blocks is not None:
                mod_blocks(6)
            for d in range(2):
                c0 = d * 64 + gb * 8
                P.dma('sp', Bl[:, 0, d * 8:(d + 1) * 8, :], bre_v[:, c0:c0 + 8, :], w=['Bl'])
                P.dma('pool', Bl[:, 1, d * 8:(d + 1) * 8, :], bim_v[:, c0:c0 + 8, :], w=['Bl'])

            def kb(ri, d):
                return kap[:, ri, d * 64 + gb * 8:d * 64 + gb * 8 + 8][:, :, None].broadcast_to([64, 8, 16])

            ub1 = u1[:, :, 0, :]
            ub2 = u2[:, :, 0, :]
            for d in range(2):
                sl = slice(d * 8, (d + 1) * 8)
                TT('dve', ub1, Bl[:, 0, sl, :], kb(0, d), ALU.mult, ['Bl', 'kap', 'u1'], ['u1'])
                TT('dve', ub2, Bl[:, 1, sl, :], kb(1, d), ALU.mult, ['Bl', 'kap', 'u2'], ['u2'])
                TT('dve', Bb[:, 0, sl, :], ub1, ub2, ALU.subtract, ['u1', 'u2', 'Bb'], ['Bb'])
                TT('dve', ub1, Bl[:, 1, sl, :], kb(0, d), ALU.mult, ['Bl', 'kap', 'u1', 'Bb'], ['u1'])
                TT('dve', ub2, Bl[:, 0, sl, :], kb(1, d), ALU.mult, ['Bl', 'kap', 'u2', 'Bb'], ['u2'])
                TT('dve', Bb[:, 1, sl, :], ub1, ub2, ALU.add, ['u1', 'u2', 'Bb'], ['Bb'])
            for ri in range(2):
                for i4 in range(4):
                    b = nextps()
                    for ii in range(4):
                        i = i4 * 4 + ii
                        TR(ps[b][0:64, ii * 128:(ii + 1) * 128], Cl0[:, ri, i, :], ident[:, :],
                           ['Cl0', 'Cl0b'], [('ps', b)])
                    for d in range(2):
                        src = ps[b][0:64, :].rearrange("p (i c) -> p c i", i=4)[:, d * 64 + gb * 8:d * 64 + gb * 8 + 8, :]
                        CP('act', CT[:, ri, d * 8:(d + 1) * 8, i4 * 4:(i4 + 1) * 4], src, [('ps', b), 'CT'], ['CT'])

            def tabv(t, ri, ks, d):
                v = t[:, ri, ks, d * 64 + gb * 8:d * 64 + gb * 8 + 8].rearrange("p k g -> p g k")
                return v[:, :, :, None].broadcast_to([64, 8, 8, 16])

            def bcs(a):
                return a[:, :, None, :].broadcast_to([64, 8, 8, 16])

            def cm4(eng, outr, outi, ar, ai, tab, ks, d, neg_im, rr, ww):
                TT(eng, u1[:], bcs(ar), tabv(tab, 0, ks, d), ALU.mult, rr + ['u1'], ['u1'])
                TT(eng, u2[:], bcs(ai), tabv(tab, 1, ks, d), ALU.mult, rr + ['u2'], ['u2'])
                TT(eng, u3[:], bcs(ar), tabv(tab, 1, ks, d), ALU.mult, rr + ['u3'], ['u3'])
                TT(eng, u4[:], bcs(ai), tabv(tab, 0, ks, d), ALU.mult, rr + ['u4'], ['u4'])
                TT(eng, outr, u1[:], u2[:], ALU.subtract, ['u1', 'u2'] + ww, ww)
                if neg_im:
                    STT(eng, outi, u3[:], -1.0, u4[:], ALU.mult, ALU.subtract, ['u3', 'u4'] + ww, ww)
                else:
                    TT(eng, outi, u3[:], u4[:], ALU.add, ['u3', 'u4'] + ww, ww)

            asc = slice(0, 8)
            desc7 = slice(7, None, -1)
            for d in range(2):
                sl = slice(d * 8, (d + 1) * 8)
                eng = 'dve'
                eng2 = 'dve'
                cm4(eng, Bs[:, 0, sl, :, :], Bs[:, 1, sl, :, :], Bb[:, 0, sl, :], Bb[:, 1, sl, :],
                    Qw if d == 0 else Pw, asc, d, False, ['Bb', 'Pw', 'Qw'], [('Bs', d)])
                cm4(eng, Cr[:, 0, sl, :, :], Cr[:, 1, sl, :, :], CT[:, 0, sl, :], CT[:, 1, sl, :],
                    Pw if d == 0 else Qw, asc, d, True, ['CT', 'Pw', 'Qw'], [('Cr', d)])
                cm4(eng2, XW[:, 0, sl, :, :], XW[:, 1, sl, :, :], Bb[:, 0, sl, :], Bb[:, 1, sl, :],
                    Pwr if d == 0 else Pw, slice(1, 9) if d == 0 else asc, d, False, ['Bb', 'Pw', 'Pwr'], [('XW', d)])
                cm4(eng2, CLv[:, sl, 0, :, :], CLv[:, sl, 1, :, :], CT[:, 0, sl, :], CT[:, 1, sl, :],
                    Pw if d == 0 else Pwr, slice(1, 9) if d == 0 else asc, d, True, ['CT', 'Pw', 'Pwr'], [('CLb', d)])
            for d in range(2):
                P.dma('sp', cl_d[:, gb * 8:(gb + 1) * 8, d, :, :], CLb[:, d * 8:(d + 1) * 8, :, :], r=[('CLb', d)], w=[('cl_d', gb, d)])
            for dg8 in range(2):
                for ri in range(2):
                    b = nextps()
                    pb = ps[b][:, :].bitcast(BF16)
                    for q in range(8):
                        dg = dg8 * 8 + q
                        TR(pb[:, q * 64:(q + 1) * 64], XW[:, ri, dg, :, :].rearrange("p s j -> p (s j)"),
                           identb[0:64, 0:64], [('XW', 0), ('XW', 1)], [('ps', b)])
                    CP('act', XMb[:, dg8 * 8:(dg8 + 1) * 8, ri, :],
                       pb[:, 0:512].rearrange("p (q c) -> p q c", q=8), [('ps', b), 'XMb'], ['XMb'])
            for d in range(2):
                P.dma('pool', xmat_d[:, gb * 8:(gb + 1) * 8, d, :, :], XMb[:, d * 8:(d + 1) * 8, :, :], r=['XMb'], w=[('xmat_d', gb, d)])
            for g8 in range(8):
                g = gb * 8 + g8
                bf = nextps()
                for d in range(2):
                    o = ps[bf][:, d * 128:(d + 1) * 128]
                    dg = d * 8 + g8
                    MM(o, Bs[:, 0, dg, :, :].rearrange("p s j -> p (s j)"), Cr[:, 0, dg, :, :].rearrange("p s j -> p (s j)"),
                       True, False, [('Bs', d), ('Cr', d)], [('ps', bf)])
                    MM(o, Bs[:, 1, dg, :, :].rearrange("p s j -> p (s j)"), Cr[:, 1, dg, :, :].rearrange("p s j -> p (s j)"),
                       False, True, [('Bs', d), ('Cr', d)], [('ps', bf)])
                TT('dve', tm1[:], ps[bf][:, 0:128], maskf[:], ALU.mult, [('ps', bf), 'maskf', 'tm1'], ['tm1'])
                TT('dve', tm2[:], ps[bf][:, 128:256], maskb[:], ALU.mult, [('ps', bf), 'maskb', 'tm2'], ['tm2'])
                TT('dve', tm1[:], tm1[:], tm2[:], ALU.add, ['tm1', 'tm2'], ['tm1'])
                STT('dve', MGb[:, g8, :], ident[:], dcs[:, g:g + 1], tm1[:], ALU.mult, ALU.add,
                    ['ident', 'dcs', 'tm1', 'MGb'], ['MGb'])
            P.dma('sp', mg_d[:, gb * 8:(gb + 1) * 8, :], MGb[:], r=['MGb'], w=[('mg_d', gb)])

    if 'C' in stages:
        setup_ssm()
        if mod_blocks is not None:
            mod_blocks(48)
        P.barrier()
    A.reset(PERS0)

    def setup_attn():
        rT = A.alloc("rT", [31, 240])
        oh = A.alloc("oh", [31, 64, 64])
        cmk = A.alloc("cmk", [64, 64])
        T0 = A.alloc("T0", [64, 240, 64])
        zt = A.alloc("zt", [64, 16, 64])
        P.dma('sp', rT[:], rpbT[:, :], w=['rT'])
        P.dma('pool', oh[:], onehot_d[:, :, :], w=['oh'])
        P.dma('sp', cmk[:], colmask_d[:, :], w=['cmk'])
        MS('pool', zt[:], 0.0, ['zt'])
        for qc in range(64):
            b = nextps()
            MM(ps[b][0:64, 0:240], oh[:, qc, :], rT[:, :], True, True, ['oh', 'rT'], [('ps', b)])
            CP(['act', 'dve'][qc % 2], T0[:, :, qc], ps[b][0:64, 0:240], [('ps', b), 'T0'], ['T0'])
        TT('dve', T0[:], T0[:], cmk[:, None, :].broadcast_to([64, 240, 64]), ALU.add, ['T0', 'cmk'], ['T0'])
        tbv = tb_d.rearrange("h p a q -> p h a q")
        T0v = T0[:].rearrange("p (h a) q -> p h a q", h=16)
        for h in range(16):
            P.dma('pool', tbv[0:64, h, 1:16, :], T0v[:, h, :, :], r=['T0'], w=[('tb_d', h, 0)])
            P.dma('pool', tbv[64:128, h, 0:15, :], T0v[:, h, :, :], r=['T0'], w=[('tb_d', h, 1)])
        P.dma('sp', tbv[0:64, :, 0, :], zt[:], r=['zt'], w=[('tb_d', 'z0')])
        P.dma('sp', tbv[0:64, :, 16, :], zt[:], r=['zt'], w=[('tb_d', 'z1')])
        P.dma('pool', tbv[64:128, :, 15, :], zt[:], r=['zt'], w=[('tb_d', 'z2')])
        P.dma('pool', tbv[64:128, :, 16, :], zt[:], r=['zt'], w=[('tb_d', 'z3')])
        kcl = [A.alloc("kcl", [128, 1024]) for _ in range(2)]
        kcb = A.alloc("kcb", [128, 1024], BF16)
        kct = A.alloc("kct", [128, 8, 128], BF16)
        for t in range(4):
            P.dma('sp', kcl[t % 2][:], ck[t * 128:(t + 1) * 128, :], w=[('kcl', t % 2)])
            CP('dve', kcb[:], kcl[t % 2][:], [('kcl', t % 2), 'kcb'], ['kcb'])
            for hh in range(2):
                b = nextps()
                pb = ps[b][:, :].bitcast(BF16)
                for q in range(4):
                    TR(pb[:, q * 128:(q + 1) * 128], kcb[:, (hh * 4 + q) * 128:(hh * 4 + q + 1) * 128], identb[:, :],
                       ['kcb', 'identb'], [('ps', b)])
                CP('act', kct[:, hh * 4:(hh + 1) * 4, :], pb[:, 0:512].rearrange("p (q c) -> p q c", q=4),
                   [('ps', b), 'kct'], ['kct'])
            P.dma('sp', kcT_d.rearrange("(a p) n -> p a n", p=128)[:, :, t * 128:(t + 1) * 128], kct[:], r=['kct'],
                  w=[('kcT_d', t)])

    if 'T' in stages:
        setup_attn()
        P.barrier()
    A.reset(PERS)

    NT2 = 1536
    NB = 1024

    def front(hT, tiles):
        normw_bc = A.alloc("normw_bc", [128, D])
        mbc = [A.alloc("mbc", [128, D]) for _ in range(2)]
        shbc = [A.alloc("shbc", [128, D]) for _ in range(2)]
        xt = [A.alloc("xt", [128, D]) for _ in range(2)]
        junk = A.alloc("junk", [128, D])
        xm = A.alloc("xm", [128, D])
        xmb = [A.alloc("xmb", [128, D], BF16) for _ in range(2)]
        nt = len(tiles)
        ss = A.alloc("ss", [128, nt])
        rs = A.alloc("rs", [128, nt])
        P.dma('sp', normw_bc[:], norm_w.partition_broadcast(128), w=['normw_bc'])
        MS('pool', ss[:], 0.0, [('ss', t) for t in range(nt)])
        for v in range(2):
            P.dma('sp', shbc[v][:], mod_d[v:v + 1, 0:D].partition_broadcast(128), w=[('shbc', v)])
            P.dma('pool', mbc[v][:], mod_d[v:v + 1, D:2 * D].partition_broadcast(128), w=[('mbc', v)])
            STT('dve', mbc[v][:], mbc[v][:], 1.0, normw_bc[:], ALU.add, ALU.mult, [('mbc', v), 'normw_bc'], [('mbc', v)])
        for t, (src, v, c0) in enumerate(tiles):
            xi = t % 2
            P.dma('sp', xt[xi][:], src, w=[('xt', xi)])
            ACT(junk[:], xt[xi][:], AF.Square, [('xt', xi), 'junk'], ['junk', ('ss', t)], accum_out=ss[:, t:t + 1])
            ACT(rs[:, t:t + 1], ss[:, t:t + 1], AF.Sqrt, [('ss', t), 'epsc'], [('rs', t)], bias=epsc[:, 0:1], scale=1.0 / D)
            RCP(rs[:, t:t + 1], rs[:, t:t + 1], [('rs', t)], [('rs', t)])
            STT('dve', xm[:], xt[xi][:], rs[:, t:t + 1], mbc[v][:], ALU.mult, ALU.mult,
                [('xt', xi), ('rs', t), ('mbc', v), 'xm'], ['xm'])
            xb_ = xmb[t % 2]
            nxb = ('xmb', t % 2)
            TT('dve', xb_[:], xm[:], shbc[v][:], ALU.add, ['xm', ('shbc', v), nxb], [nxb])
            for k4 in range(4):
                b = nextps()
                pb = ps[b][:, :].bitcast(BF16)
                for kk in range(4):
                    k = k4 * 4 + kk
                    TR(pb[:, kk * 128:(kk + 1) * 128], xb_[:, k * 128:(k + 1) * 128], identb[:, :], [nxb], [('ps', b)])
                CP('act', hT[:, k4 * 4:(k4 + 1) * 4, c0:c0 + 128], pb[:, 0:512].rearrange("p (a b) -> p a b", a=4),
                   [('ps', b)], [('hT', t, k4)])

    w_in_v = w_in.rearrange("(k p) n -> p k n", p=128)
    wc = [0]

    def mk_loader(WD=512):
        Wst = [A.alloc("Wst", [128, 16, WD]) for _ in range(2)]
        Wbf = [A.alloc("Wbf", [128, 16, WD], BF16) for _ in range(2)]

        def load_w(c0):
            i = wc[0] % 2
            wc[0] += 1
            for kh in range(2):
                P.dma('sp', Wst[i][:, kh * 8:(kh + 1) * 8, :], w_in_v[:, kh * 8:(kh + 1) * 8, c0:c0 + WD],
                      w=[('Wst', i, kh)])
                CP(['dve', 'act'][kh], Wbf[i][:, kh * 8:(kh + 1) * 8, :], Wst[i][:, kh * 8:(kh + 1) * 8, :],
                   [('Wst', i, kh)], [('Wbf', i, kh)])
            return i
        return Wbf, load_w

    def u_proj(hT, grps, Wbf, load_w, WD=512):
        Ubuf = A.alloc("Ubuf", [128, 64, 8, 16], BF16)
        for (t0, ncnk, cofs) in grps:
            ng = WD // 16
            for cb in range(1024 // WD):
                wi = load_w(cb * WD)
                for s_ in range(8):
                    b = nextps()
                    for k in range(16):
                        MM(ps[b][0:ncnk, 0:WD], hT[:, k, t0 + s_:t0 + 8 * ncnk:8], Wbf[wi][:, k, :], k == 0, k == 15,
                           [('Wbf', wi, k // 8)], [('ps', b)])
                    CP(['act', 'dve'][s_ % 2], Ubuf[0:ncnk, cb * ng:(cb + 1) * ng, s_, :],
                       ps[b][0:ncnk, 0:WD].rearrange("p (g j) -> p g j", g=ng), [('ps', b), 'Ubuf'], ['Ubuf'])
            for g4 in range(16):
                b = nextps()
                pb = ps[b][:, :].bitcast(BF16)
                for q in range(4):
                    g = g4 * 4 + q
                    TR(pb[:, q * 128:q * 128 + ncnk], Ubuf[0:ncnk, g, :, :].rearrange("p s j -> p (s j)"),
                       identb[0:ncnk, 0:ncnk], ['Ubuf'], [('ps', b)])
                CP(['act', 'dve'][g4 % 2], U2[:, g4 * 4:(g4 + 1) * 4, cofs:cofs + ncnk],
                   pb[:, 0:512].rearrange("p (q c) -> p q c", q=4)[:, :, 0:ncnk], [('ps', b), 'U2'], ['U2'])

    def projections(hT, Wbf, load_w):
        qnw_bc = A.alloc("qnw_bc", [128, 64])
        knw_bc = A.alloc("knw_bc", [128, 64])
        P.dma('sp', qnw_bc[:], qnw.partition_broadcast(128), w=['qnw_bc'])
        P.dma('sp', knw_bc[:], knw.partition_broadcast(128), w=['knw_bc'])
        tok = [A.alloc("tok", [128, 512]) for _ in range(2)]
        tokb = [A.alloc("tokb", [128, 512], BF16) for _ in range(2)]
        tokT = [A.alloc("tokT", [128, 4, 128], BF16) for _ in range(2)]
        sq = A.alloc("sq", [128, 512])
        ms = A.alloc("ms", [128, 8])
        tc_ = [0]
        for cb in range(6):
            c0 = 2048 + cb * 512
            kind = cb // 2
            o0 = (cb % 2) * 512
            wi = load_w(c0)
            tl = [0, 1, 2, 3, 8, 9, 10, 11] if kind == 0 else list(range(12))
            for t in tl:
                b = nextps()
                ti = tc_[0] % 2
                tc_[0] += 1
                for k in range(16):
                    MM(ps[b][:, :], hT[:, k, t * 128:(t + 1) * 128], Wbf[wi][:, k, :], k == 0, k == 15,
                       [('Wbf', wi, k // 8)], [('ps', b)])
                CP('act', tok[ti][:], ps[b][:, :], [('ps', b), ('tok', ti)], [('tok', ti)])
                if kind == 2:
                    P.dma('pool', v_d[t * 128:(t + 1) * 128, o0:o0 + 512], tok[ti][:], r=[('tok', ti)], w=[('v_d', t, cb)])
                    if t >= 8:
                        P.dma('pool', vo[(t - 8) * 128:(t - 7) * 128, o0:o0 + 512], tok[ti][:], r=[('tok', ti)],
                              w=[('vo', t, cb)])
                    continue
                nw = qnw_bc if kind == 0 else knw_bc
                t3 = tok[ti][:].rearrange("p (h d) -> p h d", h=8)
                TT('dve', sq[:], tok[ti][:], tok[ti][:], ALU.mult, [('tok', ti), 'sq'], ['sq'])
                P.add('dve', lambda e: e.tensor_reduce(out=ms[:], in_=sq[:].rearrange("p (h d) -> p h d", h=8),
                                                       axis=AX.X, op=ALU.add), ['sq', 'ms'], ['ms'])
                ACT(ms[:], ms[:], AF.Sqrt, ['ms', 'epsc'], ['ms'], bias=epsc[:, 0:1], scale=1.0 / 64)
                RCP(ms[:], ms[:], ['ms'], ['ms'])
                TT('dve', t3, t3, ms[:, :, None].broadcast_to([128, 8, 64]), ALU.mult, [('tok', ti), 'ms'], [('tok', ti)])
                TT('dve', t3, t3, nw[:, None, :].broadcast_to([128, 8, 64]), ALU.mult, [('tok', ti), 'qnw_bc', 'knw_bc'],
                   [('tok', ti)])
                if kind == 1 and t >= 8:
                    P.dma('pool', ko[(t - 8) * 128:(t - 7) * 128, o0:o0 + 512], tok[ti][:], r=[('tok', ti)], w=[('ko', t, cb)])
                CP('act', tokb[ti][:], tok[ti][:], [('tok', ti), ('tokb', ti)], [('tokb', ti)])
                b2 = nextps()
                pb = ps[b2][:, :].bitcast(BF16)
                for q in range(4):
                    TR(pb[:, q * 128:(q + 1) * 128], tokb[ti][:, q * 128:(q + 1) * 128], identb[:, :], [('tokb', ti)],
                       [('ps', b2)])
                CP('dve', tokT[ti][:], pb[:, 0:512].rearrange("p (q c) -> p q c", q=4), [('ps', b2), ('tokT', ti)], [('tokT', ti)])
                if kind == 0:
                    tq = t if t < 4 else t - 4
                    dsl = qT_d[o0:o0 + 512, tq * 128:(tq + 1) * 128]
                else:
                    dsl = kT_d[o0:o0 + 512, t * 128:(t + 1) * 128]
                P.dma('pool', dsl.rearrange("(q p) n -> p q n", p=128), tokT[ti][:],
                      r=[('tokT', ti)], w=[('qkT', kind, t, cb)])
        fm = [A.alloc("fm", [128, 512]) for _ in range(2)]
        fc_ = [0]
        specs = [(1024, 2, zs_d, AF.Silu), (5120, 2, za_d, AF.Silu), (6144, 4, gs_d, AF.Sigmoid), (8192, 4, ga_d, AF.Sigmoid)]
        for (cbase, nblk, dst, fn) in specs:
            for cb in range(nblk):
                wi = load_w(cbase + cb * 512)
                for f2 in range(4):
                    for tb, hoff in enumerate((0, 1024)):
                        b = nextps()
                        fi = fc_[0] % 2
                        fc_[0] += 1
                        for k in range(16):
                            MM(ps[b][:, :], Wbf[wi][:, k, f2 * 128:(f2 + 1) * 128], hT[:, k, hoff:hoff + 512],
                               k == 0, k == 15, [('Wbf', wi, k // 8)], [('ps', b)])
                        ACT(fm[fi][:], ps[b][:, :], fn, [('ps', b), ('fm', fi)], [('fm', fi)])
                        r0 = cb * 512 + f2 * 128
                        P.dma('pool', dst[r0:r0 + 128, tb * 512:(tb + 1) * 512], fm[fi][:], r=[('fm', fi)],
                              w=[('fmd', r0, tb, cbase)])

    if 'B' in stages:
        hT1 = A.alloc("hT1", [128, 16, NS], BF16)
        m1_ = A.mark()
        front(hT1, [(xs[t * 128:(t + 1) * 128, :], 1, t * 128) for t in range(16)])
        P.barrier()
        A.reset(m1_)
        Wbf, load_w = mk_loader(256)
        u_proj(hT1, [(0, 128, 0), (1024, 128, 128)], Wbf, load_w, 256)
        P.barrier()
        A.reset(PERS)
        hT2 = A.alloc("hT2", [128, 16, NT2], BF16)
        m2_ = A.mark()
        tiles = [(xown[t * 128:(t + 1) * 128, :], 1, t * 128) for t in range(4)]
        tiles += [(xhalo[t * 128:(t + 1) * 128, :], 1, 512 + t * 128) for t in range(4)]
        tiles += [(xp[t * 128:(t + 1) * 128, :], 0, 1024 + t * 128) for t in range(4)]
        front(hT2, tiles)
        P.barrier()
        A.reset(m2_)
        Wbf, load_w = mk_loader()
        m3_ = A.mark()
        u_proj(hT2, [(1024, 64, 256)], Wbf, load_w)
        P.barrier()
        A.reset(m3_)
        projections(hT2, Wbf, load_w)
        P.barrier()
    A.reset(PERS)

    def ssm_main():
        Xb2 = [A.alloc("Xb", [128, 2, 8, NCH]) for _ in range(2)]
        HP2 = [A.alloc("HP", [128, 2, 8, NCH], BF16) for _ in range(2)]
        xmp = A.alloc("xmp", [128, 8, 2, 2, 128], BF16)
        clb2 = [A.alloc("clb", [128, 4, 2, 2, 128], BF16) for _ in range(2)]
        mgb = [A.alloc("mgb", [128, 8, 128], BF16) for _ in range(2)]
        TABs = A.alloc("TABs", [64, 2, 2, 8, 128])
        P.dma('sp', TABs[:], tab_d[:, :, :, :, :], w=['TABs'])
        MS('pool', xmp[:], 0.0, ['xmp'])
        TABb = A.alloc("TABb", [128, 2, 2, 8, 2, 4])
        TAB2b = A.alloc("TAB2b", [128, 2, 2, 5, 2, 4])
        h0b = A.alloc("h0b", [128, 2, 2, 4])
        W1 = [A.alloc("W1", [128, 2, 4, 40]) for _ in range(2)]
        W2 = [A.alloc("W2", [128, 2, 4, 40]) for _ in range(2)]
        HS = [A.alloc("HS", [128, 2, 4, 40]) for _ in range(2)]
        s1p = [A.alloc("s1p", [128, 2, 4, 2, 4]) for _ in range(2)]
        s2p = [A.alloc("s2p", [128, 2, 4, 2, 4]) for _ in range(2)]
        TAB2 = A.alloc("TAB2", [64, 2, 2, 5, 128])
        q1 = A.alloc("q1", [64, 128])
        q2 = A.alloc("q2", [64, 128])
        CP('dve', TAB2[:, 0, 0, 0, :], TABs[:, 0, 0, 7, :], ['TABs'], ['TAB2'])
        CP('dve', TAB2[:, 1, 1, 0, :], TABs[:, 1, 1, 7, :], ['TABs', 'TAB2'], ['TAB2'])
        for j in range(1, 5):
            pr_, pi_ = TAB2[:, 0, 0, j - 1, :], TAB2[:, 1, 1, j - 1, :]
            TT('dve', q1[:], pr_, pr_, ALU.mult, ['TAB2', 'q1'], ['q1'])
            TT('dve', q2[:], pi_, pi_, ALU.mult, ['TAB2', 'q2'], ['q2'])
            TT('dve', TAB2[:, 0, 0, j, :], q1[:], q2[:], ALU.subtract, ['q1', 'q2', 'TAB2'], ['TAB2'])
            TT('dve', q1[:], pr_, pi_, ALU.mult, ['TAB2', 'q1'], ['q1'])
            TS('dve', TAB2[:, 1, 1, j, :], q1[:], 2.0, None, ALU.mult, None, ['q1', 'TAB2'], ['TAB2'])
        CP('dve', TAB2[:, 0, 1, :, :], TAB2[:, 0, 0, :, :], ['TAB2'], ['TAB2'])
        TS('dve', TAB2[:, 1, 0, :, :], TAB2[:, 1, 1, :, :], -1.0, None, ALU.mult, None, ['TAB2'], ['TAB2'])
        Ysb = [A.alloc("Ysb", [128, NCH]) for _ in range(2)]
        Ytok = A.alloc("Ytok", [128, 3, 8, 128])
        g1 = A.alloc("g1", [128, 1024])
        Y2 = A.alloc("Y2", [128, 1024])
        selT = A.alloc("selT", [128, 3, 128])
        P.dma('sp', selT[:], sel3[:, :, :], w=['selT'])
        ysT = [A.alloc("ysT", [128, 128, 8]) for _ in range(2)]
        pieces = [(0, 128), (128, 128), (256, 64)]
        yc = [0]
        MS('pool', Ytok[:], 0.0, ['Ytok'])
        TABsv = TABs[:].rearrange("p a r k (d g) -> p a r k d g", d=2)
        TAB2v = TAB2[:].rearrange("p a r j (d g) -> p a r j d g", d=2)
        H0v = H0[:].rearrange("p r (d g) -> p r d g", d=2)
        rngs = [(0, 256, slice(255, None, -1)), (256, 288, slice(287, 255, -1)), (288, 320, slice(319, 287, -1))]
        def p_load(gb):
            bi = gb % 2
            XB, HPb = Xb2[bi], HP2[bi]
            nX, nH = ('Xb', bi), ('HP', bi)
            XB4 = XB[:].rearrange("p r (d g) c -> p r d g c", d=2)
            XB5 = XB[:].rearrange("p r (d g) (k s) -> p r d g s k", d=2, k=8)
            nXall = [nX, ('Xbd', bi, 0), ('Xbd', bi, 1)]
            for gh in range(2):
                g0 = gb * 8 + gh * 4
                ps_ = slice(gh * 64, (gh + 1) * 64)
                P.dma('sp', xmp[:, gh * 4:(gh + 1) * 4, :, :, gh * 64:(gh + 1) * 64], xmat_d[:, g0:g0 + 4, :, :, :], w=['xmp'])
                P.dma('sp', clb2[bi][ps_, :, :, :, :], cl_d[:, g0:g0 + 4, :, :, :], w=[('clb', bi)])
            P.dma('sp', mgb[bi][:], mg_d[:, gb * 8:(gb + 1) * 8, :], w=[('mgb', bi)])

        def p_xc(gb):
            bi = gb % 2
            XB, HPb = Xb2[bi], HP2[bi]
            nX, nH = ('Xb', bi), ('HP', bi)
            XB4 = XB[:].rearrange("p r (d g) c -> p r d g c", d=2)
            XB5 = XB[:].rearrange("p r (d g) (k s) -> p r d g s k", d=2, k=8)
            nXall = [nX, ('Xbd', bi, 0), ('Xbd', bi, 1)]
            for g4 in range(4):
                for d in range(2):
                    for ri in range(2):
                        b = nextps()
                        for (c0_, c1_, rv) in ([(0, NCH, slice(0, NCH))] if d == 0 else rngs):
                            for gh in range(2):
                                g = gb * 8 + gh * 4 + g4
                                lhs = xmp[:, gh * 4 + g4, d, ri, :]
                                MM(ps[b][:, c0_:c1_], lhs, U2[:, g, rv], gh == 0, gh == 1, ['xmp'], [('ps', b)])
                        CP('act', XB[:, ri, d * 4 + g4, :].rearrange("p (k s) -> p s k", k=8),
                           ps[b][:, 0:NCH].rearrange("p (s k) -> p s k", k=8), [('ps', b), nX, ('Xbd', bi, 0), ('Xbd', bi, 1)], [nX])

        def p_rec(gb):
            bi = gb % 2
            XB, HPb = Xb2[bi], HP2[bi]
            nX, nH = ('Xb', bi), ('HP', bi)
            XB4 = XB[:].rearrange("p r (d g) c -> p r d g c", d=2)
            XB5 = XB[:].rearrange("p r (d g) (k s) -> p r d g s k", d=2, k=8)
            nXall = [nX, ('Xbd', bi, 0), ('Xbd', bi, 1)]
            for gh in range(2):
                g0 = gb * 8 + gh * 4
                ps_ = slice(gh * 64, (gh + 1) * 64)
                for a in range(2):
                    for ri in range(2):
                        CP('dve', TABb[ps_, a, ri], TABsv[:, a, ri, :, :, g0:g0 + 4], ['TABs', 'TABb'], ['TABb'])
                        CP('dve', TAB2b[ps_, a, ri], TAB2v[:, a, ri, :, :, g0:g0 + 4], ['TAB2', 'TAB2b'], ['TAB2b'])
                CP('dve', h0b[ps_], H0v[:, :, :, g0:g0 + 4], ['h0b'], ['h0b'])
            eng = 'dve'

            def ctx(d):
                return (('Xbd', bi, d), W1[d], W2[d], HS[d], ('W1', d), ('W2', d), ('HS', d), ('S1', d), ('S2', d))

            def ak(a, k, n, d):
                return TABb[:, a, :, k, d, :][:, :, :, None].broadcast_to([128, 2, 4, n])

            def aj(a, shape, j, d):
                v = TAB2b[:, a, :, j, d, :]
                for _ in range(len(shape) - 3):
                    v = v.unsqueeze(len(v.shape))
                return v.broadcast_to(shape)
            for k in range(1, 8):
                for d in range(2):
                    nXd, w1, w2, hs, nW1, nW2, nHs, nS1, nS2 = ctx(d)
                    prev = XB5[:, :, d, :, :, k - 1]
                    TT(eng, w1[:], prev, ak(0, 0, 40, d), ALU.mult, [nX, nXd, nW1, 'TABb'], [nW1])
                    SWP(eng, w2[:], prev, ak(1, 0, 40, d), [nX, nXd, nW2, 'TABb'], [nW2])
                for d in range(2):
                    nXd, w1, w2, hs, nW1, nW2, nHs, nS1, nS2 = ctx(d)
                    cur = XB5[:, :, d, :, :, k]
                    TT(eng, cur, cur, w1[:], ALU.add, [nX, nXd, nW1], [nXd])
                    TT(eng, cur, cur, w2[:], ALU.add, [nX, nXd, nW2], [nXd])
            for d in range(2):
                nXd, w1, w2, hs, nW1, nW2, nHs, nS1, nS2 = ctx(d)
                P.add(eng, lambda e, hs=hs: e.memset(hs[:, :, :, 32:40:4], 0.0), [nHs], [nHs])
                CP(eng, hs[:, :, :, 0], h0b[:, :, d, :], [nHs, 'h0b'], [nHs])
                CP(eng, hs[:, :, :, 1:32], XB5[:, :, d, :, 0:31, 7], [nX, nXd, nHs], [nHs])
                hp = hs[:, :, :, 32:40].rearrange("p r g (s k) -> p r g s k", k=4)
                xpv = XB5[:, :, d, :, 32:40, 7].rearrange("p r g (s k) -> p r g s k", k=4)
                for ri in range(2):
                    CP(eng, hp[:, ri, :, :, 1:4], xpv[:, ri, :, :, 0:3], [nX, nXd, nHs], [nHs])
            for j in range(5):
                o = 1 << j
                n = 32 - o
                for d in range(2):
                    nXd, w1, w2, hs, nW1, nW2, nHs, nS1, nS2 = ctx(d)
                    src = hs[:, :, :, 0:n]
                    TT(eng, w1[:, :, :, 0:n], src, aj(0, [128, 2, 4, n], j, d), ALU.mult, [nHs, nW1, 'TAB2b'], [nW1])
                    SWP(eng, w2[:, :, :, 0:n], src, aj(1, [128, 2, 4, n], j, d), [nHs, nW2, 'TAB2b'], [nW2])
                for d in range(2):
                    nXd, w1, w2, hs, nW1, nW2, nHs, nS1, nS2 = ctx(d)
                    dst = hs[:, :, :, o:32]
                    TT(eng, dst, dst, w1[:, :, :, 0:n], ALU.add, [nHs, nW1], [nHs])
                    TT(eng, dst, dst, w2[:, :, :, 0:n], ALU.add, [nHs, nW2], [nHs])
                if o < 4:
                    m = 4 - o
                    for d in range(2):
                        nXd, w1, w2, hs, nW1, nW2, nHs, nS1, nS2 = ctx(d)
                        hp = hs[:, :, :, 32:40].rearrange("p r g (s k) -> p r g s k", k=4)
                        srcp = hp[:, :, :, :, 0:m]
                        dstp = hp[:, :, :, :, o:4]
                        s1v = s1p[d][:, :, :, :, 0:m]
                        s2v = s2p[d][:, :, :, :, 0:m]
                        a0 = aj(0, [128, 2, 4, 2, m], j, d)
                        for ri in range(2):
                            TT(eng, s1v[:, ri], srcp[:, ri], a0[:, ri], ALU.mult, [nHs, nS1, 'TAB2b'], [nS1])
                        SWP(eng, s2v, srcp, aj(1, [128, 2, 4, 2, m], j, d), [nHs, nS2, 'TAB2b'], [nS2])
                        for ri in range(2):
                            TT(eng, dstp[:, ri], dstp[:, ri], s1v[:, ri], ALU.add, [nHs, nS1], [nHs])
                            TT(eng, dstp[:, ri], dstp[:, ri], s2v[:, ri], ALU.add, [nHs, nS2], [nHs])
            for k in range(8):
                for d in range(2):
                    nXd, w1, w2, hs, nW1, nW2, nHs, nS1, nS2 = ctx(d)
                    TT(eng, w1[:], hs[:], ak(0, k, 40, d), ALU.mult, [nHs, nW1, 'TABb'], [nW1])
                    SWP(eng, w2[:], hs[:], ak(1, k, 40, d), [nHs, nW2, 'TABb'], [nW2])
                for d in range(2):
                    nXd, w1, w2, hs, nW1, nW2, nHs, nS1, nS2 = ctx(d)
                    cur = XB5[:, :, d, :, :, k]
                    TT(eng, cur, cur, w1[:], ALU.add, [nX, nXd, nW1], [nXd])
                    TT(eng, cur, cur, w2[:], ALU.add, [nX, nXd, nW2], [nXd])

        def p_hp(gb):
            bi = gb % 2
            XB, HPb = Xb2[bi], HP2[bi]
            nX, nH = ('Xb', bi), ('HP', bi)
            XB4 = XB[:].rearrange("p r (d g) c -> p r d g c", d=2)
            XB5 = XB[:].rearrange("p r (d g) (k s) -> p r d g s k", d=2, k=8)
            nXall = [nX, ('Xbd', bi, 0), ('Xbd', bi, 1)]
            for sq_ in range(2):
                CP('act', Hl2[:, :, sq_, :, gb, :], XB5[:, :, :, :, 35 + 4 * sq_, 7], nXall + ['Hl2'], ['Hl2'])
            HP4 = HPb[:].rearrange("p r (d g) c -> p r d g c", d=2)
            XBs = XB[:].rearrange("p r q (k s) -> p r q s k", k=8)
            for ri in range(2):
                CP('act', HPb[:, ri, :, 1:257].rearrange("p q (s k) -> p q s k", k=8), XBs[:, ri, :, 0:32, :], nXall + [nH], [nH])
                CP('act', HPb[:, ri, :, 257:289].rearrange("p q (s k) -> p q s k", k=8), XBs[:, ri, :, 32:36, :], nXall + [nH], [nH])
                CP('act', HPb[:, ri, :, 289:313].rearrange("p q (s k) -> p q s k", k=8), XBs[:, ri, :, 36:39, :], nXall + [nH], [nH])
                CP('act', HPb[:, ri, :, 313:320], XBs[:, ri, :, 39, 0:7], nXall + [nH], [nH])
            CP('act', HP4[:, :, :, :, 0], h0b[:], [nH, 'h0b'], [nH])
            P.add('pool', lambda e, HPb=HPb: e.memset(HPb[:, :, :, 256:289:32], 0.0), [nH], [nH])

        def p_y(gb):
            bi = gb % 2
            XB, HPb = Xb2[bi], HP2[bi]
            nX, nH = ('Xb', bi), ('HP', bi)
            XB4 = XB[:].rearrange("p r (d g) c -> p r d g c", d=2)
            XB5 = XB[:].rearrange("p r (d g) (k s) -> p r d g s k", d=2, k=8)
            nXall = [nX, ('Xbd', bi, 0), ('Xbd', bi, 1)]
            for g8 in range(8):
                g = gb * 8 + g8
                gh, g4 = g8 // 4, g8 % 4
                ps_ = slice(gh * 64, (gh + 1) * 64)
                b = nextps()
                yi = yc[0] % 2
                yc[0] += 1
                o = ps[b]
                MM(o[:, 0:NCH], mgb[bi][:, g8, :], U2[:, g, :], True, False, [('mgb', bi)], [('ps', b)])
                for ri in range(2):
                    MM(o[:, 0:NCH], clb2[bi][ps_, g4, 0, ri, :], HPb[ps_, ri, g4, :], False, False, [nH, ('clb', bi)], [('ps', b)])
                for ri in range(2):
                    l = clb2[bi][ps_, g4, 1, ri, :]
                    for qi, (c0_, c1_, rv) in enumerate(rngs):
                        MM(o[:, c0_:c1_], l, HPb[ps_, ri, 4 + g4, rv], False, (ri == 1 and qi == 2), [nH, ('clb', bi)], [('ps', b)])
                CP('act', Ysb[yi][:], o[:, 0:NCH], [('ps', b), ('Ysb', yi)], [('Ysb', yi)])
                b2 = nextps()
                for pi, (c0, n) in enumerate(pieces):
                    TR(ps[b2][0:n, pi * 128:(pi + 1) * 128], Ysb[yi][:, c0:c0 + n], ident[:, :], [('Ysb', yi)], [('ps', b2)])
                for pi, (c0, n) in enumerate(pieces):
                    CP('act', Ytok[0:n, pi, :, g8 * 16:(g8 + 1) * 16],
                       ps[b2][0:n, pi * 128:(pi + 1) * 128].rearrange("p (r i) -> p r i", r=8), [('ps', b2), 'Ytok'], ['Ytok'])

        def p_tail(gb):
            bi = gb % 2
            XB, HPb = Xb2[bi], HP2[bi]
            nX, nH = ('Xb', bi), ('HP', bi)
            XB4 = XB[:].rearrange("p r (d g) c -> p r d g c", d=2)
            XB5 = XB[:].rearrange("p r (d g) (k s) -> p r d g s k", d=2, k=8)
            nXall = [nX, ('Xbd', bi, 0), ('Xbd', bi, 1)]
            Yf = Ytok[:].rearrange("p a r c -> p a (r c)")
            bs = [nextps(), nextps()]
            for hh in range(2):
                for pi in range(3):
                    MM(ps[bs[hh]][:, :], selT[:, pi, :], Yf[:, pi, hh * 512:(hh + 1) * 512], pi == 0, pi == 2,
                       ['Ytok', 'selT'], [('ps', bs[hh])])
                CP('act', Y2[:, hh * 512:(hh + 1) * 512], ps[bs[hh]][:, :], [('ps', bs[hh]), 'Y2'], ['Y2'])
            TT('dve', g1[:], Y2[:], Y2[:], ALU.mult, ['Y2', 'g1'], ['g1'])
            TS('dve', g1[:], g1[:], 0.044715, 1.0, ALU.mult, ALU.add, ['g1'], ['g1'])
            TT('dve', g1[:], g1[:], Y2[:], ALU.mult, ['g1', 'Y2'], ['g1'])
            ACT(g1[:], g1[:], AF.Sigmoid, ['g1'], ['g1'], scale=GELU_C)
            TT('dve', g1[:], g1[:], Y2[:], ALU.mult, ['Y2', 'g1'], ['g1'])
            g1v = g1[:].rearrange("p (r c) -> p r c", r=8)
            yti = gb % 2
            for r4 in range(2):
                b3 = nextps()
                for q in range(4):
                    TR(ps[b3][:, q * 128:(q + 1) * 128], g1v[:, r4 * 4 + q, :], ident[:, :], ['g1'], [('ps', b3)])
                CP('act', ysT[yti][:, :, r4 * 4:(r4 + 1) * 4].rearrange("p c r -> p r c"),
                   ps[b3][:, :].rearrange("p (q c) -> p q c", q=4), [('ps', b3), ('ysT', yti)], [('ysT', yti)])
            P.dma('pool', ysg_d[gb * 128:(gb + 1) * 128, :], ysT[yti][:].rearrange("p c r -> p (c r)"),
                  r=[('ysT', yti)], w=[('ysg_d', gb)])

        p_load(0)
        p_xc(0)
        p_rec(0)
        for gb in range(8):
            if gb + 1 < 8:
                p_load(gb + 1)
                p_xc(gb + 1)
            p_hp(gb)
            p_y(gb)
            if gb + 1 < 8:
                p_rec(gb + 1)
            p_tail(gb)
        st = A.alloc("st", [64, 128])
        for ri, dst in ((0, sre_o), (1, sim_o)):
            for sq_ in range(2):
                b = nextps()
                TR(ps[b][0:64, 0:128], Hl2[:, ri, sq_].rearrange("p d b g -> p (d b g)"), ident[:, :], ['Hl2'], [('ps', b)])
                CP('dve', st[:], ps[b][0:64, 0:128], [('ps', b), 'st'], ['st'])
                P.dma('pool', dst[sq_, :, :], st[:], r=['st'], w=[('so', ri, sq_)])

    if 'D' in stages:
        ssm_main()
        P.barrier()
    A.reset(PERS0)

    wback = {}

    def attention():
        Wg = A.alloc("Wg", [128, 8, 1024], BF16)
        Wso = A.alloc("Wso", [128, 8, D], BF16)
        Wao = A.alloc("Wao", [128, 8, D], BF16)
        wst = [A.alloc("wst", [128, 4, 512]) for _ in range(2)]
        wback.update(Wg=Wg, Wso=Wso, Wao=Wao, mark=A.mark())
        chunks = []
        for (src, nk, ncol, dstw, nm) in ((w_glu, 8, 1024, Wg, 'Wg'), (w_so, 8, D, Wso, 'Wso'), (w_ao, 8, D, Wao, 'Wao')):
            v = src.rearrange("(k p) n -> p k n", p=128)
            for k4 in range(nk // 4):
                for c in range(ncol // 512):
                    chunks.append((v, k4, c, dstw, nm))
        wn = [0]

        def wload(n):
            for _ in range(n):
                if wn[0] >= len(chunks):
                    return
                v, k4, c, dstw, nm = chunks[wn[0]]
                i = wn[0] % 2
                wn[0] += 1
                P.dma('sp', wst[i][:], v[:, k4 * 4:(k4 + 1) * 4, c * 512:(c + 1) * 512], w=[('wst', i)])
                CP(['dve', 'act'][i], dstw[:, k4 * 4:(k4 + 1) * 4, c * 512:(c + 1) * 512], wst[i][:], [('wst', i), nm], [nm])
        pair_tile = [4, 5, 0, 1, 2, 3, 6, 7]
        V1 = A.alloc("V1", [128, 12, 16, 65], BF16)
        V1c = A.alloc("V1c", [128, 4, 16, 65], BF16)
        vl = [A.alloc("vl", [128, 1024]) for _ in range(2)]
        rmf = A.alloc("rmf", [128, 8, 8])
        rmb = A.alloc("rmb", [128, 8, 8], BF16)
        P.dma('sp', rmf[:], rmask[:, :, :], w=['rmf'])
        CP('dve', rmb[:], rmf[:], ['rmf'], ['rmb'])
        MS('pool', V1[:], 1.0, ['V1'])
        MS('pool', V1c[:], 1.0, ['V1c'])
        for t in range(16):
            src = v_d[t * 128:(t + 1) * 128, :] if t < 12 else cv[(t - 12) * 128:(t - 11) * 128, :]
            P.dma('sp', vl[t % 2][:], src, w=[('vl', t % 2)])
            dstv = V1[:, t, :, 0:64] if t < 12 else V1c[:, t - 12, :, 0:64]
            CP('dve', dstv, vl[t % 2][:].rearrange("p (h d) -> p h d", h=16), [('vl', t % 2), 'V1', 'V1c'], ['V1', 'V1c'])
        qT = [A.alloc("qT", [64, 1024], BF16) for _ in range(2)]
        kT = [A.alloc("kT", [64, 1536], BF16) for _ in range(2)]
        kcT = [A.alloc("kcT", [64, 512], BF16) for _ in range(2)]
        TB = [A.alloc("TB", [128, 17, 64]) for _ in range(2)]
        Pc = A.alloc("Pc", [128, 4, 512], BF16)
        Pw_ = [A.alloc("Pw_", [128, 6, 64], BF16) for _ in range(2)]
        Sw = [A.alloc("Sw", [128, 6, 64]) for _ in range(2)]
        Pp = [A.alloc("Pp", [128, 2, 256], BF16) for _ in range(2)]
        Ah = A.alloc("Ah", [128, 2, 64])
        Ar = A.alloc("Ar", [64, 8, 64])
        rc = A.alloc("rc", [128, 1])
        aT = [A.alloc("aT", [64, 512]) for _ in range(2)]
        cnt = [0]
        ac = [0]
        for h in range(16):
            hi = h % 2
            wload(2)
            P.dma('sp', qT[hi][:], qT_d[h * 64:(h + 1) * 64, :], w=[('qT', hi)])
            P.dma('sp', kT[hi][:], kT_d[h * 64:(h + 1) * 64, :], w=[('kT', hi)])
            P.dma('sp', kcT[hi][:], kcT_d[h * 64:(h + 1) * 64, :], w=[('kcT', hi)])
            P.dma('sp', TB[hi][:], tb_d[h, :, :, :], w=[('TB', hi)])
            for ct in range(4):
                b = nextps()
                MM(ps[b][:, :], kcT[hi][:, ct * 128:(ct + 1) * 128], qT[hi][:, 0:512], True, True,
                   [('kcT', hi), ('qT', hi)], [('ps', b)])
                ACT(Pc[:, ct, :], ps[b][:, :], AF.Exp, [('ps', b), 'Pc'], ['Pc'], scale=0.125)
            for ql in range(8):
                j0 = min(ql, 4)
                j1 = max(ql, 4) + 7
                p0 = j0 // 2
                p1 = j1 // 2
                npair = p1 - p0 + 1
                wi_ = cnt[0] % 2
                cnt[0] += 1
                b = nextps()
                for pi in range(npair):
                    tl = pair_tile[p0 + pi]
                    MM(ps[b][:, pi * 64:(pi + 1) * 64], kT[hi][:, tl * 128:(tl + 1) * 128], qT[hi][:, ql * 64:(ql + 1) * 64],
                       True, True, [('kT', hi), ('qT', hi)], [('ps', b)])
                i0 = 2 * p0 - ql + 3 + 1
                STT('dve', Sw[wi_][:, 0:npair, :], ps[b][:, 0:npair * 64].rearrange("p (a q) -> p a q", q=64), 0.125,
                    TB[hi][:, i0:i0 + 2 * npair:2, :], ALU.mult, ALU.add, [('ps', b), ('TB', hi), ('Sw', wi_)], [('Sw', wi_)])
                ACT(Pw_[wi_][:, 0:npair, :], Sw[wi_][:, 0:npair, :], AF.Exp, [('Sw', wi_), ('Pw_', wi_)], [('Pw_', wi_)])
                TT('dve', Pw_[wi_][:, 0:npair, :], Pw_[wi_][:, 0:npair, :],
                   rmb[:, ql, p0:p0 + npair][:, :, None].broadcast_to([128, npair, 64]), ALU.mult,
                   [('Pw_', wi_), 'rmb'], [('Pw_', wi_)])
                b2 = nextps()
                for pi in range(npair):
                    MM(ps[b2][0:64, 0:65], Pw_[wi_][:, pi, :], V1[:, pair_tile[p0 + pi], h, :], pi == 0, False,
                       [('Pw_', wi_), 'V1'], [('ps', b2)])
                for ct in range(4):
                    MM(ps[b2][0:64, 0:65], Pc[:, ct, ql * 64:(ql + 1) * 64], V1c[:, ct, h, :], False, ct == 3,
                       ['Pc', 'V1c'], [('ps', b2)])
                RCP(rc[0:64, :], ps[b2][0:64, 64:65], [('ps', b2), 'rc'], ['rc'])
                TS('dve', Ar[:, ql, :], ps[b2][0:64, 0:64], rc[0:64, 0:1], None, ALU.mult, None, [('ps', b2), 'rc', 'Ar'], ['Ar'])
            b = nextps()
            for q in range(8):
                TR(ps[b][0:64, q * 64:(q + 1) * 64], Ar[:, q, :], ident[0:64, 0:64], ['Ar'], [('ps', b)])
            ai = ac[0] % 2
            ac[0] += 1
            CP('act', aT[ai][:], ps[b][0:64, :], [('ps', b), ('aT', ai)], [('aT', ai)])
            P.dma('pool', attnT_d[h * 64:(h + 1) * 64, 0:512], aT[ai][:], r=[('aT', ai)], w=[('attnT', h)])
            for sq_ in range(2):
                tk = 1024 + sq_ * 256
                tq = 512 + sq_ * 256
                pi_ = cnt[0] % 2
                cnt[0] += 1
                for kt in range(2):
                    b = nextps()
                    MM(ps[b][:, 0:256], kT[hi][:, tk + kt * 128:tk + (kt + 1) * 128], qT[hi][:, tq:tq + 256], True, True,
                       [('kT', hi), ('qT', hi)], [('ps', b)])
                    ACT(Pp[pi_][:, kt, :], ps[b][:, 0:256], AF.Exp, [('ps', b), ('Pp', pi_)], [('Pp', pi_)], scale=0.125)
                b4 = nextps()
                for qc in range(2):
                    b2 = nextps()
                    for kt in range(2):
                        MM(ps[b2][:, 0:65], Pp[pi_][:, kt, qc * 128:(qc + 1) * 128], V1[:, 8 + sq_ * 2 + kt, h, :], kt == 0, kt == 1,
                           [('Pp', pi_), 'V1'], [('ps', b2)])
                    RCP(rc[:, :], ps[b2][:, 64:65], [('ps', b2), 'rc'], ['rc'])
                    TS('dve', Ah[:, qc, :], ps[b2][:, 0:64], rc[:, 0:1], None, ALU.mult, None, [('ps', b2), 'rc', 'Ah'], ['Ah'])
                    TR(ps[b4][0:64, qc * 128:(qc + 1) * 128], Ah[:, qc, :], ident[:, :], ['Ah'], [('ps', b4)])
                ai = ac[0] % 2
                ac[0] += 1
                CP('act', aT[ai][:, 0:256], ps[b4][0:64, 0:256], [('ps', b4), ('aT', ai)], [('aT', ai)])
                P.dma('pool', attnT_d[h * 64:(h + 1) * 64, tq:tq + 256], aT[ai][:, 0:256], r=[('aT', ai)], w=[('attnTp', h, sq_)])

    if 'E' in stages:
        attention()
        P.barrier()
    A.reset(wback.get('mark', PERS0))

    mgd_d = dscr("mgd_d", [D, 1024], BF16)

    def back1():
        Wg, Wso, Wao = wback['Wg'], wback['Wso'], wback['Wao']
        bg = A.alloc("bg", [128, 8])
        P.dma('sp', bg[:], bglu[:, :], w=['bg'])
        TBK = 256
        ysg = A.alloc("ysg", [128, 8, TBK])
        ysgb = A.alloc("ysgb", [128, 8, TBK], BF16)
        zs = A.alloc("zs", [128, 8, TBK])
        za = A.alloc("za", [128, 8, TBK])
        at = A.alloc("at", [128, 8, TBK])
        atb = A.alloc("atb", [128, 8, TBK], BF16)
        ys2 = A.alloc("ys2", [128, 8, TBK], BF16)
        glu = A.alloc("glu", [128, TBK])
        gsa = [A.alloc("gsa", [128, 2, TBK]) for _ in range(2)]
        mg = A.alloc("mg", [128, 16, TBK], BF16)
        m1 = A.alloc("m1", [128, TBK])
        m2 = A.alloc("m2", [128, TBK])
        fmv = lambda dten, tb: dten[:, tb * TBK:(tb + 1) * TBK].rearrange("(k p) n -> p k n", p=128)
        for tb in range(NB // TBK):
            P.dma('sp', ysg[:], fmv(ysg_d, tb), w=['ysg'])
            P.dma('sp', zs[:], fmv(zs_d, tb), w=['zs'])
            P.dma('sp', za[:], fmv(za_d, tb), w=['za'])
            P.dma('sp', at[:], fmv(attnT_d, tb), w=['at'])
            CP('act', ysgb[:], ysg[:], ['ysg', 'ysgb'], ['ysgb'])
            TT('dve', at[:], at[:], za[:], ALU.mult, ['at', 'za'], ['at'])
            CP('act', atb[:], at[:], ['at', 'atb'], ['atb'])
            for m in range(8):
                b = nextps()
                for k in range(8):
                    MM(ps[b][:, 0:TBK], Wg[:, k, m * 128:(m + 1) * 128], ysgb[:, k, :], k == 0, k == 7, ['Wg', 'ysgb'], [('ps', b)])
                ACT(glu[:], ps[b][:, 0:TBK], AF.Sigmoid, [('ps', b), 'bg', 'glu'], ['glu'], bias=bg[:, m:m + 1])
                TT('dve', glu[:], glu[:], ysg[:, m, :], ALU.mult, ['glu', 'ysg'], ['glu'])
                TT('dve', ys2[:, m, :], glu[:], zs[:, m, :], ALU.mult, ['glu', 'zs', ('ys2', m)], [('ys2', m)])
            allys2 = [('ys2', m) for m in range(8)]
            for nn in range(16):
                gi = nn % 2
                P.dma('sp', gsa[gi][:, 0, :], gs_d[nn * 128:(nn + 1) * 128, tb * TBK:(tb + 1) * TBK], w=[('gsa', gi, 0)])
                P.dma('sp', gsa[gi][:, 1, :], ga_d[nn * 128:(nn + 1) * 128, tb * TBK:(tb + 1) * TBK], w=[('gsa', gi, 1)])
                b = nextps()
                b2 = nextps()
                for k in range(8):
                    MM(ps[b][:, 0:TBK], Wso[:, k, nn * 128:(nn + 1) * 128], ys2[:, k, :], k == 0, k == 7, ['Wso'] + allys2, [('ps', b)])
                for k in range(8):
                    MM(ps[b2][:, 0:TBK], Wao[:, k, nn * 128:(nn + 1) * 128], atb[:, k, :], k == 0, k == 7, ['Wao', 'atb'], [('ps', b2)])
                TT('dve', m1[:], ps[b][:, 0:TBK], gsa[gi][:, 0, :], ALU.mult, [('ps', b), ('gsa', gi, 0), 'm1'], ['m1'])
                TT('dve', m2[:], ps[b2][:, 0:TBK], gsa[gi][:, 1, :], ALU.mult, [('ps', b2), ('gsa', gi, 1), 'm2'], ['m2'])
                TT('dve', mg[:, nn, :], m1[:], m2[:], ALU.add, ['m1', 'm2', 'mg'], ['mg'])
            P.dma('pool', fmv(mgd_d, tb), mg[:], r=['mg'], w=[('mgd_d', tb)])

    def back2():
        Wo = A.alloc("Wo", [128, 16, D], BF16)
        wst = [A.alloc("wst", [128, 4, 512]) for _ in range(2)]
        n = [0]
        v_ = w_o.rearrange("(k p) n -> p k n", p=128)
        for k4 in range(4):
            for c in range(4):
                i = n[0] % 2
                n[0] += 1
                P.dma('sp', wst[i][:], v_[:, k4 * 4:(k4 + 1) * 4, c * 512:(c + 1) * 512], w=[('wst', i)])
                CP(['dve', 'act'][i], Wo[:, k4 * 4:(k4 + 1) * 4, c * 512:(c + 1) * 512], wst[i][:], [('wst', i), 'Wo'], ['Wo'])
        gbc = [A.alloc("gbc", [128, D]) for _ in range(2)]
        for v in range(2):
            P.dma('sp', gbc[v][:], mod_d[v:v + 1, 2 * D:3 * D].partition_broadcast(128), w=[('gbc', v)])
        mgl = [A.alloc("mgl", [128, 16, 128], BF16) for _ in range(2)]
        xin = [A.alloc("xin", [128, D]) for _ in range(2)]
        yo = [A.alloc("yo", [128, D]) for _ in range(2)]
        for t in range(NB // 128):
            tok0 = t * 128
            xi = t % 2
            v = 1 if tok0 < 512 else 0
            src = xown[tok0:tok0 + 128, :] if tok0 < 512 else xp[tok0 - 512:tok0 - 512 + 128, :]
            dsto = ys_o[tok0:tok0 + 128, :] if tok0 < 512 else yp_o[tok0 - 512:tok0 - 512 + 128, :]
            P.dma('sp', xin[xi][:], src, w=[('xin', xi)])
            P.dma('sp', mgl[xi][:], mgd_d[:, tok0:tok0 + 128].rearrange("(k p) n -> p k n", p=128), w=[('mgl', xi)])
            for nb in range(4):
                b = nextps()
                for k in range(16):
                    MM(ps[b][:, :], mgl[xi][:, k, :], Wo[:, k, nb * 512:(nb + 1) * 512], k == 0, k == 15,
                       ['Wo', ('mgl', xi)], [('ps', b)])
                TT('dve', yo[xi][:, nb * 512:(nb + 1) * 512], ps[b][:, :], gbc[v][:, nb * 512:(nb + 1) * 512], ALU.mult,
                   [('ps', b), ('gbc', v), ('yo', xi)], [('yo', xi)])
            TT('dve', yo[xi][:], yo[xi][:], xin[xi][:], ALU.add, [('yo', xi), ('xin', xi)], [('yo', xi)])
            P.dma('pool', dsto, yo[xi][:], r=[('yo', xi)], w=[('yout', tok0)])

    if 'F' in stages:
        back1()
        P.barrier()
        A.reset(PERS0)
        back2()
        P.barrier()
    P.emit(es)
    es.close()
    return nc, P


_NC = None


def _consts():
    c = {}
    c["ident"] = np.eye(128, dtype=np.float32)
    s = np.arange(128) // 16
    c["maskf"] = (s[None, :] >= s[:, None]).astype(np.float32)
    c["maskb"] = (s[:, None] >= s[None, :]).astype(np.float32)
    kc = np.arange(64)[:, None]
    qc = np.arange(64)[None, :]
    cs = np.clip(qc - 8, 0, 48)
    c["colmask"] = np.where((kc >= cs) & (kc < cs + 16), 0.0, NEG).astype(np.float32)
    dc = np.clip(kc - qc + 15, 0, 30)
    oh = np.zeros((31, 64, 64), np.float32)
    for q in range(64):
        for k in range(64):
            oh[dc[k, q], q, k] = 1.0
    c["onehot"] = oh
    return c


def kernel(x_prompt, x_sample, cache_k, cache_v, state_ssm_re, state_ssm_im, c, c_ctx,
           norm_w, w_ada, b_ada, w_in, q_norm_w, k_norm_w, rel_pos_bias,
           ssm_a_re, ssm_a_im, ssm_log_dt, ssm_b_re, ssm_b_im, ssm_c_re, ssm_c_im, ssm_d,
           w_glu, b_glu, w_ssm_out, w_att_out, w_o):
    global _NC
    f = lambda a: np.ascontiguousarray(np.asarray(a, dtype=np.float32))
    if _NC is None:
        _NC = build()
    nc = _NC[0]
    x_prompt = f(x_prompt)
    x_sample = f(x_sample)
    c = f(c)
    c_ctx = f(c_ctx)
    shared = _consts()
    shared.update({
        "w_ada": f(w_ada)[0], "b_ada": f(b_ada)[0][None, :], "norm_w": f(norm_w)[0][None, :], "w_in": f(w_in)[0],
        "qnw": f(q_norm_w)[0][None, :], "knw": f(k_norm_w)[0][None, :],
        "rpbT": np.ascontiguousarray(f(rel_pos_bias)[0].reshape(240, 31).T),
        "are": f(ssm_a_re)[0].reshape(128, 64), "aim": f(ssm_a_im)[0].reshape(128, 64),
        "logdt": f(ssm_log_dt)[0].reshape(1, 128),
        "bre": f(ssm_b_re)[0].reshape(128, 64, 16), "bim": f(ssm_b_im)[0].reshape(128, 64, 16),
        "cre": f(ssm_c_re)[0].reshape(128, 16, 64), "cim": f(ssm_c_im)[0].reshape(128, 16, 64),
        "dcol": np.ascontiguousarray(np.tile(f(ssm_d)[0].reshape(64, 16).T, (8, 1))),
        "w_glu": f(w_glu)[0], "bglu": np.ascontiguousarray(f(b_glu)[0].reshape(8, 128).T),
        "w_so": f(w_ssm_out)[0], "w_ao": f(w_att_out)[0], "w_o": f(w_o)[0],
    })
    in_maps = []
    for core in range(NCORES):
        b, q = core % 2, core // 2
        m = dict(shared)
        m["xs"] = x_sample[b]
        m["xown"] = x_sample[b, q * 512:(q + 1) * 512]
        kb = 8 * q - 4
        hrows = [min(max(kb + j, 0), 31) for j in (0, 1, 2, 3, 12, 13, 14, 15)]
        m["xhalo"] = np.ascontiguousarray(np.concatenate([x_sample[b, r * 64:(r + 1) * 64] for r in hrows], 0))
        sel = np.zeros((128, 3, 128), np.float32)
        for pi in range(2):
            for cc in range(128):
                o = pi * 128 + cc - 64 * q
                if 0 <= o < 64:
                    sel[cc, pi, o] = 1.0
        for cc in range(64):
            sel[cc, 2, 64 + cc] = 1.0
        m["sel3"] = sel
        rm = np.zeros((128, 8, 8), np.float32)
        for ql in range(8):
            rs = min(max(8 * q + ql - 4, 0), 24)
            for j in range(16):
                if rs - kb <= j < rs - kb + 8:
                    rm[(j % 2) * 64:(j % 2) * 64 + 64, ql, j // 2] = 1.0
        m["rmask"] = rm
        m["xp"] = x_prompt[2 * core:2 * core + 2].reshape(NPR, D)
        m["cond2"] = np.ascontiguousarray(np.stack([c_ctx, c[b]]).reshape(32, 128))
        m["ck"] = f(cache_k)[b, 0].reshape(512, 1024)
        m["cv"] = f(cache_v)[b, 0].reshape(512, 1024)
        m["h0re"] = f(state_ssm_re)[b, 0].reshape(128, 64)
        m["h0im"] = f(state_ssm_im)[b, 0].reshape(128, 64)
        in_maps.append(m)
    res = run_bass_kernel_spmd(nc, in_maps, core_ids=list(range(NCORES)))
    R = res.results
    y_p = np.concatenate([r["yp_o"].reshape(2, 256, D) for r in R], 0)
    y_s = np.zeros((2, NS, D), np.float32)
    for core in range(NCORES):
        b, q = core % 2, core // 2
        y_s[b, q * 512:(q + 1) * 512] = R[core]["ys_o"]
    new_k = np.concatenate([r["ko"].reshape(2, 1, 256, 16, 64) for r in R], 0)
    new_v = np.concatenate([r["vo"].reshape(2, 1, 256, 16, 64) for r in R], 0)
    def _st(a):
        return a.reshape(2, 2, 8, 4, 2, 64).transpose(0, 1, 2, 4, 3, 5).reshape(2, 1, 2, 64, 64)
    st_re = np.concatenate([_st(r["sre_o"]) for r in R], 0)
    st_im = np.concatenate([_st(r["sim_o"]) for r in R], 0)
    return (y_p, y_s, new_k, new_v, st_re, st_im)
```

```python
import numpy as np
from contextlib import ExitStack
import concourse.bass as bass
import concourse.mybir as mybir
from concourse.bass_utils import run_bass_kernel_spmd

F32 = mybir.dt.float32
BF16 = mybir.dt.bfloat16
ALU = mybir.AluOpType
AF = mybir.ActivationFunctionType
AX = mybir.AxisListType

D = 2048
NCORES = 8
EPS = 1e-6
IN_W = 10240
NS = 2048
NPR = 512
NT = 2560
NCH = 320
NEG = -1e30
GELU_C = 1.5957691216057308


class Prog:
    NDMA = 8
    ENGS = ['pe', 'act', 'dve', 'pool', 'sp']

    def __init__(self, nc):
        self.nc = nc
        self.ops = []
        self.lastw = {}
        self.readers = {}
        self.forced = set()

    def add(self, eng, fn, r=(), w=(), dma=False, deps=None):
        idx = len(self.ops)
        dd = set(deps) if deps else set()
        for x in r:
            if x in self.lastw:
                dd.add(self.lastw[x])
        for x in w:
            if x in self.lastw:
                dd.add(self.lastw[x])
            for kk, vv in self.readers.get(x, {}).items():
                if kk == 'dmas':
                    dd.update(vv)
                elif vv is not None:
                    dd.add(vv)
        for x in r:
            self.readers.setdefault(x, {})[(eng, dma)] = idx if not dma else None
            if dma:
                self.readers[x].setdefault('dmas', []).append(idx)
        for x in w:
            self.lastw[x] = idx
            self.readers[x] = {}
        self.ops.append(dict(eng=eng, fn=fn, deps=dd, dma=dma))
        return idx

    def dma(self, eng, out, in_, r=(), w=(), **kw):
        return self.add(eng, lambda e: e.dma_start(out=out, in_=in_, **kw), r, w, dma=True)

    def barrier(self):
        lastc = {}
        lastd = {}
        for i, op in enumerate(self.ops):
            if op['fn'] is None:
                continue
            if op['dma']:
                lastd.setdefault(op['eng'], []).append(i)
            else:
                lastc[op['eng']] = i
        deps = set(lastc.values())
        for e, l in lastd.items():
            deps.update(l[-self.NDMA:])
        self.forced.update(lastc.values())
        for e in self.ENGS:
            self.add(e, None, deps=deps)
        self.lastw = {}
        self.readers = {}

    def emit(self, es):
        nc = self.nc
        ops = self.ops
        engs = self.ENGS
        csem = {e: es.enter_context(nc.semaphore("c_" + e)) for e in engs if e != 'sp'}
        dsem = {e: [es.enter_context(nc.semaphore("d_%s%d" % (e, i))) for i in range(self.NDMA)]
                for e in ['sp', 'act', 'pool']}

        def elide(dop, op):
            return (not dop['dma']) and (not op['dma']) and dop['eng'] == 'pe' and op['eng'] == 'pe' \
                and op['fn'] is not None

        needed = [False] * len(ops)
        for i in self.forced:
            needed[i] = True
        for i, op in enumerate(ops):
            for d in op['deps']:
                if elide(ops[d], op):
                    continue
                needed[d] = True
        ccount = {e: 0 for e in engs}
        dcount = {e: 0 for e in engs}
        ev = [None] * len(ops)
        pre = [None] * len(ops)
        for i, op in enumerate(ops):
            e = op['eng']
            if op['fn'] is None:
                continue
            if op['dma']:
                n = dcount[e]
                dcount[e] += 1
                sem = dsem[e][n % self.NDMA]
                ev[i] = (sem, 16 * (n // self.NDMA + 1))
                if n >= self.NDMA:
                    pre[i] = (sem, 16 * (n // self.NDMA))
            elif needed[i]:
                ccount[e] += 1
                ev[i] = (csem[e], ccount[e])
        per = {e: [] for e in engs}
        for i, op in enumerate(ops):
            per[op['eng']].append(i)
        self.stats = dict(ccount=ccount, dcount=dcount, nops={e: len(per[e]) for e in engs})

        def run(ename, eobj):
            waited = {}
            for i in per[ename]:
                op = ops[i]
                waits = []
                if pre[i] is not None:
                    waits.append(pre[i])
                for d in sorted(op['deps']):
                    if elide(ops[d], op):
                        continue
                    if ev[d] is None:
                        continue
                    waits.append(ev[d])
                for sem, val in waits:
                    key = id(sem)
                    if waited.get(key, 0) >= val:
                        continue
                    waited[key] = val
                    eobj.wait_ge(sem, val)
                if op['fn'] is None:
                    continue
                ins = op['fn'](eobj)
                if ev[i] is not None:
                    ins.then_inc(ev[i][0], 16 if op['dma'] else 1)

        with nc.Block() as block:
            @block.tensor
            def _(e):
                run('pe', e)

            @block.scalar
            def _(e):
                run('act', e)

            @block.vector
            def _(e):
                run('dve', e)

            @block.gpsimd
            def _(e):
                run('pool', e)

            @block.sync
            def _(e):
                run('sp', e)


class Arena:
    BASE = 16512
    LIMIT = 229344

    def __init__(self, nc):
        self.nc = nc
        self.off = self.BASE
        self.cnt = 0

    def alloc(self, name, shape, dt=F32):
        n = 1
        for s in shape[1:]:
            n *= s
        nb = n * (4 if dt == F32 else 2)
        nb = (nb + 63) // 64 * 64
        assert self.off + nb <= self.LIMIT, "SBUF arena overflow at %s: %d + %d" % (name, self.off, nb)
        self.cnt += 1
        t = self.nc.alloc_sbuf_tensor_at("%s_%d" % (name, self.cnt), shape, dt, offset=self.off)
        self.off += nb
        return t

    def mark(self):
        return self.off

    def reset(self, m):
        self.off = m


def build(stages="ACTBDEF"):
    nc = bass.Bass("TRN2", target_bir_lowering=False)
    P = Prog(nc)
    es = ExitStack()
    A = Arena(nc)

    def din(name, shape):
        return nc.dram_tensor(name, shape, F32, kind="ExternalInput").ap()

    def dout(name, shape):
        return nc.dram_tensor(name, shape, F32, kind="ExternalOutput").ap()

    def dscr(name, shape, dt=F32):
        return nc.dram_tensor(name, shape, dt, kind="Internal").ap()

    xs = din("xs", [NS, D])
    xp = din("xp", [NPR, D])
    xown = din("xown", [512, D])
    xhalo = din("xhalo", [512, D])
    sel3 = din("sel3", [128, 3, 128])
    rmask = din("rmask", [128, 8, 8])
    cond2 = din("cond2", [32, 128])
    ck = din("ck", [512, 1024])
    cv = din("cv", [512, 1024])
    h0re = din("h0re", [128, 64])
    h0im = din("h0im", [128, 64])
    are = din("are", [128, 64])
    aim = din("aim", [128, 64])
    logdt = din("logdt", [1, 128])
    bre = din("bre", [128, 64, 16])
    bim = din("bim", [128, 64, 16])
    cre = din("cre", [128, 16, 64])
    cim = din("cim", [128, 16, 64])
    dcol = din("dcol", [128, 64])
    w_ada = din("w_ada", [D, 3 * D])
    b_ada = din("b_ada", [1, 3 * D])
    norm_w = din("norm_w", [1, D])
    w_in = din("w_in", [D, IN_W])
    qnw = din("qnw", [1, 64])
    knw = din("knw", [1, 64])
    rpbT = din("rpbT", [31, 240])
    w_glu = din("w_glu", [1024, 1024])
    bglu = din("bglu", [128, 8])
    w_so = din("w_so", [1024, D])
    w_ao = din("w_ao", [1024, D])
    w_o = din("w_o", [D, D])
    ident_d = din("ident", [128, 128])
    maskf_d = din("maskf", [128, 128])
    maskb_d = din("maskb", [128, 128])
    colmask_d = din("colmask", [64, 64])
    onehot_d = din("onehot", [31, 64, 64])

    ys_o = dout("ys_o", [512, D])
    yp_o = dout("yp_o", [NPR, D])
    ko = dout("ko", [NPR, 1024])
    vo = dout("vo", [NPR, 1024])
    sre_o = dout("sre_o", [2, 64, 128])
    sim_o = dout("sim_o", [2, 64, 128])

    mod_d = dscr("mod_d", [2, 3 * D])
    v_d = dscr("v_d", [1536, 1024])
    qT_d = dscr("qT_d", [1024, 1024], BF16)
    kT_d = dscr("kT_d", [1024, 1536], BF16)
    kcT_d = dscr("kcT_d", [1024, 512], BF16)
    zs_d = dscr("zs_d", [1024, 1024])
    za_d = dscr("za_d", [1024, 1024])
    gs_d = dscr("gs_d", [D, 1024])
    ga_d = dscr("ga_d", [D, 1024])
    ysg_d = dscr("ysg_d", [1024, 1024])
    attnT_d = dscr("attnT_d", [1024, 1024])
    xmat_d = dscr("xmat_d", [128, 64, 2, 2, 64], BF16)
    cl_d = dscr("cl_d", [64, 64, 2, 2, 128], BF16)
    mg_d = dscr("mg_d", [128, 64, 128], BF16)
    tab_d = dscr("tab_d", [64, 2, 2, 8, 128])
    tb_d = dscr("tb_d", [16, 128, 17, 64])

    ps = [es.enter_context(nc.psum_tensor("ps%d" % i, [128, 512], F32)) for i in range(8)]
    pcnt = [0]

    def nextps():
        i = pcnt[0] % 8
        pcnt[0] += 1
        return i

    def TT(eng, out, in0, in1, op, r, w):
        P.add(eng, lambda e: e.tensor_tensor(out=out, in0=in0, in1=in1, op=op), r, w)

    def TS(eng, out, in0, s1, s2, op0, op1, r, w):
        if s2 is None:
            P.add(eng, lambda e: e.tensor_scalar(out=out, in0=in0, scalar1=s1, scalar2=None, op0=op0), r, w)
        else:
            P.add(eng, lambda e: e.tensor_scalar(out=out, in0=in0, scalar1=s1, scalar2=s2, op0=op0, op1=op1), r, w)

    def STT(eng, out, in0, scalar, in1, op0, op1, r, w):
        P.add(eng, lambda e: e.scalar_tensor_tensor(out=out, in0=in0, scalar=scalar, in1=in1, op0=op0, op1=op1), r, w)

    def ACT(out, in_, func, r, w, **kw):
        P.add('act', lambda e: e.activation(out=out, in_=in_, func=func, **kw), r, w)

    def CP(eng, out, in_, r, w):
        if eng == 'act':
            P.add('act', lambda e: e.copy(out=out, in_=in_), r, w)
        else:
            P.add(eng, lambda e: e.tensor_copy(out=out, in_=in_), r, w)

    def MS(eng, out, val, w):
        P.add(eng, lambda e: e.memset(out, val), (), w)

    def MM(out, lhsT, rhs, start, stop, r, w):
        P.add('pe', lambda e: e.matmul(out, lhsT=lhsT, rhs=rhs, start=start, stop=stop), r, w)

    def TR(out, in_, idn, r, w):
        P.add('pe', lambda e: e.transpose(out=out, in_=in_, identity=idn), r, w)

    def RCP(out, in_, r, w):
        P.add('dve', lambda e: e.reciprocal(out=out, in_=in_), r, w)

    def SWP(eng, out, in_, tab, r, w):
        TT(eng, out[:, 0], in_[:, 1], tab[:, 0], ALU.mult, r, w)
        TT(eng, out[:, 1], in_[:, 0], tab[:, 1], ALU.mult, r, w)

    dq = [0]

    def DQ():
        dq[0] += 1
        return ['sp', 'pool'][dq[0] % 2]

    ident = A.alloc("ident", [128, 128])
    identb = A.alloc("identb", [128, 128], BF16)
    epsc = A.alloc("epsc", [128, 1])
    H0 = A.alloc("H0", [64, 2, 128])
    Hl2 = A.alloc("Hl2", [128, 2, 2, 2, 8, 4])
    PERS0 = A.mark()
    U2 = A.alloc("U2", [128, 64, NCH], BF16)
    PERS = A.mark()

    P.dma('sp', ident[:], ident_d[:, :], w=['ident'])
    CP('pool', identb[:], ident[:], ['ident'], ['identb'])
    MS('pool', epsc[:], EPS, ['epsc'])
    P.barrier()

    def stage_mod():
        cc = A.alloc("cc", [32, 128])
        sT = A.alloc("sT", [128, 32])
        badar = [A.alloc("badar", [2, 128]) for _ in range(3)]
        modrow = [A.alloc("modrow", [2, 128]) for _ in range(3)]
        Wst = [A.alloc("Wst", [128, 16, 128]) for _ in range(3)]
        P.dma('sp', cc[:], cond2[:, :], w=['cc'])
        ACT(cc[:], cc[:], AF.Silu, ['cc'], ['cc'])
        b0 = nextps()
        TR(ps[b0][:, 0:32], cc[:, :], ident[0:32, 0:32], ['cc'], [('ps', b0)])
        CP('dve', sT[:], ps[b0][:, 0:32], [('ps', b0)], ['sT'])
        w_ada_v = w_ada.rearrange("(k p) n -> p k n", p=128)
        st = [0]

        def blocks(n):
            for _ in range(n):
                nb = st[0]
                if nb >= 48:
                    return
                st[0] += 1
                wi = nb % 3
                for kh in range(2):
                    P.dma('sp', Wst[wi][:, kh * 8:(kh + 1) * 8, :],
                          w_ada_v[:, kh * 8:(kh + 1) * 8, nb * 128:(nb + 1) * 128], w=[('Wst', wi, kh)])
                P.dma('sp', badar[wi][:], b_ada[:, nb * 128:(nb + 1) * 128].partition_broadcast(2), w=[('badar', wi)])
                b = nextps()
                for k in range(16):
                    MM(ps[b][0:2, 0:128], sT[:, k::16], Wst[wi][:, k, :], k == 0, k == 15,
                       ['sT', ('Wst', wi, k // 8)], [('ps', b)])
                TT('dve', modrow[wi][0:2, 0:128], ps[b][0:2, 0:128], badar[wi][0:2, 0:128], ALU.add,
                   [('ps', b), ('badar', wi)], [('modrow', wi)])
                P.dma('pool', mod_d[:, nb * 128:(nb + 1) * 128], modrow[wi][0:2, 0:128], r=[('modrow', wi)], w=[('mod_d', nb)])
        return blocks

    mod_blocks = None
    if 'A' in stages:
        mod_blocks = stage_mod()
        mod_blocks(4)

    def setup_ssm():
        ld = A.alloc("ld", [128, 64])
        AreT = A.alloc("AreT", [64, 128])
        AimT = A.alloc("AimT", [64, 128])
        dtb = A.alloc("dtb", [64, 128])
        th = A.alloc("th", [64, 128])
        tn = A.alloc("tn", [64, 128])
        rho = A.alloc("rho", [64, 128])
        irho2 = A.alloc("irho2", [64, 128])
        cs = A.alloc("cs", [64, 128])
        sn = A.alloc("sn", [64, 128])
        Pw = A.alloc("Pw", [64, 2, 9, 128])
        Qw = A.alloc("Qw", [64, 2, 8, 128])
        t1 = A.alloc("t1", [64, 128])
        t2 = A.alloc("t2", [64, 128])
        t3 = A.alloc("t3", [64, 128])
        kap = A.alloc("kap", [64, 2, 128])
        for src, dst, nm in ((are, AreT[:], 'AreT'), (aim, AimT[:], 'AimT'), (h0re, H0[:, 0, :], 'H0'), (h0im, H0[:, 1, :], 'H0')):
            P.dma('sp', ld[:], src[:, :], w=['ld'])
            b = nextps()
            TR(ps[b][0:64, 0:128], ld[:, :], ident[:, :], ['ld'], [('ps', b)])
            CP('dve', dst, ps[b][0:64, 0:128], [('ps', b), nm], [nm])
        P.dma('sp', dtb[:], logdt.partition_broadcast(64), w=['dtb'])
        ACT(dtb[:], dtb[:], AF.Exp, ['dtb'], ['dtb'])
        TS('dve', AreT[:], AreT[:], -1e-4, None, ALU.min, None, ['AreT'], ['AreT'])
        TT('dve', t1[:], AreT[:], dtb[:], ALU.mult, ['AreT', 'dtb'], ['t1'])
        ACT(rho[:], t1[:], AF.Exp, ['t1'], ['rho'])
        ACT(irho2[:], t1[:], AF.Exp, ['t1'], ['irho2'], scale=-2.0)
        TT('dve', th[:], AimT[:], dtb[:], ALU.mult, ['AimT', 'dtb'], ['th'])
        TS('dve', tn[:], th[:], float(1 / (2 * np.pi)), 12582912.0, ALU.mult, ALU.add, ['th'], ['tn'])
        TS('dve', tn[:], tn[:], -12582912.0, float(-2 * np.pi), ALU.add, ALU.mult, ['tn'], ['tn'])
        TT('dve', th[:], th[:], tn[:], ALU.add, ['th', 'tn'], ['th'])
        ACT(sn[:], th[:], AF.Sin, ['th'], ['sn'])
        ACT(t2[:], th[:], AF.Sin, ['th', 't2'], ['t2'], scale=0.5)
        TT('dve', t2[:], t2[:], t2[:], ALU.mult, ['t2'], ['t2'])
        TS('dve', cs[:], t2[:], -2.0, 1.0, ALU.mult, ALU.add, ['t2'], ['cs'])
        MS('pool', Pw[:, 0, 0, :], 1.0, ['Pw'])
        MS('pool', Pw[:, 1, 0, :], 0.0, ['Pw'])
        TT('dve', Pw[:, 0, 1, :], rho[:], cs[:], ALU.mult, ['rho', 'cs', 'Pw'], ['Pw'])
        TT('dve', Pw[:, 1, 1, :], rho[:], sn[:], ALU.mult, ['rho', 'sn', 'Pw'], ['Pw'])

        def cmul(eng, outr, outi, ar, ai, br, bi, rr, ww):
            TT(eng, t1[:], ar, br, ALU.mult, rr + ['t1'], ['t1'])
            TT(eng, t2[:], ai, bi, ALU.mult, rr + ['t2'], ['t2'])
            TT(eng, t3[:], ar, bi, ALU.mult, rr + ['t3'], ['t3'])
            TT(eng, outr, t1[:], t2[:], ALU.subtract, ['t1', 't2'] + ww, ww)
            TT(eng, t1[:], ai, br, ALU.mult, rr + ww + ['t1'], ['t1'])
            TT(eng, outi, t3[:], t1[:], ALU.add, ['t1', 't3'] + ww, ww)

        for k in range(2, 9):
            cmul('dve', Pw[:, 0, k, :], Pw[:, 1, k, :], Pw[:, 0, k - 1, :], Pw[:, 1, k - 1, :],
                 Pw[:, 0, 1, :], Pw[:, 1, 1, :], ['Pw'], ['Pw'])
        MS('pool', Qw[:, 0, 0, :], 1.0, ['Qw'])
        MS('pool', Qw[:, 1, 0, :], 0.0, ['Qw'])
        TT('dve', Qw[:, 0, 1, :], Pw[:, 0, 1, :], irho2[:], ALU.mult, ['Pw', 'irho2', 'Qw'], ['Qw'])
        STT('dve', Qw[:, 1, 1, :], Pw[:, 1, 1, :], -1.0, irho2[:], ALU.mult, ALU.mult, ['Pw', 'irho2', 'Qw'], ['Qw'])
        for k in range(2, 8):
            cmul('dve', Qw[:, 0, k, :], Qw[:, 1, k, :], Qw[:, 0, k - 1, :], Qw[:, 1, k - 1, :],
                 Qw[:, 0, 1, :], Qw[:, 1, 1, :], ['Qw'], ['Qw'])
        Pwr = A.alloc("Pwr", [64, 2, 9, 128])
        for k in range(9):
            CP('dve', Pwr[:, :, k, :], Pw[:, :, 8 - k, :], ['Pw', 'Pwr'], ['Pwr'])
        TAB = A.alloc("TAB", [64, 2, 2, 8, 128])
        CP('dve', TAB[:, 0, 0, 0, :], Pw[:, 0, 8, :], ['Pw'], ['TAB'])
        CP('dve', TAB[:, 1, 1, 0, :], Pw[:, 1, 8, :], ['Pw', 'TAB'], ['TAB'])
        for k in range(1, 8):
            cmul('dve', TAB[:, 0, 0, k, :], TAB[:, 1, 1, k, :], TAB[:, 0, 0, k - 1, :], TAB[:, 1, 1, k - 1, :],
                 Pw[:, 0, 8, :], Pw[:, 1, 8, :], ['TAB', 'Pw'], ['TAB'])
        CP('dve', TAB[:, 0, 1, :, :], TAB[:, 0, 0, :, :], ['TAB'], ['TAB'])
        TS('dve', TAB[:, 1, 0, :, :], TAB[:, 1, 1, :, :], -1.0, None, ALU.mult, None, ['TAB'], ['TAB'])
        P.dma('sp', tab_d[:, :, :, :, :], TAB[:], r=['TAB'], w=['tab_d'])
        nr = A.alloc("nr", [64, 128])
        den = A.alloc("den", [64, 128])
        TS('dve', nr[:], Pw[:, 0, 1, :], -1.0, None, ALU.add, None, ['Pw'], ['nr'])
        TT('dve', t1[:], AreT[:], AreT[:], ALU.mult, ['AreT', 't1'], ['t1'])
        TT('dve', t2[:], AimT[:], AimT[:], ALU.mult, ['AimT', 't2'], ['t2'])
        TT('dve', den[:], t1[:], t2[:], ALU.add, ['t1', 't2'], ['den'])
        RCP(den[:], den[:], ['den'], ['den'])
        TT('dve', t1[:], nr[:], AreT[:], ALU.mult, ['nr', 'AreT', 't1'], ['t1'])
        TT('dve', t2[:], Pw[:, 1, 1, :], AimT[:], ALU.mult, ['Pw', 'AimT', 't2'], ['t2'])
        TT('dve', t1[:], t1[:], t2[:], ALU.add, ['t1', 't2'], ['t1'])
        TT('dve', kap[:, 0, :], t1[:], den[:], ALU.mult, ['t1', 'den'], ['kap'])
        TT('dve', t1[:], Pw[:, 1, 1, :], AreT[:], ALU.mult, ['Pw', 'AreT', 't1'], ['t1'])
        TT('dve', t2[:], nr[:], AimT[:], ALU.mult, ['nr', 'AimT', 't2'], ['t2'])
        TT('dve', t1[:], t1[:], t2[:], ALU.subtract, ['t1', 't2'], ['t1'])
        TT('dve', kap[:, 1, :], t1[:], den[:], ALU.mult, ['t1', 'den', 'kap'], ['kap'])

        maskf = A.alloc("maskf", [128, 128])
        maskb = A.alloc("maskb", [128, 128])
        dcs = A.alloc("dcs", [128, 64])
        P.dma('sp', maskf[:], maskf_d[:, :], w=['maskf'])
        P.dma('sp', maskb[:], maskb_d[:, :], w=['maskb'])
        P.dma('sp', dcs[:], dcol[:, :], w=['dcs'])
        Bl = A.alloc("Bl", [64, 2, 16, 16])
        Bb = A.alloc("Bb", [64, 2, 16, 16])
        Cl0 = A.alloc("Cl0", [128, 2, 16, 64])
        CT = A.alloc("CT", [64, 2, 16, 16])
        Bs = A.alloc("Bs", [64, 2, 16, 8, 16])
        Cr = A.alloc("Cr", [64, 2, 16, 8, 16])
        XW = A.alloc("XW", [64, 2, 16, 8, 16], BF16)
        CLb = A.alloc("CLb", [64, 16, 2, 128], BF16)
        CLv = CLb[:].rearrange("p a r (s i) -> p a r s i", i=16)
        XMb = A.alloc("XMb", [128, 16, 2, 64], BF16)
        MGb = A.alloc("MGb", [128, 8, 128], BF16)
        tm1 = A.alloc("tm1", [128, 128])
        tm2 = A.alloc("tm2", [128, 128])
        u1 = A.alloc("u1", [64, 8, 8, 16])
        u2 = A.alloc("u2", [64, 8, 8, 16])
        u3 = A.alloc("u3", [64, 8, 8, 16])
        u4 = A.alloc("u4", [64, 8, 8, 16])
        P.dma('sp', Cl0[:, 0, :, :], cre[:, :, :], w=['Cl0'])
        P.dma('pool', Cl0[:, 1, :, :], cim[:, :, :], w=['Cl0b'])
        bre_v = bre.rearrange("a p j -> p a j")
        bim_v = bim.rearrange("a p j -> p a j")
        for gb in range(8):
            if mod_blocks is not None:
                mod_blocks(6)
            for d in range(2):
                c0 = d * 64 + gb * 8
                P.dma('sp', Bl[:, 0, d * 8:(d + 1) * 8, :], bre_v[:, c0:c0 + 8, :], w=['Bl'])
                P.dma('pool', Bl[:, 1, d * 8:(d + 1) * 8, :], bim_v[:, c0:c0 + 8, :], w=['Bl'])

            def kb(ri, d):
                return kap[:, ri, d * 64 + gb * 8:d * 64 + gb * 8 + 8][:, :, None].broadcast_to([64, 8, 16])

            ub1 = u1[:, :, 0, :]
            ub2 = u2[:, :, 0, :]
            for d in range(2):
                sl = slice(d * 8, (d + 1) * 8)
                TT('dve', ub1, Bl[:, 0, sl, :], kb(0, d), ALU.mult, ['Bl', 'kap', 'u1'], ['u1'])
                TT('dve', ub2, Bl[:, 1, sl, :], kb(1, d), ALU.mult, ['Bl', 'kap', 'u2'], ['u2'])
                TT('dve', Bb[:, 0, sl, :], ub1, ub2, ALU.subtract, ['u1', 'u2', 'Bb'], ['Bb'])
                TT('dve', ub1, Bl[:, 1, sl, :], kb(0, d), ALU.mult, ['Bl', 'kap', 'u1', 'Bb'], ['u1'])
                TT('dve', ub2, Bl[:, 0, sl, :], kb(1, d), ALU.mult, ['Bl', 'kap', 'u2', 'Bb'], ['u2'])
                TT('dve', Bb[:, 1, sl, :], ub1, ub2, ALU.add, ['u1', 'u2', 'Bb'], ['Bb'])
            for ri in range(2):
                for i4 in range(4):
                    b = nextps()
                    for ii in range(4):
                        i = i4 * 4 + ii
                        TR(ps[b][0:64, ii * 128:(ii + 1) * 128], Cl0[:, ri, i, :], ident[:, :],
                           ['Cl0', 'Cl0b'], [('ps', b)])
                    for d in range(2):
                        src = ps[b][0:64, :].rearrange("p (i c) -> p c i", i=4)[:, d * 64 + gb * 8:d * 64 + gb * 8 + 8, :]
                        CP('act', CT[:, ri, d * 8:(d + 1) * 8, i4 * 4:(i4 + 1) * 4], src, [('ps', b), 'CT'], ['CT'])

            def tabv(t, ri, ks, d):
                v = t[:, ri, ks, d * 64 + gb * 8:d * 64 + gb * 8 + 8].rearrange("p k g -> p g k")
                return v[:, :, :, None].broadcast_to([64, 8, 8, 16])

            def bcs(a):
                return a[:, :, None, :].broadcast_to([64, 8, 8, 16])

            def cm4(eng, outr, outi, ar, ai, tab, ks, d, neg_im, rr, ww):
                TT(eng, u1[:], bcs(ar), tabv(tab, 0, ks, d), ALU.mult, rr + ['u1'], ['u1'])
                TT(eng, u2[:], bcs(ai), tabv(tab, 1, ks, d), ALU.mult, rr + ['u2'], ['u2'])
                TT(eng, u3[:], bcs(ar), tabv(tab, 1, ks, d), ALU.mult, rr + ['u3'], ['u3'])
                TT(eng, u4[:], bcs(ai), tabv(tab, 0, ks, d), ALU.mult, rr + ['u4'], ['u4'])
                TT(eng, outr, u1[:], u2[:], ALU.subtract, ['u1', 'u2'] + ww, ww)
                if neg_im:
                    STT(eng, outi, u3[:], -1.0, u4[:], ALU.mult, ALU.subtract, ['u3', 'u4'] + ww, ww)
                else:
                    TT(eng, outi, u3[:], u4[:], ALU.add, ['u3', 'u4'] + ww, ww)

            asc = slice(0, 8)
            desc7 = slice(7, None, -1)
            for d in range(2):
                sl = slice(d * 8, (d + 1) * 8)
                eng = 'dve'
                eng2 = 'dve'
                cm4(eng, Bs[:, 0, sl, :, :], Bs[:, 1, sl, :, :], Bb[:, 0, sl, :], Bb[:, 1, sl, :],
                    Qw if d == 0 else Pw, asc, d, False, ['Bb', 'Pw', 'Qw'], [('Bs', d)])
                cm4(eng, Cr[:, 0, sl, :, :], Cr[:, 1, sl, :, :], CT[:, 0, sl, :], CT[:, 1, sl, :],
                    Pw if d == 0 else Qw, asc, d, True, ['CT', 'Pw', 'Qw'], [('Cr', d)])
                cm4(eng2, XW[:, 0, sl, :, :], XW[:, 1, sl, :, :], Bb[:, 0, sl, :], Bb[:, 1, sl, :],
                    Pwr if d == 0 else Pw, slice(1, 9) if d == 0 else asc, d, False, ['Bb', 'Pw', 'Pwr'], [('XW', d)])
                cm4(eng2, CLv[:, sl, 0, :, :], CLv[:, sl, 1, :, :], CT[:, 0, sl, :], CT[:, 1, sl, :],
                    Pw if d == 0 else Pwr, slice(1, 9) if d == 0 else asc, d, True, ['CT', 'Pw', 'Pwr'], [('CLb', d)])
            for d in range(2):
                P.dma('sp', cl_d[:, gb * 8:(gb + 1) * 8, d, :, :], CLb[:, d * 8:(d + 1) * 8, :, :], r=[('CLb', d)], w=[('cl_d', gb, d)])
            for dg8 in range(2):
                for ri in range(2):
                    b = nextps()
                    pb = ps[b][:, :].bitcast(BF16)
                    for q in range(8):
                        dg = dg8 * 8 + q
                        TR(pb[:, q * 64:(q + 1) * 64], XW[:, ri, dg, :, :].rearrange("p s j -> p (s j)"),
                           identb[0:64, 0:64], [('XW', 0), ('XW', 1)], [('ps', b)])
                    CP('act', XMb[:, dg8 * 8:(dg8 + 1) * 8, ri, :],
                       pb[:, 0:512].rearrange("p (q c) -> p q c", q=8), [('ps', b), 'XMb'], ['XMb'])
            for d in range(2):
                P.dma('pool', xmat_d[:, gb * 8:(gb + 1) * 8, d, :, :], XMb[:, d * 8:(d + 1) * 8, :, :], r=['XMb'], w=[('xmat_d', gb, d)])
            for g8 in range(8):
                g = gb * 8 + g8
                bf = nextps()
                for d in range(2):
                    o = ps[bf][:, d * 128:(d + 1) * 128]
                    dg = d * 8 + g8
                    MM(o, Bs[:, 0, dg, :, :].rearrange("p s j -> p (s j)"), Cr[:, 0, dg, :, :].rearrange("p s j -> p (s j)"),
                       True, False, [('Bs', d), ('Cr', d)], [('ps', bf)])
                    MM(o, Bs[:, 1, dg, :, :].rearrange("p s j -> p (s j)"), Cr[:, 1, dg, :, :].rearrange("p s j -> p (s j)"),
                       False, True, [('Bs', d), ('Cr', d)], [('ps', bf)])
                TT('dve', tm1[:], ps[bf][:, 0:128], maskf[:], ALU.mult, [('ps', bf), 'maskf', 'tm1'], ['tm1'])
                TT('dve', tm2[:], ps[bf][:, 128:256], maskb[:], ALU.mult, [('ps', bf), 'maskb', 'tm2'], ['tm2'])
                TT('dve', tm1[:], tm1[:], tm2[:], ALU.add, ['tm1', 'tm2'], ['tm1'])
                STT('dve', MGb[:, g8, :], ident[:], dcs[:, g:g + 1], tm1[:], ALU.mult, ALU.add,
                    ['ident', 'dcs', 'tm1', 'MGb'], ['MGb'])
            P.dma('sp', mg_d[:, gb * 8:(gb + 1) * 8, :], MGb[:], r=['MGb'], w=[('mg_d', gb)])

    if 'C' in stages:
        setup_ssm()
        if mod_blocks is not None:
            mod_blocks(48)
        P.barrier()
    A.reset(PERS0)

    def setup_attn():
        rT = A.alloc("rT", [31, 240])
        oh = A.alloc("oh", [31, 64, 64])
        cmk = A.alloc("cmk", [64, 64])
        T0 = A.alloc("T0", [64, 240, 64])
        zt = A.alloc("zt", [64, 16, 64])
        P.dma('sp', rT[:], rpbT[:, :], w=['rT'])
        P.dma('pool', oh[:], onehot_d[:, :, :], w=['oh'])
        P.dma('sp', cmk[:], colmask_d[:, :], w=['cmk'])
        MS('pool', zt[:], 0.0, ['zt'])
        for qc in range(64):
            b = nextps()
            MM(ps[b][0:64, 0:240], oh[:, qc, :], rT[:, :], True, True, ['oh', 'rT'], [('ps', b)])
            CP(['act', 'dve'][qc % 2], T0[:, :, qc], ps[b][0:64, 0:240], [('ps', b), 'T0'], ['T0'])
        TT('dve', T0[:], T0[:], cmk[:, None, :].broadcast_to([64, 240, 64]), ALU.add, ['T0', 'cmk'], ['T0'])
        tbv = tb_d.rearrange("h p a q -> p h a q")
        T0v = T0[:].rearrange("p (h a) q -> p h a q", h=16)
        for h in range(16):
            P.dma('pool', tbv[0:64, h, 1:16, :], T0v[:, h, :, :], r=['T0'], w=[('tb_d', h, 0)])
            P.dma('pool', tbv[64:128, h, 0:15, :], T0v[:, h, :, :], r=['T0'], w=[('tb_d', h, 1)])
        P.dma('sp', tbv[0:64, :, 0, :], zt[:], r=['zt'], w=[('tb_d', 'z0')])
        P.dma('sp', tbv[0:64, :, 16, :], zt[:], r=['zt'], w=[('tb_d', 'z1')])
        P.dma('pool', tbv[64:128, :, 15, :], zt[:], r=['zt'], w=[('tb_d', 'z2')])
        P.dma('pool', tbv[64:128, :, 16, :], zt[:], r=['zt'], w=[('tb_d', 'z3')])
        kcl = [A.alloc("kcl", [128, 1024]) for _ in range(2)]
        kcb = A.alloc("kcb", [128, 1024], BF16)
        kct = A.alloc("kct", [128, 8, 128], BF16)
        for t in range(4):
            P.dma('sp', kcl[t % 2][:], ck[t * 128:(t + 1) * 128, :], w=[('kcl', t % 2)])
            CP('dve', kcb[:], kcl[t % 2][:], [('kcl', t % 2), 'kcb'], ['kcb'])
            for hh in range(2):
                b = nextps()
                pb = ps[b][:, :].bitcast(BF16)
                for q in range(4):
                    TR(pb[:, q * 128:(q + 1) * 128], kcb[:, (hh * 4 + q) * 128:(hh * 4 + q + 1) * 128], identb[:, :],
                       ['kcb', 'identb'], [('ps', b)])
                CP('act', kct[:, hh * 4:(hh + 1) * 4, :], pb[:, 0:512].rearrange("p (q c) -> p q c", q=4),
                   [('ps', b), 'kct'], ['kct'])
            P.dma('sp', kcT_d.rearrange("(a p) n -> p a n", p=128)[:, :, t * 128:(t + 1) * 128], kct[:], r=['kct'],
                  w=[('kcT_d', t)])

    if 'T' in stages:
        setup_attn()
        P.barrier()
    A.reset(PERS)

    NT2 = 1536
    NB = 1024

    def front(hT, tiles):
        normw_bc = A.alloc("normw_bc", [128, D])
        mbc = [A.alloc("mbc", [128, D]) for _ in range(2)]
        shbc = [A.alloc("shbc", [128, D]) for _ in range(2)]
        xt = [A.alloc("xt", [128, D]) for _ in range(2)]
        junk = A.alloc("junk", [128, D])
        xm = A.alloc("xm", [128, D])
        xmb = [A.alloc("xmb", [128, D], BF16) for _ in range(2)]
        nt = len(tiles)
        ss = A.alloc("ss", [128, nt])
        rs = A.alloc("rs", [128, nt])
        P.dma('sp', normw_bc[:], norm_w.partition_broadcast(128), w=['normw_bc'])
        MS('pool', ss[:], 0.0, [('ss', t) for t in range(nt)])
        for v in range(2):
            P.dma('sp', shbc[v][:], mod_d[v:v + 1, 0:D].partition_broadcast(128), w=[('shbc', v)])
            P.dma('pool', mbc[v][:], mod_d[v:v + 1, D:2 * D].partition_broadcast(128), w=[('mbc', v)])
            STT('dve', mbc[v][:], mbc[v][:], 1.0, normw_bc[:], ALU.add, ALU.mult, [('mbc', v), 'normw_bc'], [('mbc', v)])
        for t, (src, v, c0) in enumerate(tiles):
            xi = t % 2
            P.dma('sp', xt[xi][:], src, w=[('xt', xi)])
            ACT(junk[:], xt[xi][:], AF.Square, [('xt', xi), 'junk'], ['junk', ('ss', t)], accum_out=ss[:, t:t + 1])
            ACT(rs[:, t:t + 1], ss[:, t:t + 1], AF.Sqrt, [('ss', t), 'epsc'], [('rs', t)], bias=epsc[:, 0:1], scale=1.0 / D)
            RCP(rs[:, t:t + 1], rs[:, t:t + 1], [('rs', t)], [('rs', t)])
            STT('dve', xm[:], xt[xi][:], rs[:, t:t + 1], mbc[v][:], ALU.mult, ALU.mult,
                [('xt', xi), ('rs', t), ('mbc', v), 'xm'], ['xm'])
            xb_ = xmb[t % 2]
            nxb = ('xmb', t % 2)
            TT('dve', xb_[:], xm[:], shbc[v][:], ALU.add, ['xm', ('shbc', v), nxb], [nxb])
            for k4 in range(4):
                b = nextps()
                pb = ps[b][:, :].bitcast(BF16)
                for kk in range(4):
                    k = k4 * 4 + kk
                    TR(pb[:, kk * 128:(kk + 1) * 128], xb_[:, k * 128:(k + 1) * 128], identb[:, :], [nxb], [('ps', b)])
                CP('act', hT[:, k4 * 4:(k4 + 1) * 4, c0:c0 + 128], pb[:, 0:512].rearrange("p (a b) -> p a b", a=4),
                   [('ps', b)], [('hT', t, k4)])

    w_in_v = w_in.rearrange("(k p) n -> p k n", p=128)
    wc = [0]

    def mk_loader(WD=512):
        Wst = [A.alloc("Wst", [128, 16, WD]) for _ in range(2)]
        Wbf = [A.alloc("Wbf", [128, 16, WD], BF16) for _ in range(2)]

        def load_w(c0):
            i = wc[0] % 2
            wc[0] += 1
            for kh in range(2):
                P.dma('sp', Wst[i][:, kh * 8:(kh + 1) * 8, :], w_in_v[:, kh * 8:(kh + 1) * 8, c0:c0 + WD],
                      w=[('Wst', i, kh)])
                CP(['dve', 'act'][kh], Wbf[i][:, kh * 8:(kh + 1) * 8, :], Wst[i][:, kh * 8:(kh + 1) * 8, :],
                   [('Wst', i, kh)], [('Wbf', i, kh)])
            return i
        return Wbf, load_w

    def u_proj(hT, grps, Wbf, load_w, WD=512):
        Ubuf = A.alloc("Ubuf", [128, 64, 8, 16], BF16)
        for (t0, ncnk, cofs) in grps:
            ng = WD // 16
            for cb in range(1024 // WD):
                wi = load_w(cb * WD)
                for s_ in range(8):
                    b = nextps()
                    for k in range(16):
                        MM(ps[b][0:ncnk, 0:WD], hT[:, k, t0 + s_:t0 + 8 * ncnk:8], Wbf[wi][:, k, :], k == 0, k == 15,
                           [('Wbf', wi, k // 8)], [('ps', b)])
                    CP(['act', 'dve'][s_ % 2], Ubuf[0:ncnk, cb * ng:(cb + 1) * ng, s_, :],
                       ps[b][0:ncnk, 0:WD].rearrange("p (g j) -> p g j", g=ng), [('ps', b), 'Ubuf'], ['Ubuf'])
            for g4 in range(16):
                b = nextps()
                pb = ps[b][:, :].bitcast(BF16)
                for q in range(4):
                    g = g4 * 4 + q
                    TR(pb[:, q * 128:q * 128 + ncnk], Ubuf[0:ncnk, g, :, :].rearrange("p s j -> p (s j)"),
                       identb[0:ncnk, 0:ncnk], ['Ubuf'], [('ps', b)])
                CP(['act', 'dve'][g4 % 2], U2[:, g4 * 4:(g4 + 1) * 4, cofs:cofs + ncnk],
                   pb[:, 0:512].rearrange("p (q c) -> p q c", q=4)[:, :, 0:ncnk], [('ps', b), 'U2'], ['U2'])

    def projections(hT, Wbf, load_w):
        qnw_bc = A.alloc("qnw_bc", [128, 64])
        knw_bc = A.alloc("knw_bc", [128, 64])
        P.dma('sp', qnw_bc[:], qnw.partition_broadcast(128), w=['qnw_bc'])
        P.dma('sp', knw_bc[:], knw.partition_broadcast(128), w=['knw_bc'])
        tok = [A.alloc("tok", [128, 512]) for _ in range(2)]
        tokb = [A.alloc("tokb", [128, 512], BF16) for _ in range(2)]
        tokT = [A.alloc("tokT", [128, 4, 128], BF16) for _ in range(2)]
        sq = A.alloc("sq", [128, 512])
        ms = A.alloc("ms", [128, 8])
        tc_ = [0]
        for cb in range(6):
            c0 = 2048 + cb * 512
            kind = cb // 2
            o0 = (cb % 2) * 512
            wi = load_w(c0)
            tl = [0, 1, 2, 3, 8, 9, 10, 11] if kind == 0 else list(range(12))
            for t in tl:
                b = nextps()
                ti = tc_[0] % 2
                tc_[0] += 1
                for k in range(16):
                    MM(ps[b][:, :], hT[:, k, t * 128:(t + 1) * 128], Wbf[wi][:, k, :], k == 0, k == 15,
                       [('Wbf', wi, k // 8)], [('ps', b)])
                CP('act', tok[ti][:], ps[b][:, :], [('ps', b), ('tok', ti)], [('tok', ti)])
                if kind == 2:
                    P.dma('pool', v_d[t * 128:(t + 1) * 128, o0:o0 + 512], tok[ti][:], r=[('tok', ti)], w=[('v_d', t, cb)])
                    if t >= 8:
                        P.dma('pool', vo[(t - 8) * 128:(t - 7) * 128, o0:o0 + 512], tok[ti][:], r=[('tok', ti)],
                              w=[('vo', t, cb)])
                    continue
                nw = qnw_bc if kind == 0 else knw_bc
                t3 = tok[ti][:].rearrange("p (h d) -> p h d", h=8)
                TT('dve', sq[:], tok[ti][:], tok[ti][:], ALU.mult, [('tok', ti), 'sq'], ['sq'])
                P.add('dve', lambda e: e.tensor_reduce(out=ms[:], in_=sq[:].rearrange("p (h d) -> p h d", h=8),
                                                       axis=AX.X, op=ALU.add), ['sq', 'ms'], ['ms'])
                ACT(ms[:], ms[:], AF.Sqrt, ['ms', 'epsc'], ['ms'], bias=epsc[:, 0:1], scale=1.0 / 64)
                RCP(ms[:], ms[:], ['ms'], ['ms'])
                TT('dve', t3, t3, ms[:, :, None].broadcast_to([128, 8, 64]), ALU.mult, [('tok', ti), 'ms'], [('tok', ti)])
                TT('dve', t3, t3, nw[:, None, :].broadcast_to([128, 8, 64]), ALU.mult, [('tok', ti), 'qnw_bc', 'knw_bc'],
                   [('tok', ti)])
                if kind == 1 and t >= 8:
                    P.dma('pool', ko[(t - 8) * 128:(t - 7) * 128, o0:o0 + 512], tok[ti][:], r=[('tok', ti)], w=[('ko', t, cb)])
                CP('act', tokb[ti][:], tok[ti][:], [('tok', ti), ('tokb', ti)], [('tokb', ti)])
                b2 = nextps()
                pb = ps[b2][:, :].bitcast(BF16)
                for q in range(4):
                    TR(pb[:, q * 128:(q + 1) * 128], tokb[ti][:, q * 128:(q + 1) * 128], identb[:, :], [('tokb', ti)],
                       [('ps', b2)])
                CP('dve', tokT[ti][:], pb[:, 0:512].rearrange("p (q c) -> p q c", q=4), [('ps', b2), ('tokT', ti)], [('tokT', ti)])
                if kind == 0:
                    tq = t if t < 4 else t - 4
                    dsl = qT_d[o0:o0 + 512, tq * 128:(tq + 1) * 128]
                else:
                    dsl = kT_d[o0:o0 + 512, t * 128:(t + 1) * 128]
                P.dma('pool', dsl.rearrange("(q p) n -> p q n", p=128), tokT[ti][:],
                      r=[('tokT', ti)], w=[('qkT', kind, t, cb)])
        fm = [A.alloc("fm", [128, 512]) for _ in range(2)]
        fc_ = [0]
        specs = [(1024, 2, zs_d, AF.Silu), (5120, 2, za_d, AF.Silu), (6144, 4, gs_d, AF.Sigmoid), (8192, 4, ga_d, AF.Sigmoid)]
        for (cbase, nblk, dst, fn) in specs:
            for cb in range(nblk):
                wi = load_w(cbase + cb * 512)
                for f2 in range(4):
                    for tb, hoff in enumerate((0, 1024)):
                        b = nextps()
                        fi = fc_[0] % 2
                        fc_[0] += 1
                        for k in range(16):
                            MM(ps[b][:, :], Wbf[wi][:, k, f2 * 128:(f2 + 1) * 128], hT[:, k, hoff:hoff + 512],
                               k == 0, k == 15, [('Wbf', wi, k // 8)], [('ps', b)])
                        ACT(fm[fi][:], ps[b][:, :], fn, [('ps', b), ('fm', fi)], [('fm', fi)])
                        r0 = cb * 512 + f2 * 128
                        P.dma('pool', dst[r0:r0 + 128, tb * 512:(tb + 1) * 512], fm[fi][:], r=[('fm', fi)],
                              w=[('fmd', r0, tb, cbase)])

    if 'B' in stages:
        hT1 = A.alloc("hT1", [128, 16, NS], BF16)
        m1_ = A.mark()
        front(hT1, [(xs[t * 128:(t + 1) * 128, :], 1, t * 128) for t in range(16)])
        P.barrier()
        A.reset(m1_)
        Wbf, load_w = mk_loader(256)
        u_proj(hT1, [(0, 128, 0), (1024, 128, 128)], Wbf, load_w, 256)
        P.barrier()
        A.reset(PERS)
        hT2 = A.alloc("hT2", [128, 16, NT2], BF16)
        m2_ = A.mark()
        tiles = [(xown[t * 128:(t + 1) * 128, :], 1, t * 128) for t in range(4)]
        tiles += [(xhalo[t * 128:(t + 1) * 128, :], 1, 512 + t * 128) for t in range(4)]
        tiles += [(xp[t * 128:(t + 1) * 128, :], 0, 1024 + t * 128) for t in range(4)]
        front(hT2, tiles)
        P.barrier()
        A.reset(m2_)
        Wbf, load_w = mk_loader()
        m3_ = A.mark()
        u_proj(hT2, [(1024, 64, 256)], Wbf, load_w)
        P.barrier()
        A.reset(m3_)
        projections(hT2, Wbf, load_w)
        P.barrier()
    A.reset(PERS)

    def ssm_main():
        Xb2 = [A.alloc("Xb", [128, 2, 8, NCH]) for _ in range(2)]
        HP2 = [A.alloc("HP", [128, 2, 8, NCH], BF16) for _ in range(2)]
        xmp = A.alloc("xmp", [128, 8, 2, 2, 128], BF16)
        clb2 = [A.alloc("clb", [128, 4, 2, 2, 128], BF16) for _ in range(2)]
        mgb = [A.alloc("mgb", [128, 8, 128], BF16) for _ in range(2)]
        TABs = A.alloc("TABs", [64, 2, 2, 8, 128])
        P.dma('sp', TABs[:], tab_d[:, :, :, :, :], w=['TABs'])
        MS('pool', xmp[:], 0.0, ['xmp'])
        TABb = A.alloc("TABb", [128, 2, 2, 8, 2, 4])
        TAB2b = A.alloc("TAB2b", [128, 2, 2, 5, 2, 4])
        h0b = A.alloc("h0b", [128, 2, 2, 4])
        W1 = [A.alloc("W1", [128, 2, 4, 40]) for _ in range(2)]
        W2 = [A.alloc("W2", [128, 2, 4, 40]) for _ in range(2)]
        HS = [A.alloc("HS", [128, 2, 4, 40]) for _ in range(2)]
        s1p = [A.alloc("s1p", [128, 2, 4, 2, 4]) for _ in range(2)]
        s2p = [A.alloc("s2p", [128, 2, 4, 2, 4]) for _ in range(2)]
        TAB2 = A.alloc("TAB2", [64, 2, 2, 5, 128])
        q1 = A.alloc("q1", [64, 128])
        q2 = A.alloc("q2", [64, 128])
        CP('dve', TAB2[:, 0, 0, 0, :], TABs[:, 0, 0, 7, :], ['TABs'], ['TAB2'])
        CP('dve', TAB2[:, 1, 1, 0, :], TABs[:, 1, 1, 7, :], ['TABs', 'TAB2'], ['TAB2'])
        for j in range(1, 5):
            pr_, pi_ = TAB2[:, 0, 0, j - 1, :], TAB2[:, 1, 1, j - 1, :]
            TT('dve', q1[:], pr_, pr_, ALU.mult, ['TAB2', 'q1'], ['q1'])
            TT('dve', q2[:], pi_, pi_, ALU.mult, ['TAB2', 'q2'], ['q2'])
            TT('dve', TAB2[:, 0, 0, j, :], q1[:], q2[:], ALU.subtract, ['q1', 'q2', 'TAB2'], ['TAB2'])
            TT('dve', q1[:], pr_, pi_, ALU.mult, ['TAB2', 'q1'], ['q1'])
            TS('dve', TAB2[:, 1, 1, j, :], q1[:], 2.0, None, ALU.mult, None, ['q1', 'TAB2'], ['TAB2'])
        CP('dve', TAB2[:, 0, 1, :, :], TAB2[:, 0, 0, :, :], ['TAB2'], ['TAB2'])
        TS('dve', TAB2[:, 1, 0, :, :], TAB2[:, 1, 1, :, :], -1.0, None, ALU.mult, None, ['TAB2'], ['TAB2'])
        Ysb = [A.alloc("Ysb", [128, NCH]) for _ in range(2)]
        Ytok = A.alloc("Ytok", [128, 3, 8, 128])
        g1 = A.alloc("g1", [128, 1024])
        Y2 = A.alloc("Y2", [128, 1024])
        selT = A.alloc("selT", [128, 3, 128])
        P.dma('sp', selT[:], sel3[:, :, :], w=['selT'])
        ysT = [A.alloc("ysT", [128, 128, 8]) for _ in range(2)]
        pieces = [(0, 128), (128, 128), (256, 64)]
        yc = [0]
        MS('pool', Ytok[:], 0.0, ['Ytok'])
        TABsv = TABs[:].rearrange("p a r k (d g) -> p a r k d g", d=2)
        TAB2v = TAB2[:].rearrange("p a r j (d g) -> p a r j d g", d=2)
        H0v = H0[:].rearrange("p r (d g) -> p r d g", d=2)
        rngs = [(0, 256, slice(255, None, -1)), (256, 288, slice(287, 255, -1)), (288, 320, slice(319, 287, -1))]
        def p_load(gb):
            bi = gb % 2
            XB, HPb = Xb2[bi], HP2[bi]
            nX, nH = ('Xb', bi), ('HP', bi)
            XB4 = XB[:].rearrange("p r (d g) c -> p r d g c", d=2)
            XB5 = XB[:].rearrange("p r (d g) (k s) -> p r d g s k", d=2, k=8)
            nXall = [nX, ('Xbd', bi, 0), ('Xbd', bi, 1)]
            for gh in range(2):
                g0 = gb * 8 + gh * 4
                ps_ = slice(gh * 64, (gh + 1) * 64)
                P.dma('sp', xmp[:, gh * 4:(gh + 1) * 4, :, :, gh * 64:(gh + 1) * 64], xmat_d[:, g0:g0 + 4, :, :, :], w=['xmp'])
                P.dma('sp', clb2[bi][ps_, :, :, :, :], cl_d[:, g0:g0 + 4, :, :, :], w=[('clb', bi)])
            P.dma('sp', mgb[bi][:], mg_d[:, gb * 8:(gb + 1) * 8, :], w=[('mgb', bi)])

        def p_xc(gb):
            bi = gb % 2
            XB, HPb = Xb2[bi], HP2[bi]
            nX, nH = ('Xb', bi), ('HP', bi)
            XB4 = XB[:].rearrange("p r (d g) c -> p r d g c", d=2)
            XB5 = XB[:].rearrange("p r (d g) (k s) -> p r d g s k", d=2, k=8)
            nXall = [nX, ('Xbd', bi, 0), ('Xbd', bi, 1)]
            for g4 in range(4):
                for d in range(2):
                    for ri in range(2):
                        b = nextps()
                        for (c0_, c1_, rv) in ([(0, NCH, slice(0, NCH))] if d == 0 else rngs):
                            for gh in range(2):
                                g = gb * 8 + gh * 4 + g4
                                lhs = xmp[:, gh * 4 + g4, d, ri, :]
                                MM(ps[b][:, c0_:c1_], lhs, U2[:, g, rv], gh == 0, gh == 1, ['xmp'], [('ps', b)])
                        CP('act', XB[:, ri, d * 4 + g4, :].rearrange("p (k s) -> p s k", k=8),
                           ps[b][:, 0:NCH].rearrange("p (s k) -> p s k", k=8), [('ps', b), nX, ('Xbd', bi, 0), ('Xbd', bi, 1)], [nX])

        def p_rec(gb):
            bi = gb % 2
            XB, HPb = Xb2[bi], HP2[bi]
            nX, nH = ('Xb', bi), ('HP', bi)
            XB4 = XB[:].rearrange("p r (d g) c -> p r d g c", d=2)
            XB5 = XB[:].rearrange("p r (d g) (k s) -> p r d g s k", d=2, k=8)
            nXall = [nX, ('Xbd', bi, 0), ('Xbd', bi, 1)]
            for gh in range(2):
                g0 = gb * 8 + gh * 4
                ps_ = slice(gh * 64, (gh + 1) * 64)
                for a in range(2):
                    for ri in range(2):
                        CP('dve', TABb[ps_, a, ri], TABsv[:, a, ri, :, :, g0:g0 + 4], ['TABs', 'TABb'], ['TABb'])
                        CP('dve', TAB2b[ps_, a, ri], TAB2v[:, a, ri, :, :, g0:g0 + 4], ['TAB2', 'TAB2b'], ['TAB2b'])
                CP('dve', h0b[ps_], H0v[:, :, :, g0:g0 + 4], ['h0b'], ['h0b'])
            eng = 'dve'

            def ctx(d):
                return (('Xbd', bi, d), W1[d], W2[d], HS[d], ('W1', d), ('W2', d), ('HS', d), ('S1', d), ('S2', d))

            def ak(a, k, n, d):
                return TABb[:, a, :, k, d, :][:, :, :, None].broadcast_to([128, 2, 4, n])

            def aj(a, shape, j, d):
                v = TAB2b[:, a, :, j, d, :]
                for _ in range(len(shape) - 3):
                    v = v.unsqueeze(len(v.shape))
                return v.broadcast_to(shape)
            for k in range(1, 8):
                for d in range(2):
                    nXd, w1, w2, hs, nW1, nW2, nHs, nS1, nS2 = ctx(d)
                    prev = XB5[:, :, d, :, :, k - 1]
                    TT(eng, w1[:], prev, ak(0, 0, 40, d), ALU.mult, [nX, nXd, nW1, 'TABb'], [nW1])
                    SWP(eng, w2[:], prev, ak(1, 0, 40, d), [nX, nXd, nW2, 'TABb'], [nW2])
                for d in range(2):
                    nXd, w1, w2, hs, nW1, nW2, nHs, nS1, nS2 = ctx(d)
                    cur = XB5[:, :, d, :, :, k]
                    TT(eng, cur, cur, w1[:], ALU.add, [nX, nXd, nW1], [nXd])
                    TT(eng, cur, cur, w2[:], ALU.add, [nX, nXd, nW2], [nXd])
            for d in range(2):
                nXd, w1, w2, hs, nW1, nW2, nHs, nS1, nS2 = ctx(d)
                P.add(eng, lambda e, hs=hs: e.memset(hs[:, :, :, 32:40:4], 0.0), [nHs], [nHs])
                CP(eng, hs[:, :, :, 0], h0b[:, :, d, :], [nHs, 'h0b'], [nHs])
                CP(eng, hs[:, :, :, 1:32], XB5[:, :, d, :, 0:31, 7], [nX, nXd, nHs], [nHs])
                hp = hs[:, :, :, 32:40].rearrange("p r g (s k) -> p r g s k", k=4)
                xpv = XB5[:, :, d, :, 32:40, 7].rearrange("p r g (s k) -> p r g s k", k=4)
                for ri in range(2):
                    CP(eng, hp[:, ri, :, :, 1:4], xpv[:, ri, :, :, 0:3], [nX, nXd, nHs], [nHs])
            for j in range(5):
                o = 1 << j
                n = 32 - o
                for d in range(2):
                    nXd, w1, w2, hs, nW1, nW2, nHs, nS1, nS2 = ctx(d)
                    src = hs[:, :, :, 0:n]
                    TT(eng, w1[:, :, :, 0:n], src, aj(0, [128, 2, 4, n], j, d), ALU.mult, [nHs, nW1, 'TAB2b'], [nW1])
                    SWP(eng, w2[:, :, :, 0:n], src, aj(1, [128, 2, 4, n], j, d), [nHs, nW2, 'TAB2b'], [nW2])
                for d in range(2):
                    nXd, w1, w2, hs, nW1, nW2, nHs, nS1, nS2 = ctx(d)
                    dst = hs[:, :, :, o:32]
                    TT(eng, dst, dst, w1[:, :, :, 0:n], ALU.add, [nHs, nW1], [nHs])
                    TT(eng, dst, dst, w2[:, :, :, 0:n], ALU.add, [nHs, nW2], [nHs])
                if o < 4:
                    m = 4 - o
                    for d in range(2):
                        nXd, w1, w2, hs, nW1, nW2, nHs, nS1, nS2 = ctx(d)
                        hp = hs[:, :, :, 32:40].rearrange("p r g (s k) -> p r g s k", k=4)
                        srcp = hp[:, :, :, :, 0:m]
                        dstp = hp[:, :, :, :, o:4]
                        s1v = s1p[d][:, :, :, :, 0:m]
                        s2v = s2p[d][:, :, :, :, 0:m]
                        a0 = aj(0, [128, 2, 4, 2, m], j, d)
                        for ri in range(2):
                            TT(eng, s1v[:, ri], srcp[:, ri], a0[:, ri], ALU.mult, [nHs, nS1, 'TAB2b'], [nS1])
                        SWP(eng, s2v, srcp, aj(1, [128, 2, 4, 2, m], j, d), [nHs, nS2, 'TAB2b'], [nS2])
                        for ri in range(2):
                            TT(eng, dstp[:, ri], dstp[:, ri], s1v[:, ri], ALU.add, [nHs, nS1], [nHs])
                            TT(eng, dstp[:, ri], dstp[:, ri], s2v[:, ri], ALU.add, [nHs, nS2], [nHs])
            for k in range(8):
                for d in range(2):
                    nXd, w1, w2, hs, nW1, nW2, nHs, nS1, nS2 = ctx(d)
                    TT(eng, w1[:], hs[:], ak(0, k, 40, d), ALU.mult, [nHs, nW1, 'TABb'], [nW1])
                    SWP(eng, w2[:], hs[:], ak(1, k, 40, d), [nHs, nW2, 'TABb'], [nW2])
                for d in range(2):
                    nXd, w1, w2, hs, nW1, nW2, nHs, nS1, nS2 = ctx(d)
                    cur = XB5[:, :, d, :, :, k]
                    TT(eng, cur, cur, w1[:], ALU.add, [nX, nXd, nW1], [nXd])
                    TT(eng, cur, cur, w2[:], ALU.add, [nX, nXd, nW2], [nXd])

        def p_hp(gb):
            bi = gb % 2
            XB, HPb = Xb2[bi], HP2[bi]
            nX, nH = ('Xb', bi), ('HP', bi)
            XB4 = XB[:].rearrange("p r (d g) c -> p r d g c", d=2)
            XB5 = XB[:].rearrange("p r (d g) (k s) -> p r d g s k", d=2, k=8)
            nXall = [nX, ('Xbd', bi, 0), ('Xbd', bi, 1)]
            for sq_ in range(2):
                CP('act', Hl2[:, :, sq_, :, gb, :], XB5[:, :, :, :, 35 + 4 * sq_, 7], nXall + ['Hl2'], ['Hl2'])
            HP4 = HPb[:].rearrange("p r (d g) c -> p r d g c", d=2)
            XBs = XB[:].rearrange("p r q (k s) -> p r q s k", k=8)
            for ri in range(2):
                CP('act', HPb[:, ri, :, 1:257].rearrange("p q (s k) -> p q s k", k=8), XBs[:, ri, :, 0:32, :], nXall + [nH], [nH])
                CP('act', HPb[:, ri, :, 257:289].rearrange("p q (s k) -> p q s k", k=8), XBs[:, ri, :, 32:36, :], nXall + [nH], [nH])
                CP('act', HPb[:, ri, :, 289:313].rearrange("p q (s k) -> p q s k", k=8), XBs[:, ri, :, 36:39, :], nXall + [nH], [nH])
                CP('act', HPb[:, ri, :, 313:320], XBs[:, ri, :, 39, 0:7], nXall + [nH], [nH])
            CP('act', HP4[:, :, :, :, 0], h0b[:], [nH, 'h0b'], [nH])
            P.add('pool', lambda e, HPb=HPb: e.memset(HPb[:, :, :, 256:289:32], 0.0), [nH], [nH])

        def p_y(gb):
            bi = gb % 2
            XB, HPb = Xb2[bi], HP2[bi]
            nX, nH = ('Xb', bi), ('HP', bi)
            XB4 = XB[:].rearrange("p r (d g) c -> p r d g c", d=2)
            XB5 = XB[:].rearrange("p r (d g) (k s) -> p r d g s k", d=2, k=8)
            nXall = [nX, ('Xbd', bi, 0), ('Xbd', bi, 1)]
            for g8 in range(8):
                g = gb * 8 + g8
                gh, g4 = g8 // 4, g8 % 4
                ps_ = slice(gh * 64, (gh + 1) * 64)
                b = nextps()
                yi = yc[0] % 2
                yc[0] += 1
                o = ps[b]
                MM(o[:, 0:NCH], mgb[bi][:, g8, :], U2[:, g, :], True, False, [('mgb', bi)], [('ps', b)])
                for ri in range(2):
                    MM(o[:, 0:NCH], clb2[bi][ps_, g4, 0, ri, :], HPb[ps_, ri, g4, :], False, False, [nH, ('clb', bi)], [('ps', b)])
                for ri in range(2):
                    l = clb2[bi][ps_, g4, 1, ri, :]
                    for qi, (c0_, c1_, rv) in enumerate(rngs):
                        MM(o[:, c0_:c1_], l, HPb[ps_, ri, 4 + g4, rv], False, (ri == 1 and qi == 2), [nH, ('clb', bi)], [('ps', b)])
                CP('act', Ysb[yi][:], o[:, 0:NCH], [('ps', b), ('Ysb', yi)], [('Ysb', yi)])
                b2 = nextps()
                for pi, (c0, n) in enumerate(pieces):
                    TR(ps[b2][0:n, pi * 128:(pi + 1) * 128], Ysb[yi][:, c0:c0 + n], ident[:, :], [('Ysb', yi)], [('ps', b2)])
                for pi, (c0, n) in enumerate(pieces):
                    CP('act', Ytok[0:n, pi, :, g8 * 16:(g8 + 1) * 16],
                       ps[b2][0:n, pi * 128:(pi + 1) * 128].rearrange("p (r i) -> p r i", r=8), [('ps', b2), 'Ytok'], ['Ytok'])

        def p_tail(gb):
            bi = gb % 2
            XB, HPb = Xb2[bi], HP2[bi]
            nX, nH = ('Xb', bi), ('HP', bi)
            XB4 = XB[:].rearrange("p r (d g) c -> p r d g c", d=2)
            XB5 = XB[:].rearrange("p r (d g) (k s) -> p r d g s k", d=2, k=8)
            nXall = [nX, ('Xbd', bi, 0), ('Xbd', bi, 1)]
            Yf = Ytok[:].rearrange("p a r c -> p a (r c)")
            bs = [nextps(), nextps()]
            for hh in range(2):
                for pi in range(3):
                    MM(ps[bs[hh]][:, :], selT[:, pi, :], Yf[:, pi, hh * 512:(hh + 1) * 512], pi == 0, pi == 2,
                       ['Ytok', 'selT'], [('ps', bs[hh])])
                CP('act', Y2[:, hh * 512:(hh + 1) * 512], ps[bs[hh]][:, :], [('ps', bs[hh]), 'Y2'], ['Y2'])
            TT('dve', g1[:], Y2[:], Y2[:], ALU.mult, ['Y2', 'g1'], ['g1'])
            TS('dve', g1[:], g1[:], 0.044715, 1.0, ALU.mult, ALU.add, ['g1'], ['g1'])
            TT('dve', g1[:], g1[:], Y2[:], ALU.mult, ['g1', 'Y2'], ['g1'])
            ACT(g1[:], g1[:], AF.Sigmoid, ['g1'], ['g1'], scale=GELU_C)
            TT('dve', g1[:], g1[:], Y2[:], ALU.mult, ['Y2', 'g1'], ['g1'])
            g1v = g1[:].rearrange("p (r c) -> p r c", r=8)
            yti = gb % 2
            for r4 in range(2):
                b3 = nextps()
                for q in range(4):
                    TR(ps[b3][:, q * 128:(q + 1) * 128], g1v[:, r4 * 4 + q, :], ident[:, :], ['g1'], [('ps', b3)])
                CP('act', ysT[yti][:, :, r4 * 4:(r4 + 1) * 4].rearrange("p c r -> p r c"),
                   ps[b3][:, :].rearrange("p (q c) -> p q c", q=4), [('ps', b3), ('ysT', yti)], [('ysT', yti)])
            P.dma('pool', ysg_d[gb * 128:(gb + 1) * 128, :], ysT[yti][:].rearrange("p c r -> p (c r)"),
                  r=[('ysT', yti)], w=[('ysg_d', gb)])

        p_load(0)
        p_xc(0)
        p_rec(0)
        for gb in range(8):
            if gb + 1 < 8:
                p_load(gb + 1)
                p_xc(gb + 1)
            p_hp(gb)
            p_y(gb)
            if gb + 1 < 8:
                p_rec(gb + 1)
            p_tail(gb)
        st = A.alloc("st", [64, 128])
        for ri, dst in ((0, sre_o), (1, sim_o)):
            for sq_ in range(2):
                b = nextps()
                TR(ps[b][0:64, 0:128], Hl2[:, ri, sq_].rearrange("p d b g -> p (d b g)"), ident[:, :], ['Hl2'], [('ps', b)])
                CP('dve', st[:], ps[b][0:64, 0:128], [('ps', b), 'st'], ['st'])
                P.dma('pool', dst[sq_, :, :], st[:], r=['st'], w=[('so', ri, sq_)])

    if 'D' in stages:
        ssm_main()
        P.barrier()
    A.reset(PERS0)

    wback = {}

    def attention():
        Wg = A.alloc("Wg", [128, 8, 1024], BF16)
        Wso = A.alloc("Wso", [128, 8, D], BF16)
        Wao = A.alloc("Wao", [128, 8, D], BF16)
        wst = [A.alloc("wst", [128, 4, 512]) for _ in range(2)]
        wback.update(Wg=Wg, Wso=Wso, Wao=Wao, mark=A.mark())
        chunks = []
        for (src, nk, ncol, dstw, nm) in ((w_glu, 8, 1024, Wg, 'Wg'), (w_so, 8, D, Wso, 'Wso'), (w_ao, 8, D, Wao, 'Wao')):
            v = src.rearrange("(k p) n -> p k n", p=128)
            for k4 in range(nk // 4):
                for c in range(ncol // 512):
                    chunks.append((v, k4, c, dstw, nm))
        wn = [0]

        def wload(n):
            for _ in range(n):
                if wn[0] >= len(chunks):
                    return
                v, k4, c, dstw, nm = chunks[wn[0]]
                i = wn[0] % 2
                wn[0] += 1
                P.dma('sp', wst[i][:], v[:, k4 * 4:(k4 + 1) * 4, c * 512:(c + 1) * 512], w=[('wst', i)])
                CP(['dve', 'act'][i], dstw[:, k4 * 4:(k4 + 1) * 4, c * 512:(c + 1) * 512], wst[i][:], [('wst', i), nm], [nm])
        pair_tile = [4, 5, 0, 1, 2, 3, 6, 7]
        V1 = A.alloc("V1", [128, 12, 16, 65], BF16)
        V1c = A.alloc("V1c", [128, 4, 16, 65], BF16)
        vl = [A.alloc("vl", [128, 1024]) for _ in range(4)]
        rmf = A.alloc("rmf", [128, 8, 8])
        rmb = A.alloc("rmb", [128, 8, 8], BF16)
        P.dma('sp', rmf[:], rmask[:, :, :], w=['rmf'])
        CP('dve', rmb[:], rmf[:], ['rmf'], ['rmb'])
        MS('pool', V1[:], 1.0, ['V1'])
        MS('pool', V1c[:], 1.0, ['V1c'])
        for t in range(16):
            src = v_d[t * 128:(t + 1) * 128, :] if t < 12 else cv[(t - 12) * 128:(t - 11) * 128, :]
            P.dma('sp', vl[t % 4][:], src, w=[('vl', t % 4)])
            dstv = V1[:, t, :, 0:64] if t < 12 else V1c[:, t - 12, :, 0:64]
            CP(['dve', 'act'][t % 2], dstv, vl[t % 4][:].rearrange("p (h d) -> p h d", h=16), [('vl', t % 4), 'V1', 'V1c'], ['V1', 'V1c'])
        qT = [A.alloc("qT", [64, 1024], BF16) for _ in range(2)]
        kT = [A.alloc("kT", [64, 1536], BF16) for _ in range(2)]
        kcT = [A.alloc("kcT", [64, 512], BF16) for _ in range(2)]
        TB = [A.alloc("TB", [128, 17, 64]) for _ in range(2)]
        Pc = A.alloc("Pc", [128, 4, 512], BF16)
        Pw_ = [A.alloc("Pw_", [128, 6, 64], BF16) for _ in range(2)]
        Sw = [A.alloc("Sw", [128, 6, 64]) for _ in range(2)]
        Pp = [A.alloc("Pp", [128, 2, 256], BF16) for _ in range(2)]
        Ah = A.alloc("Ah", [128, 2, 64])
        Ar = A.alloc("Ar", [64, 8, 64])
        rc = A.alloc("rc", [128, 1])
        aT = [A.alloc("aT", [64, 512]) for _ in range(2)]
        cnt = [0]
        ac = [0]
        for h in range(16):
            hi = h % 2
            P.dma('sp', qT[hi][:], qT_d[h * 64:(h + 1) * 64, :], w=[('qT', hi)])
            P.dma('sp', kT[hi][:], kT_d[h * 64:(h + 1) * 64, :], w=[('kT', hi)])
            P.dma('sp', kcT[hi][:], kcT_d[h * 64:(h + 1) * 64, :], w=[('kcT', hi)])
            P.dma('sp', TB[hi][:], tb_d[h, :, :, :], w=[('TB', hi)])
            wload(2)
            for ct in range(4):
                b = nextps()
                MM(ps[b][:, :], kcT[hi][:, ct * 128:(ct + 1) * 128], qT[hi][:, 0:512], True, True,
                   [('kcT', hi), ('qT', hi)], [('ps', b)])
                ACT(Pc[:, ct, :], ps[b][:, :], AF.Exp, [('ps', b), 'Pc'], ['Pc'], scale=0.125)
            for ql in range(8):
                j0 = min(ql, 4)
                j1 = max(ql, 4) + 7
                p0 = j0 // 2
                p1 = j1 // 2
                npair = p1 - p0 + 1
                wi_ = cnt[0] % 2
                cnt[0] += 1
                b = nextps()
                for pi in range(npair):
                    tl = pair_tile[p0 + pi]
                    MM(ps[b][:, pi * 64:(pi + 1) * 64], kT[hi][:, tl * 128:(tl + 1) * 128], qT[hi][:, ql * 64:(ql + 1) * 64],
                       True, True, [('kT', hi), ('qT', hi)], [('ps', b)])
                i0 = 2 * p0 - ql + 3 + 1
                STT('dve', Sw[wi_][:, 0:npair, :], ps[b][:, 0:npair * 64].rearrange("p (a q) -> p a q", q=64), 0.125,
                    TB[hi][:, i0:i0 + 2 * npair:2, :], ALU.mult, ALU.add, [('ps', b), ('TB', hi), ('Sw', wi_)], [('Sw', wi_)])
                ACT(Pw_[wi_][:, 0:npair, :], Sw[wi_][:, 0:npair, :], AF.Exp, [('Sw', wi_), ('Pw_', wi_)], [('Pw_', wi_)])
                TT('dve', Pw_[wi_][:, 0:npair, :], Pw_[wi_][:, 0:npair, :],
                   rmb[:, ql, p0:p0 + npair][:, :, None].broadcast_to([128, npair, 64]), ALU.mult,
                   [('Pw_', wi_), 'rmb'], [('Pw_', wi_)])
                b2 = nextps()
                for pi in range(npair):
                    MM(ps[b2][0:64, 0:65], Pw_[wi_][:, pi, :], V1[:, pair_tile[p0 + pi], h, :], pi == 0, False,
                       [('Pw_', wi_), 'V1'], [('ps', b2)])
                for ct in range(4):
                    MM(ps[b2][0:64, 0:65], Pc[:, ct, ql * 64:(ql + 1) * 64], V1c[:, ct, h, :], False, ct == 3,
                       ['Pc', 'V1c'], [('ps', b2)])
                RCP(rc[0:64, :], ps[b2][0:64, 64:65], [('ps', b2), 'rc'], ['rc'])
                TS('dve', Ar[:, ql, :], ps[b2][0:64, 0:64], rc[0:64, 0:1], None, ALU.mult, None, [('ps', b2), 'rc', 'Ar'], ['Ar'])
            b = nextps()
            for q in range(8):
                TR(ps[b][0:64, q * 64:(q + 1) * 64], Ar[:, q, :], ident[0:64, 0:64], ['Ar'], [('ps', b)])
            ai = ac[0] % 2
            ac[0] += 1
            CP('act', aT[ai][:], ps[b][0:64, :], [('ps', b), ('aT', ai)], [('aT', ai)])
            P.dma('pool', attnT_d[h * 64:(h + 1) * 64, 0:512], aT[ai][:], r=[('aT', ai)], w=[('attnT', h)])
            for sq_ in range(2):
                tk = 1024 + sq_ * 256
                tq = 512 + sq_ * 256
                pi_ = cnt[0] % 2
                cnt[0] += 1
                for kt in range(2):
                    b = nextps()
                    MM(ps[b][:, 0:256], kT[hi][:, tk + kt * 128:tk + (kt + 1) * 128], qT[hi][:, tq:tq + 256], True, True,
                       [('kT', hi), ('qT', hi)], [('ps', b)])
                    ACT(Pp[pi_][:, kt, :], ps[b][:, 0:256], AF.Exp, [('ps', b), ('Pp', pi_)], [('Pp', pi_)], scale=0.125)
                b4 = nextps()
                for qc in range(2):
                    b2 = nextps()
                    for kt in range(2):
                        MM(ps[b2][:, 0:65], Pp[pi_][:, kt, qc * 128:(qc + 1) * 128], V1[:, 8 + sq_ * 2 + kt, h, :], kt == 0, kt == 1,
                           [('Pp', pi_), 'V1'], [('ps', b2)])
                    RCP(rc[:, :], ps[b2][:, 64:65], [('ps', b2), 'rc'], ['rc'])
                    TS('dve', Ah[:, qc, :], ps[b2][:, 0:64], rc[:, 0:1], None, ALU.mult, None, [('ps', b2), 'rc', 'Ah'], ['Ah'])
                    TR(ps[b4][0:64, qc * 128:(qc + 1) * 128], Ah[:, qc, :], ident[:, :], ['Ah'], [('ps', b4)])
                ai = ac[0] % 2
                ac[0] += 1
                CP('act', aT[ai][:, 0:256], ps[b4][0:64, 0:256], [('ps', b4), ('aT', ai)], [('aT', ai)])
                P.dma('pool', attnT_d[h * 64:(h + 1) * 64, tq:tq + 256], aT[ai][:, 0:256], r=[('aT', ai)], w=[('attnTp', h, sq_)])

    if 'E' in stages:
        attention()
        P.barrier()
    A.reset(wback.get('mark', PERS0))

    mgd_d = dscr("mgd_d", [D, 1024], BF16)

    def back1():
        Wg, Wso, Wao = wback['Wg'], wback['Wso'], wback['Wao']
        bg = A.alloc("bg", [128, 8])
        P.dma('sp', bg[:], bglu[:, :], w=['bg'])
        TBK = 256
        ysg = A.alloc("ysg", [128, 8, TBK])
        ysgb = A.alloc("ysgb", [128, 8, TBK], BF16)
        zs = A.alloc("zs", [128, 8, TBK])
        za = A.alloc("za", [128, 8, TBK])
        at = A.alloc("at", [128, 8, TBK])
        atb = A.alloc("atb", [128, 8, TBK], BF16)
        ys2 = A.alloc("ys2", [128, 8, TBK], BF16)
        glu = A.alloc("glu", [128, TBK])
        gsa = [A.alloc("gsa", [128, 2, TBK]) for _ in range(2)]
        mg = A.alloc("mg", [128, 16, TBK], BF16)
        m1 = A.alloc("m1", [128, TBK])
        m2 = A.alloc("m2", [128, TBK])
        fmv = lambda dten, tb: dten[:, tb * TBK:(tb + 1) * TBK].rearrange("(k p) n -> p k n", p=128)
        for tb in range(NB // TBK):
            P.dma('sp', ysg[:], fmv(ysg_d, tb), w=['ysg'])
            P.dma('sp', zs[:], fmv(zs_d, tb), w=['zs'])
            P.dma('sp', za[:], fmv(za_d, tb), w=['za'])
            P.dma('sp', at[:], fmv(attnT_d, tb), w=['at'])
            CP('act', ysgb[:], ysg[:], ['ysg', 'ysgb'], ['ysgb'])
            TT('dve', at[:], at[:], za[:], ALU.mult, ['at', 'za'], ['at'])
            CP('act', atb[:], at[:], ['at', 'atb'], ['atb'])
            for m in range(8):
                b = nextps()
                for k in range(8):
                    MM(ps[b][:, 0:TBK], Wg[:, k, m * 128:(m + 1) * 128], ysgb[:, k, :], k == 0, k == 7, ['Wg', 'ysgb'], [('ps', b)])
                ACT(glu[:], ps[b][:, 0:TBK], AF.Sigmoid, [('ps', b), 'bg', 'glu'], ['glu'], bias=bg[:, m:m + 1])
                TT('dve', glu[:], glu[:], ysg[:, m, :], ALU.mult, ['glu', 'ysg'], ['glu'])
                TT('dve', ys2[:, m, :], glu[:], zs[:, m, :], ALU.mult, ['glu', 'zs', ('ys2', m)], [('ys2', m)])
            allys2 = [('ys2', m) for m in range(8)]
            for nn in range(16):
                gi = nn % 2
                P.dma('sp', gsa[gi][:, 0, :], gs_d[nn * 128:(nn + 1) * 128, tb * TBK:(tb + 1) * TBK], w=[('gsa', gi, 0)])
                P.dma('sp', gsa[gi][:, 1, :], ga_d[nn * 128:(nn + 1) * 128, tb * TBK:(tb + 1) * TBK], w=[('gsa', gi, 1)])
                b = nextps()
                b2 = nextps()
                for k in range(8):
                    MM(ps[b][:, 0:TBK], Wso[:, k, nn * 128:(nn + 1) * 128], ys2[:, k, :], k == 0, k == 7, ['Wso'] + allys2, [('ps', b)])
                for k in range(8):
                    MM(ps[b2][:, 0:TBK], Wao[:, k, nn * 128:(nn + 1) * 128], atb[:, k, :], k == 0, k == 7, ['Wao', 'atb'], [('ps', b2)])
                TT('dve', m1[:], ps[b][:, 0:TBK], gsa[gi][:, 0, :], ALU.mult, [('ps', b), ('gsa', gi, 0), 'm1'], ['m1'])
                TT('dve', m2[:], ps[b2][:, 0:TBK], gsa[gi][:, 1, :], ALU.mult, [('ps', b2), ('gsa', gi, 1), 'm2'], ['m2'])
                TT('dve', mg[:, nn, :], m1[:], m2[:], ALU.add, ['m1', 'm2', 'mg'], ['mg'])
            P.dma('pool', fmv(mgd_d, tb), mg[:], r=['mg'], w=[('mgd_d', tb)])

    def back2():
        Wo = A.alloc("Wo", [128, 16, D], BF16)
        wst = [A.alloc("wst", [128, 4, 512]) for _ in range(2)]
        n = [0]
        v_ = w_o.rearrange("(k p) n -> p k n", p=128)
        for k4 in range(4):
            for c in range(4):
                i = n[0] % 2
                n[0] += 1
                P.dma('sp', wst[i][:], v_[:, k4 * 4:(k4 + 1) * 4, c * 512:(c + 1) * 512], w=[('wst', i)])
                CP(['dve', 'act'][i], Wo[:, k4 * 4:(k4 + 1) * 4, c * 512:(c + 1) * 512], wst[i][:], [('wst', i), 'Wo'], ['Wo'])
        gbc = [A.alloc("gbc", [128, D]) for _ in range(2)]
        for v in range(2):
            P.dma('sp', gbc[v][:], mod_d[v:v + 1, 2 * D:3 * D].partition_broadcast(128), w=[('gbc', v)])
        mgl = [A.alloc("mgl", [128, 16, 128], BF16) for _ in range(2)]
        xin = [A.alloc("xin", [128, D]) for _ in range(2)]
        yo = [A.alloc("yo", [128, D]) for _ in range(2)]
        for t in range(NB // 128):
            tok0 = t * 128
            xi = t % 2
            v = 1 if tok0 < 512 else 0
            src = xown[tok0:tok0 + 128, :] if tok0 < 512 else xp[tok0 - 512:tok0 - 512 + 128, :]
            dsto = ys_o[tok0:tok0 + 128, :] if tok0 < 512 else yp_o[tok0 - 512:tok0 - 512 + 128, :]
            P.dma('sp', xin[xi][:], src, w=[('xin', xi)])
            P.dma('sp', mgl[xi][:], mgd_d[:, tok0:tok0 + 128].rearrange("(k p) n -> p k n", p=128), w=[('mgl', xi)])
            for nb in range(4):
                b = nextps()
                for k in range(16):
                    MM(ps[b][:, :], mgl[xi][:, k, :], Wo[:, k, nb * 512:(nb + 1) * 512], k == 0, k == 15,
                       ['Wo', ('mgl', xi)], [('ps', b)])
                TT('dve', yo[xi][:, nb * 512:(nb + 1) * 512], ps[b][:, :], gbc[v][:, nb * 512:(nb + 1) * 512], ALU.mult,
                   [('ps', b), ('gbc', v), ('yo', xi)], [('yo', xi)])
            TT('dve', yo[xi][:], yo[xi][:], xin[xi][:], ALU.add, [('yo', xi), ('xin', xi)], [('yo', xi)])
            P.dma('pool', dsto, yo[xi][:], r=[('yo', xi)], w=[('yout', tok0)])

    if 'F' in stages:
        back1()
        P.barrier()
        A.reset(PERS0)
        back2()
        P.barrier()
    P.emit(es)
    es.close()
    return nc, P


_NC = None


def _consts():
    c = {}
    c["ident"] = np.eye(128, dtype=np.float32)
    s = np.arange(128) // 16
    c["maskf"] = (s[None, :] >= s[:, None]).astype(np.float32)
    c["maskb"] = (s[:, None] >= s[None, :]).astype(np.float32)
    kc = np.arange(64)[:, None]
    qc = np.arange(64)[None, :]
    cs = np.clip(qc - 8, 0, 48)
    c["colmask"] = np.where((kc >= cs) & (kc < cs + 16), 0.0, NEG).astype(np.float32)
    dc = np.clip(kc - qc + 15, 0, 30)
    oh = np.zeros((31, 64, 64), np.float32)
    for q in range(64):
        for k in range(64):
            oh[dc[k, q], q, k] = 1.0
    c["onehot"] = oh
    return c


def kernel(x_prompt, x_sample, cache_k, cache_v, state_ssm_re, state_ssm_im, c, c_ctx,
           norm_w, w_ada, b_ada, w_in, q_norm_w, k_norm_w, rel_pos_bias,
           ssm_a_re, ssm_a_im, ssm_log_dt, ssm_b_re, ssm_b_im, ssm_c_re, ssm_c_im, ssm_d,
           w_glu, b_glu, w_ssm_out, w_att_out, w_o):
    global _NC
    f = lambda a: np.ascontiguousarray(np.asarray(a, dtype=np.float32))
    if _NC is None:
        _NC = build()
    nc = _NC[0]
    x_prompt = f(x_prompt)
    x_sample = f(x_sample)
    c = f(c)
    c_ctx = f(c_ctx)
    shared = _consts()
    shared.update({
        "w_ada": f(w_ada)[0], "b_ada": f(b_ada)[0][None, :], "norm_w": f(norm_w)[0][None, :], "w_in": f(w_in)[0],
        "qnw": f(q_norm_w)[0][None, :], "knw": f(k_norm_w)[0][None, :],
        "rpbT": np.ascontiguousarray(f(rel_pos_bias)[0].reshape(240, 31).T),
        "are": f(ssm_a_re)[0].reshape(128, 64), "aim": f(ssm_a_im)[0].reshape(128, 64),
        "logdt": f(ssm_log_dt)[0].reshape(1, 128),
        "bre": f(ssm_b_re)[0].reshape(128, 64, 16), "bim": f(ssm_b_im)[0].reshape(128, 64, 16),
        "cre": f(ssm_c_re)[0].reshape(128, 16, 64), "cim": f(ssm_c_im)[0].reshape(128, 16, 64),
        "dcol": np.ascontiguousarray(np.tile(f(ssm_d)[0].reshape(64, 16).T, (8, 1))),
        "w_glu": f(w_glu)[0], "bglu": np.ascontiguousarray(f(b_glu)[0].reshape(8, 128).T),
        "w_so": f(w_ssm_out)[0], "w_ao": f(w_att_out)[0], "w_o": f(w_o)[0],
    })
    in_maps = []
    for core in range(NCORES):
        b, q = core % 2, core // 2
        m = dict(shared)
        m["xs"] = x_sample[b]
        m["xown"] = x_sample[b, q * 512:(q + 1) * 512]
        kb = 8 * q - 4
        hrows = [min(max(kb + j, 0), 31) for j in (0, 1, 2, 3, 12, 13, 14, 15)]
        m["xhalo"] = np.ascontiguousarray(np.concatenate([x_sample[b, r * 64:(r + 1) * 64] for r in hrows], 0))
        sel = np.zeros((128, 3, 128), np.float32)
        for pi in range(2):
            for cc in range(128):
                o = pi * 128 + cc - 64 * q
                if 0 <= o < 64:
                    sel[cc, pi, o] = 1.0
        for cc in range(64):
            sel[cc, 2, 64 + cc] = 1.0
        m["sel3"] = sel
        rm = np.zeros((128, 8, 8), np.float32)
        for ql in range(8):
            rs = min(max(8 * q + ql - 4, 0), 24)
            for j in range(16):
                if rs - kb <= j < rs - kb + 8:
                    rm[(j % 2) * 64:(j % 2) * 64 + 64, ql, j // 2] = 1.0
        m["rmask"] = rm
        m["xp"] = x_prompt[2 * core:2 * core + 2].reshape(NPR, D)
        m["cond2"] = np.ascontiguousarray(np.stack([c_ctx, c[b]]).reshape(32, 128))
        m["ck"] = f(cache_k)[b, 0].reshape(512, 1024)
        m["cv"] = f(cache_v)[b, 0].reshape(512, 1024)
        m["h0re"] = f(state_ssm_re)[b, 0].reshape(128, 64)
        m["h0im"] = f(state_ssm_im)[b, 0].reshape(128, 64)
        in_maps.append(m)
    res = run_bass_kernel_spmd(nc, in_maps, core_ids=list(range(NCORES)))
    R = res.results
    y_p = np.concatenate([r["yp_o"].reshape(2, 256, D) for r in R], 0)
    y_s = np.zeros((2, NS, D), np.float32)
    for core in range(NCORES):
        b, q = core % 2, core // 2
        y_s[b, q * 512:(q + 1) * 512] = R[core]["ys_o"]
    new_k = np.concatenate([r["ko"].reshape(2, 1, 256, 16, 64) for r in R], 0)
    new_v = np.concatenate([r["vo"].reshape(2, 1, 256, 16, 64) for r in R], 0)
    def _st(a):
        return a.reshape(2, 2, 8, 4, 2, 64).transpose(0, 1, 2, 4, 3, 5).reshape(2, 1, 2, 64, 64)
    st_re = np.concatenate([_st(r["sre_o"]) for r in R], 0)
    st_im = np.concatenate([_st(r["sim_o"]) for r in R], 0)
    return (y_p, y_s, new_k, new_v, st_re, st_im)
```
